# Optimizing a Trainium2 kernel written in Bass

```python
import math
import jax, jax.numpy as jnp
from jax import lax
import numpy as np

D_MODEL = 1024
BATCH = 8
SEQ = 2048
DEPTH = 2
DEC_BATCH = 128
DEC_SEQ = 8
PAST_LEN = 16384
PAGE_SIZE = 128

N_MIXERS = 2
N_POOL_LAYERS = (DEPTH + 1) // 2
N_SSM_LAYERS = DEPTH // 2
POOL_WINDOWS = (2, 4, 8, 16)
N_POOL_GROUPS = len(POOL_WINDOWS)
POOL_GROUP_DIM = D_MODEL // N_POOL_GROUPS
POOL_BUF = max(POOL_WINDOWS) - 1
SSM_GROUP_DIM = 16
SSM_GROUPS = D_MODEL // SSM_GROUP_DIM
SSM_STATE = 64
SCAN_BLOCK = 128
D_FF = -(-(8 * D_MODEL) // (3 * 256)) * 256
PLE_DIM = 256
EPS = 1e-6

kernel_name = "pool_s5_hybrid_decode_step"


def rmsnorm(x, g):
    xf = x.astype(jnp.float32)
    y = xf * lax.rsqrt(jnp.mean(xf * xf, axis=-1, keepdims=True) + EPS) * g.astype(jnp.float32)
    return y.astype(x.dtype)


def pool_mixer(h, prev, start, w_grp, scale):
    bt, t_len, _ = h.shape
    rows = jnp.concatenate([prev.astype(jnp.float32), h.astype(jnp.float32)], axis=1)
    cs = jnp.concatenate([jnp.zeros((bt, 1, D_MODEL), jnp.float32), jnp.cumsum(rows, axis=1)], axis=1)
    end = cs[:, POOL_BUF + 1:]
    pos = start + jnp.arange(t_len)
    outs = []
    for gi, w in enumerate(POOL_WINDOWS):
        c0, c1 = gi * POOL_GROUP_DIM, (gi + 1) * POOL_GROUP_DIM
        beg = cs[:, POOL_BUF + 1 - w: POOL_BUF + 1 - w + t_len, c0:c1]
        cnt = jnp.minimum(w, pos + 1).astype(jnp.float32)[None, :, None]
        diff = (end[..., c0:c1] - beg) / cnt - rows[:, POOL_BUF:, c0:c1]
        outs.append(jnp.einsum('btc,cd->btd', diff, w_grp[gi].astype(jnp.float32)))
    out = jnp.concatenate(outs, axis=-1) * scale.astype(jnp.float32)
    return out.astype(h.dtype), rows[:, -POOL_BUF:].astype(prev.dtype)


def _complex_affine_combine(e1, e2):
    a1r, a1i, b1r, b1i = e1
    a2r, a2i, b2r, b2i = e2
    return (a2r * a1r - a2i * a1i,
            a2r * a1i + a2i * a1r,
            a2r * b1r - a2i * b1i + b2r,
            a2r * b1i + a2i * b1r + b2i)


def ssm_mixer(h, h0_re, h0_im, lam_re, lam_im, log_dt, b_re, b_im, c_re, c_im, d, w_glu):
    bt, t_len, _ = h.shape
    f32 = jnp.float32
    lam_re = lam_re.astype(f32); lam_im = lam_im.astype(f32)
    dt = jnp.exp(log_dt.astype(f32))[:, None]
    mag = jnp.exp(lam_re * dt)
    lb_re = mag * jnp.cos(lam_im * dt)
    lb_im = mag * jnp.sin(lam_im * dt)
    den = lam_re * lam_re + lam_im * lam_im
    f_re = ((lb_re - 1.0) * lam_re + lb_im * lam_im) / den
    f_im = (lb_im * lam_re - (lb_re - 1.0) * lam_im) / den
    b_re = b_re.astype(f32); b_im = b_im.astype(f32)
    bb_re = f_re[..., None] * b_re - f_im[..., None] * b_im
    bb_im = f_re[..., None] * b_im + f_im[..., None] * b_re
    c_re = c_re.astype(f32); c_im = c_im.astype(f32)

    u = h.astype(f32)
    blk = SCAN_BLOCK if t_len % SCAN_BLOCK == 0 else t_len
    nb = t_len // blk
    ub = u.reshape(bt, nb, blk, SSM_GROUPS, SSM_GROUP_DIM).transpose(1, 0, 2, 3, 4)

    def step(carry, u_blk):
        hr, hi = carry
        br = jnp.einsum('blgh,gph->blgp', u_blk, bb_re)
        bi = jnp.einsum('blgh,gph->blgp', u_blk, bb_im)
        br = br.at[:, 0].add(lb_re * hr - lb_im * hi)
        bi = bi.at[:, 0].add(lb_re * hi + lb_im * hr)
        ar = jnp.broadcast_to(lb_re, br.shape)
        ai = jnp.broadcast_to(lb_im, br.shape)
        _, _, sr, si = lax.associative_scan(_complex_affine_combine, (ar, ai, br, bi), axis=1)
        y = jnp.einsum('blgp,ghp->blgh', sr, c_re) - jnp.einsum('blgp,ghp->blgh', si, c_im)
        return (sr[:, -1], si[:, -1]), y

    (hr, hi), ys = lax.scan(step, (h0_re.astype(f32), h0_im.astype(f32)), ub)
    y = ys.transpose(1, 0, 2, 3, 4).reshape(bt, t_len, D_MODEL) + d.astype(f32) * u
    z = jax.nn.gelu(y)
    a, g = jnp.split(jnp.einsum('btd,de->bte', z, w_glu.astype(f32)), 2, axis=-1)
    out = a * jax.nn.sigmoid(g)
    return out.astype(h.dtype), hr.astype(h0_re.dtype), hi.astype(h0_im.dtype)


def swiglu(h, w_gate, w_up, w_down):
    return (jax.nn.silu(h @ w_gate) * (h @ w_up)) @ w_down


def per_layer_embed(x, p_i, g, w_in, w_gate):
    gate = jax.nn.sigmoid(rmsnorm(x, g) @ w_gate)
    return (p_i @ w_in) * gate


def run_trunk(x, p, start, pool_state, ssm_re, ssm_im, prm):
    pool_new, re_new, im_new = [], [], []
    for i in range(DEPTH):
        j = i // N_MIXERS
        h = rmsnorm(x, prm['g_mix'][i])
        if i % N_MIXERS == 0:
            out, st = pool_mixer(h, pool_state[j], start, prm['pool_w'][j], prm['pool_scale'][j])
            pool_new.append(st)
        else:
            out, sr, si = ssm_mixer(h, ssm_re[j], ssm_im[j], prm['ssm_lambda_re'][j], prm['ssm_lambda_im'][j],
                                    prm['ssm_log_dt'][j], prm['ssm_b_re'][j], prm['ssm_b_im'][j],
                                    prm['ssm_c_re'][j], prm['ssm_c_im'][j], prm['ssm_d'][j], prm['ssm_w_glu'][j])
            re_new.append(sr)
            im_new.append(si)
        x = x + out
        x = x + swiglu(rmsnorm(x, prm['g_ffn'][i]), prm['ffn_w_gate'][i], prm['ffn_w_up'][i], prm['ffn_w_down'][i])
        x = x + per_layer_embed(x, p[i], prm['g_ple'][i], prm['ple_w_in'][i], prm['ple_w_gate'][i])
    return rmsnorm(x, prm['g_final']), jnp.stack(pool_new), jnp.stack(re_new), jnp.stack(im_new)


def setup_inputs(seed: int = 0) -> dict:
    key = jax.random.key(seed)
    ks = jax.random.split(key, 32)
    f32 = jnp.float32

    def nrm(k, shape, s):
        return jax.random.normal(k, shape, f32) * s

    lam_im_base = jnp.pi * jnp.arange(SSM_STATE, dtype=f32)
    return {
        'x_prompt': nrm(ks[0], (BATCH, SEQ, D_MODEL), 1.0),
        'x_sample': nrm(ks[1], (DEC_BATCH, DEC_SEQ, D_MODEL), 1.0),
        'state_pool': nrm(ks[2], (N_POOL_LAYERS, DEC_BATCH, POOL_BUF, D_MODEL), 1.0),
        'state_ssm_re': nrm(ks[3], (N_SSM_LAYERS, DEC_BATCH, SSM_GROUPS, SSM_STATE), 0.1),
        'state_ssm_im': nrm(ks[4], (N_SSM_LAYERS, DEC_BATCH, SSM_GROUPS, SSM_STATE), 0.1),
        'p_prompt': nrm(ks[5], (DEPTH, BATCH, SEQ, PLE_DIM), 1.0),
        'p_sample': nrm(ks[6], (DEPTH, DEC_BATCH, DEC_SEQ, PLE_DIM), 1.0),
        'g_mix': 1.0 + nrm(ks[7], (DEPTH, D_MODEL), 0.05),
        'g_ffn': 1.0 + nrm(ks[8], (DEPTH, D_MODEL), 0.05),
        'g_ple': 1.0 + nrm(ks[9], (DEPTH, D_MODEL), 0.05),
        'g_final': 1.0 + nrm(ks[10], (D_MODEL,), 0.05),
        'pool_w': nrm(ks[11], (N_POOL_LAYERS, N_POOL_GROUPS, POOL_GROUP_DIM, POOL_GROUP_DIM), POOL_GROUP_DIM ** -0.5),
        'pool_scale': 1.0 + nrm(ks[12], (N_POOL_LAYERS, D_MODEL), 0.1),
        'ssm_lambda_re': -0.5 + nrm(ks[13], (N_SSM_LAYERS, SSM_GROUPS, SSM_STATE), 0.01),
        'ssm_lambda_im': lam_im_base + nrm(ks[14], (N_SSM_LAYERS, SSM_GROUPS, SSM_STATE), 0.01),
        'ssm_log_dt': jax.random.uniform(ks[15], (N_SSM_LAYERS, SSM_GROUPS), f32, math.log(1e-3), math.log(1e-1)),
        'ssm_b_re': nrm(ks[16], (N_SSM_LAYERS, SSM_GROUPS, SSM_STATE, SSM_GROUP_DIM), (2 * SSM_GROUP_DIM) ** -0.5),
        'ssm_b_im': nrm(ks[17], (N_SSM_LAYERS, SSM_GROUPS, SSM_STATE, SSM_GROUP_DIM), (2 * SSM_GROUP_DIM) ** -0.5),
        'ssm_c_re': nrm(ks[18], (N_SSM_LAYERS, SSM_GROUPS, SSM_GROUP_DIM, SSM_STATE), SSM_STATE ** -0.5),
        'ssm_c_im': nrm(ks[19], (N_SSM_LAYERS, SSM_GROUPS, SSM_GROUP_DIM, SSM_STATE), SSM_STATE ** -0.5),
        'ssm_d': 1.0 + nrm(ks[20], (N_SSM_LAYERS, D_MODEL), 0.1),
        'ssm_w_glu': nrm(ks[21], (N_SSM_LAYERS, D_MODEL, 2 * D_MODEL), D_MODEL ** -0.5),
        'ffn_w_gate': nrm(ks[22], (DEPTH, D_MODEL, D_FF), D_MODEL ** -0.5),
        'ffn_w_up': nrm(ks[23], (DEPTH, D_MODEL, D_FF), D_MODEL ** -0.5),
        'ffn_w_down': nrm(ks[24], (DEPTH, D_FF, D_MODEL), D_FF ** -0.5),
        'ple_w_in': nrm(ks[25], (DEPTH, PLE_DIM, D_MODEL), PLE_DIM ** -0.5),
        'ple_w_gate': nrm(ks[26], (DEPTH, D_MODEL, D_MODEL), D_MODEL ** -0.5),
    }


def reference(x_prompt, x_sample, state_pool, state_ssm_re, state_ssm_im, p_prompt, p_sample,
              g_mix, g_ffn, g_ple, g_final, pool_w, pool_scale,
              ssm_lambda_re, ssm_lambda_im, ssm_log_dt, ssm_b_re, ssm_b_im, ssm_c_re, ssm_c_im,
              ssm_d, ssm_w_glu, ffn_w_gate, ffn_w_up, ffn_w_down, ple_w_in, ple_w_gate):
    prm = dict(g_mix=g_mix, g_ffn=g_ffn, g_ple=g_ple, g_final=g_final, pool_w=pool_w, pool_scale=pool_scale,
               ssm_lambda_re=ssm_lambda_re, ssm_lambda_im=ssm_lambda_im, ssm_log_dt=ssm_log_dt,
               ssm_b_re=ssm_b_re, ssm_b_im=ssm_b_im, ssm_c_re=ssm_c_re, ssm_c_im=ssm_c_im,
               ssm_d=ssm_d, ssm_w_glu=ssm_w_glu, ffn_w_gate=ffn_w_gate, ffn_w_up=ffn_w_up,
               ffn_w_down=ffn_w_down, ple_w_in=ple_w_in, ple_w_gate=ple_w_gate)
    pool0 = jnp.zeros((N_POOL_LAYERS, x_prompt.shape[0], POOL_BUF, D_MODEL), x_prompt.dtype)
    ssm0 = jnp.zeros((N_SSM_LAYERS, x_prompt.shape[0], SSM_GROUPS, SSM_STATE), state_ssm_re.dtype)
    y_prompt, pool_prompt, ssm_re_prompt, ssm_im_prompt = run_trunk(
        x_prompt, p_prompt, 0, pool0, ssm0, ssm0, prm)
    y_sample, pool_sample, ssm_re_sample, ssm_im_sample = run_trunk(
        x_sample, p_sample, PAST_LEN, state_pool, state_ssm_re, state_ssm_im, prm)
    return (y_prompt, y_sample, pool_prompt, pool_sample, ssm_re_prompt, ssm_im_prompt, ssm_re_sample, ssm_im_sample)
```

```python
import numpy as np
import concourse.bass as bass
import concourse.mybir as mybir
from concourse.bass_utils import run_bass_kernel_spmd

F32 = mybir.dt.float32
BF16 = mybir.dt.bfloat16
I32 = mybir.dt.int32
AF = mybir.ActivationFunctionType
ALU = mybir.AluOpType

NCORES = 8
D = 1024
NCH = 8
TP = 2048
NSEQ = 16
TSEQ = 8
TS = NSEQ * TSEQ
NT = TP + TS
DFF = 2816
PLE = 256
EPS = 1e-6
HPAD = 16
NBX = 144
HW = HPAD + 16 * NBX
TBS = [(0, 512), (512, 512), (1024, 512), (1536, 512), (2048, 128)]
POOL_W = (2, 4, 8, 16)
FF_SLICES = [(0, 4), (4, 4), (8, 4), (12, 4), (16, 3), (19, 3)]
LCH = 16
NB = TP // LCH
GV = {"g_mix0": 0, "g_mix1": 1, "g_ffn0": 2, "g_ffn1": 3, "g_ple0": 4, "g_ple1": 5, "g_final": 6,
      "pool_scale": 7, "ssm_d": 8}
ARENA = 97 * 1024
GRAN = 1024


class FW:
    ENG = ("tensor", "vector", "scalar", "gpsimd", "sync")

    def __init__(self, nc, n_dma_sems=32):
        self.nc = nc
        self.eng = {e: getattr(nc, e) for e in self.ENG}
        self.esem = {e: nc.alloc_semaphore("es_" + e) for e in self.ENG}
        self.ecount = {e: 0 for e in self.ENG}
        self.waited = {e: {} for e in self.ENG}
        self.dsems = [nc.alloc_semaphore("ds%d" % i) for i in range(n_dma_sems)]
        self.dcount = [0] * n_dma_sems
        self.dnext = 0
        self.last_write = {}
        self.reads_since = {}
        self.ps_next = 0
        self.gsems = []

    def _wait(self, e, dep):
        sem, val = dep
        key = id(sem)
        if self.waited[e].get(key, 0) >= val:
            return
        self.eng[e].wait_ge(sem, val)
        self.waited[e][key] = val

    def _deps(self, reads, writes):
        d = []
        for k in reads:
            if k in self.last_write:
                d.append(self.last_write[k])
        for k in writes:
            if k in self.last_write:
                d.append(self.last_write[k])
            d.extend(self.reads_since.get(k, ()))
        return d

    def _commit(self, dep, reads, writes):
        for k in reads:
            lst = self.reads_since.setdefault(k, [])
            lst[:] = [x for x in lst if x[0] is not dep[0]]
            lst.append(dep)
        for k in writes:
            self.last_write[k] = dep
            self.reads_since[k] = []

    def op(self, e, fn, reads=(), writes=()):
        own = self.esem[e]
        for dep in self._deps(reads, writes):
            if e == "tensor" and dep[0] is own:
                continue
            self._wait(e, dep)
        inst = fn(self.eng[e])
        self.ecount[e] += 1
        inst.then_inc(own, 1)
        dep = (own, self.ecount[e])
        self._commit(dep, reads, writes)
        return dep

    def dma(self, e, out, in_, reads=(), writes=(), **kw):
        if e == "gpsimd":
            sem = self.nc.alloc_semaphore("gd%d" % len(self.gsems))
            self.gsems.append(sem)
            for dep in self._deps(reads, writes):
                self._wait(e, dep)
            self.eng[e].dma_start(out=out, in_=in_, **kw).then_inc(sem, 16)
            dep = (sem, 16)
            self._commit(dep, reads, writes)
            return dep
        i = self.dnext
        self.dnext = (self.dnext + 1) % len(self.dsems)
        sem = self.dsems[i]
        if self.dcount[i] > 0:
            self._wait(e, (sem, self.dcount[i]))
        for dep in self._deps(reads, writes):
            self._wait(e, dep)
        self.eng[e].dma_start(out=out, in_=in_, **kw).then_inc(sem, 16)
        self.dcount[i] += 16
        dep = (sem, self.dcount[i])
        self._commit(dep, reads, writes)
        return dep

    def finish(self, e="sync"):
        for i, sem in enumerate(self.dsems):
            if self.dcount[i] > 0:
                self._wait(e, (sem, self.dcount[i]))
        for sem in self.gsems:
            self._wait(e, (sem, 16))

    def bank(self):
        b = self.ps_next
        self.ps_next = (self.ps_next + 1) % 8
        return b


class V:
    def __init__(self, ap, keys):
        self.ap = ap
        self.keys = keys


def build_program(enable_ssm=True):
    nc = bass.Bass("TRN2", target_bir_lowering=False)
    fw = FW(nc)

    def din(name, shape):
        return nc.dram_tensor(name, list(shape), F32, kind="ExternalInput").ap()

    def dout(name, shape):
        return nc.dram_tensor(name, list(shape), F32, kind="ExternalOutput").ap()

    xp = din("xp", [TP, D]); xs = din("xs", [TS, D])
    pp = din("pp", [2, TP, PLE]); psm = din("psm", [2, TS, PLE])
    spool = din("spool", [NSEQ * 15, D])
    sre = din("sre", [NSEQ * 32, 128]); sim = din("sim", [NSEQ * 32, 128])
    gvecs = din("gvecs", [9, D])
    pool_w = din("pool_w", [4, 256, 256])
    lam_re = din("lam_re", [32, 128]); lam_im = din("lam_im", [32, 128]); log_dt = din("log_dt", [32, 2])
    b_re = din("b_re", [32, 128, 16]); b_im = din("b_im", [32, 128, 16])
    c_re = din("c_re", [32, 2, 16, 64]); c_im = din("c_im", [32, 2, 16, 64])
    w_glu = din("w_glu", [D, 2 * D])
    w_gate = din("w_gate", [2, D, DFF]); w_up = din("w_up", [2, D, DFF]); w_down = din("w_down", [2, DFF, D])
    ple_w_in = din("ple_w_in", [2, PLE, D]); ple_w_gate = din("ple_w_gate", [2, D, D])

    y_p = dout("y_p", [TP, D]); y_s = dout("y_s", [TS, D])
    pool_p = dout("pool_p", [15, D]); pool_s = dout("pool_s", [NSEQ * 15, D])
    sre_p = dout("sre_p", [32, 128]); sim_p = dout("sim_p", [32, 128])
    sre_s = dout("sre_s", [NSEQ * 32, 128]); sim_s = dout("sim_s", [NSEQ * 32, 128])

    x = nc.alloc_sbuf_tensor("x", [128, NCH, NT], F32).ap()
    hb = nc.alloc_sbuf_tensor("hb", [128, NCH + 1, HW], BF16).ap()
    ident = nc.alloc_sbuf_tensor("ident", [128, 128], F32).ap()
    ones_bf = nc.alloc_sbuf_tensor("ones_bf", [128, 128], BF16).ap()
    gvec = nc.alloc_sbuf_tensor("gvec", [128, NCH, 16], F32).ap()
    iot = nc.alloc_sbuf_tensor("iot", [128, 128], I32).ap()
    R = nc.alloc_sbuf_tensor("arena", [128, ARENA // 2], BF16).ap()
    psb = [nc.alloc_psum_tensor("ps%d" % i, [128, 512], F32).ap() for i in range(8)]

    def av(off, nbytes, dtype=BF16, pat=None, **dims):
        assert off % 4 == 0 and nbytes % 4 == 0 and off + nbytes <= ARENA, (off, nbytes)
        ap = R[:, off // 2:(off + nbytes) // 2]
        if dtype != BF16:
            ap = ap.bitcast(dtype)
        if pat is not None:
            ap = ap.rearrange(pat, **dims)
        keys = [("R", s) for s in range(off // GRAN, (off + nbytes + GRAN - 1) // GRAN)]
        return V(ap, keys)

    def xk(c, tbi):
        return ("x", c, tbi)

    def hk(c, tbi):
        return ("h", c, tbi)

    def pk(b):
        return ("ps", b)

    op = fw.op
    K1 = 1024

    op("gpsimd", lambda g: g.iota(iot, [[1, 128]], base=0, channel_multiplier=-1), writes=["iot"])
    op("vector", lambda v: v.tensor_scalar(out=ident, in0=iot, scalar1=0, scalar2=None, op0=ALU.is_equal),
       reads=["iot"], writes=["ident"])
    op("vector", lambda v: v.memset(ones_bf, 1.0), writes=["ones"])
    op("gpsimd", lambda g: g.memset(hb[:, :, 0:HPAD], 0.0), writes=["hpad"])

    gv_rows = av(88 * K1, 4096, F32)
    for i in range(9):
        fw.dma("sync", gv_rows.ap[i:i + 1, :], gvecs[i:i + 1, :], writes=gv_rows.keys)
    b0 = fw.bank()
    for c in range(NCH):
        op("tensor", lambda t, c=c: t.transpose(psb[b0][:, c * 16:c * 16 + 9], gv_rows.ap[0:9, c * 128:(c + 1) * 128],
                                                 ident[0:9, 0:9]),
           reads=gv_rows.keys + ["ident"], writes=[pk(b0)])
    op("vector", lambda v: v.tensor_copy(out=gvec[:, :, 0:9], in_=psb[b0][:, 0:128].rearrange("p (c i) -> p c i", i=16)[:, :, 0:9]),
       reads=[pk(b0)], writes=["gvec"])

    def gs(name, c):
        i = GV[name]
        return gvec[:, c, i:i + 1]

    WFF = [0, 24 * K1]

    def ffn_views(b, nf):
        o = WFF[b]
        wg = av(o, 8 * K1, BF16, "p (k n) -> p k n", k=8)
        wu = av(o + 8 * K1, 8 * K1, BF16, "p (k n) -> p k n", k=8)
        wd = av(o + 16 * K1, 8 * K1, BF16, "p (f n) -> p f n", f=4)
        return wg, wu, wd

    def load_ffn_slice(layer, si):
        f0, nf = FF_SLICES[si]
        b = si % 2
        wg, wu, wd = ffn_views(b, nf)
        c0, c1 = f0 * 128, (f0 + nf) * 128
        fw.dma("gpsimd", wg.ap[:, :, 0:nf * 128], w_gate[layer, :, c0:c1].rearrange("(k p) n -> p k n", p=128),
               writes=wg.keys)
        fw.dma("gpsimd", wu.ap[:, :, 0:nf * 128], w_up[layer, :, c0:c1].rearrange("(k p) n -> p k n", p=128),
               writes=wu.keys)
        fw.dma("gpsimd", wd.ap[:, 0:nf, :], w_down[layer, c0:c1, :].rearrange("(f p) n -> p f n", p=128),
               writes=wd.keys)

    xin = [av(63 * K1, 4 * K1, F32), av(93 * K1, 4 * K1, F32),
           V(hb[:, NCH, HPAD:HPAD + 2 * D].bitcast(F32), [hk(NCH, t_) for t_ in range(5)])]
    ev = [0]

    def evac_copy(out, in_, reads, writes, eng=None):
        ev[0] += 1
        if eng == "vector" or (eng is None and ev[0] % 2 == 0):
            op("vector", lambda v: v.tensor_copy(out=out, in_=in_), reads=reads, writes=writes)
        else:
            op("scalar", lambda s: s.copy(out=out, in_=in_), reads=reads, writes=writes)

    def x_load_tb(tbi):
        t0b, nb_ = TBS[tbi]
        for tt in range(t0b // 128, (t0b + nb_) // 128):
            xi = xin[tt % 3]
            src = xp[tt * 128:(tt + 1) * 128, :] if tt < 16 else xs
            fw.dma("sync", xi.ap, src, writes=xi.keys)
            t0 = tt * 128
            for half in range(2):
                b = fw.bank()
                for cc in range(4):
                    c = half * 4 + cc
                    op("tensor", lambda t, b=b, cc=cc, c=c, xi=xi: t.transpose(psb[b][:, cc * 128:(cc + 1) * 128],
                                                                             xi.ap[:, c * 128:(c + 1) * 128], ident),
                       reads=xi.keys + ["ident"], writes=[pk(b)])
                evac_copy(x[:, half * 4:half * 4 + 4, t0:t0 + 128], psb[b].rearrange("p (c t) -> p c t", c=4),
                          reads=[pk(b)], writes=[xk(c, tbi) for c in range(half * 4, half * 4 + 4)])

    SQB = [68 * K1, 76 * K1]
    RT = 84 * K1
    RSTD = [86 * K1, 88 * K1]
    nctr = [0]

    def norm_stats(tbi):
        t0, n = TBS[tbi]
        b = nctr[0] % 2
        nctr[0] += 1
        sq = av(SQB[b], 8 * K1, BF16, "p (c t) -> p c t", c=8)
        for c in range(NCH):
            op("scalar", lambda s, c=c: s.activation(out=sq.ap[:, c, 0:n], in_=x[:, c, t0:t0 + n], func=AF.Square),
               reads=[xk(c, tbi)], writes=sq.keys)
        pb = fw.bank()
        for c in range(NCH):
            op("tensor", lambda t, c=c: t.matmul(psb[pb][:, 0:n], lhsT=ones_bf, rhs=sq.ap[:, c, 0:n],
                                                 start=(c == 0), stop=(c == NCH - 1)),
               reads=sq.keys + ["ones"], writes=[pk(pb)])
        rt = av(RT, 2 * K1, F32)
        rstd = av(RSTD[b], 2 * K1, F32)
        op("scalar", lambda s: s.activation(out=rt.ap[:, 0:n], in_=psb[pb][:, 0:n], func=AF.Ln, scale=1.0 / D, bias=EPS),
           reads=[pk(pb)], writes=rt.keys)
        op("scalar", lambda s: s.activation(out=rstd.ap[:, 0:n], in_=rt.ap[:, 0:n], func=AF.Exp, scale=-0.5),
           reads=rt.keys, writes=rstd.keys)
        return rstd

    def norm_gen(tbi, gname, dst_fn=None, out=None):
        t0, n = TBS[tbi]
        b = nctr[0] % 2
        nctr[0] += 1
        sq = av(SQB[b], 8 * K1, BF16, "p (c t) -> p c t", c=8)
        for c in range(NCH):
            op("scalar", lambda s, c=c: s.activation(out=sq.ap[:, c, 0:n], in_=x[:, c, t0:t0 + n], func=AF.Square),
               reads=[xk(c, tbi)], writes=sq.keys)
        yield
        pb = fw.bank()
        for c in range(NCH):
            op("tensor", lambda t, c=c: t.matmul(psb[pb][:, 0:n], lhsT=ones_bf, rhs=sq.ap[:, c, 0:n],
                                                 start=(c == 0), stop=(c == NCH - 1)),
               reads=sq.keys + ["ones"], writes=[pk(pb)])
        rt = av(RT, 2 * K1, F32)
        rstd = av(RSTD[b], 2 * K1, F32)
        op("scalar", lambda s: s.activation(out=rt.ap[:, 0:n], in_=psb[pb][:, 0:n], func=AF.Ln, scale=1.0 / D, bias=EPS),
           reads=[pk(pb)], writes=rt.keys)
        op("scalar", lambda s: s.activation(out=rstd.ap[:, 0:n], in_=rt.ap[:, 0:n], func=AF.Exp, scale=-0.5),
           reads=rt.keys, writes=rstd.keys)
        for c in range(NCH):
            if dst_fn is None:
                o, in0, in1, wk = hb[:, c, HPAD + t0:HPAD + t0 + n], x[:, c, t0:t0 + n], rstd.ap[:, 0:n], [hk(c, tbi)]
            else:
                o, in0, in1, wk = dst_fn(c, rstd)
            op("vector", lambda v, o=o, in0=in0, in1=in1, c=c: v.scalar_tensor_tensor(
                out=o, in0=in0, scalar=gs(gname, c), in1=in1, op0=ALU.mult, op1=ALU.mult),
               reads=[xk(c, tbi), "gvec"] + rstd.keys, writes=wk)
        if out is not None:
            out.append(rstd)

    def norm_to_hb(tbi, gname, dst_fn=None):
        t0, n = TBS[tbi]
        rstd = norm_stats(tbi)
        for c in range(NCH):
            if dst_fn is None:
                o, in0, in1, wk = hb[:, c, HPAD + t0:HPAD + t0 + n], x[:, c, t0:t0 + n], rstd.ap[:, 0:n], [hk(c, tbi)]
            else:
                o, in0, in1, wk = dst_fn(c, rstd)
            op("vector", lambda v, o=o, in0=in0, in1=in1, c=c: v.scalar_tensor_tensor(
                out=o, in0=in0, scalar=gs(gname, c), in1=in1, op0=ALU.mult, op1=ALU.mult),
               reads=[xk(c, tbi), "gvec"] + rstd.keys, writes=wk)
        return rstd

    ACT_B = [48 * K1, 52 * K1]
    SL_B = [56 * K1, 57 * K1]

    def ffn(layer, pre_normed=False, start_hook=None, mid_hook=None, tail_hook=None, each_gen=None):
        gname = "g_ffn%d" % layer
        if not pre_normed:
            for tbi in range(5):
                norm_to_hb(tbi, gname)
        if start_hook is not None:
            start_hook()
        ctr = 0
        for si, (f0, nf) in enumerate(FF_SLICES):
            if si + 1 < len(FF_SLICES):
                load_ffn_slice(layer, si + 1)
            wg, wu, wd = ffn_views(si % 2, nf)

            def gate_up(tbi, actv):
                t0, n = TBS[tbi]
                hr = [hk(k, tbi) for k in range(NCH)]
                for f in range(nf):
                    pg = fw.bank()
                    for k in range(NCH):
                        op("tensor", lambda t, k=k, f=f, pg=pg: t.matmul(
                            psb[pg][:, 0:n], lhsT=wg.ap[:, k, f * 128:(f + 1) * 128], rhs=hb[:, k, HPAD + t0:HPAD + t0 + n],
                            start=(k == 0), stop=(k == NCH - 1)), reads=wg.keys + hr, writes=[pk(pg)])
                    pu = fw.bank()
                    for k in range(NCH):
                        op("tensor", lambda t, k=k, f=f, pu=pu: t.matmul(
                            psb[pu][:, 0:n], lhsT=wu.ap[:, k, f * 128:(f + 1) * 128], rhs=hb[:, k, HPAD + t0:HPAD + t0 + n],
                            start=(k == 0), stop=(k == NCH - 1)), reads=wu.keys + hr, writes=[pk(pu)])
                    sl = av(SL_B[f % 2], 1 * K1, BF16)
                    op("scalar", lambda s, pg=pg, sl=sl: s.activation(out=sl.ap[:, 0:n], in_=psb[pg][:, 0:n], func=AF.Silu),
                       reads=[pk(pg)], writes=sl.keys)
                    op("vector", lambda v, pu=pu, sl=sl, f=f: v.tensor_tensor(
                        out=actv.ap[:, f, 0:n], in0=psb[pu][:, 0:n], in1=sl.ap[:, 0:n], op=ALU.mult),
                       reads=[pk(pu)] + sl.keys, writes=actv.keys)

            def down(tbi, actv):
                t0, n = TBS[tbi]
                for c in range(NCH):
                    pd = fw.bank()
                    for f in range(nf):
                        op("tensor", lambda t, f=f, c=c, pd=pd: t.matmul(
                            psb[pd][:, 0:n], lhsT=wd.ap[:, f, c * 128:(c + 1) * 128], rhs=actv.ap[:, f, 0:n],
                            start=(f == 0), stop=(f == nf - 1)), reads=wd.keys + actv.keys, writes=[pk(pd)])
                    op("vector", lambda v, c=c, pd=pd: v.tensor_tensor(
                        out=x[:, c, t0:t0 + n], in0=psb[pd][:, 0:n], in1=x[:, c, t0:t0 + n], op=ALU.add),
                       reads=[pk(pd), xk(c, tbi)], writes=[xk(c, tbi)])
                if each_gen is not None:
                    next(each_gen, None)
                if si == len(FF_SLICES) - 1 and tail_hook is not None:
                    if tbi >= 1:
                        tail_hook(tbi - 1)
                    if tbi == len(TBS) - 1:
                        tail_hook(tbi)

            prev = None
            for tbi in range(len(TBS)):
                actv = av(ACT_B[ctr % 2], 4 * K1, BF16, "p (f t) -> p f t", f=4)
                ctr += 1
                gate_up(tbi, actv)
                if prev is not None:
                    down(*prev)
                prev = (tbi, actv)
            down(*prev)
            if si == len(FF_SLICES) - 2 and mid_hook is not None:
                mid_hook()

    SG_B = [48 * K1, 50 * K1]
    T2_B = [52 * K1, 54 * K1]
    gctr = [0]

    def gated_block(tbi, c, mm_val, mm_gate, pview=None, xview=None, tmp_base=None):
        t0, n = TBS[tbi]
        pview = pview or (lambda a: a)
        xview = xview or (lambda a: a)
        pg = fw.bank()
        for i, (l, r, rk) in enumerate(mm_gate):
            op("tensor", lambda t, l=l, r=r, i=i: t.matmul(pview(psb[pg][:, 0:n]), lhsT=l, rhs=r, start=(i == 0),
                                                          stop=(i == len(mm_gate) - 1)), reads=rk, writes=[pk(pg)])
        pv = fw.bank()
        for i, (l, r, rk) in enumerate(mm_val):
            op("tensor", lambda t, l=l, r=r, i=i: t.matmul(pview(psb[pv][:, 0:n]), lhsT=l, rhs=r, start=(i == 0),
                                                          stop=(i == len(mm_val) - 1)), reads=rk, writes=[pk(pv)])
        b = gctr[0] % 2
        gctr[0] += 1
        if tmp_base is None:
            sg = av(SG_B[b], 2 * K1, F32)
            t2 = av(T2_B[b], 2 * K1, F32)
        else:
            sg = av(tmp_base + b * 4 * K1, 2 * K1, F32)
            t2 = av(tmp_base + b * 4 * K1 + 2 * K1, 2 * K1, F32)
        op("scalar", lambda s: s.activation(out=sg.ap[:, 0:n], in_=psb[pg][:, 0:n], func=AF.Sigmoid),
           reads=[pk(pg)], writes=sg.keys)
        op("vector", lambda v: v.tensor_tensor(out=t2.ap[:, 0:n], in0=psb[pv][:, 0:n], in1=sg.ap[:, 0:n], op=ALU.mult),
           reads=[pk(pv)] + sg.keys, writes=t2.keys)
        xv = xview(x[:, c, t0:t0 + n])
        op("gpsimd", lambda g: g.tensor_tensor(out=xv, in0=xv, in1=pview(t2.ap[:, 0:n]), op=ALU.add),
           reads=t2.keys + [xk(c, tbi)], writes=[xk(c, tbi)])

    def ple_views():
        wgt = av(0, 16 * K1, BF16, "p (k n) -> p k n", k=8)
        win = av(16 * K1, 4 * K1, BF16, "p (k n) -> p k n", k=2)
        pT = av(58 * K1, 2 * NT * 2, BF16, "p (k t) -> p k t", k=2)
        return wgt, win, pT

    def ple_weights(layer):
        wgt, win, pT = ple_views()
        fw.dma("gpsimd", wgt.ap, ple_w_gate[layer].rearrange("(k p) n -> p k n", p=128), writes=wgt.keys)
        fw.dma("gpsimd", win.ap, ple_w_in[layer].rearrange("(k p) n -> p k n", p=128), writes=win.keys)

    def ple_ptrans(layer):
        wgt, win, pT = ple_views()
        pin = [av(90 * K1, 1 * K1, F32), av(91 * K1, 1 * K1, F32)]
        for tt in range(NT // 128):
            pi_ = pin[tt % 2]
            src = pp[layer, tt * 128:(tt + 1) * 128, :] if tt < 16 else psm[layer]
            fw.dma("sync", pi_.ap, src, writes=pi_.keys)
            b = fw.bank()
            for k in range(2):
                op("tensor", lambda t, k=k, b=b, pi_=pi_: t.transpose(psb[b][:, k * 128:(k + 1) * 128],
                                                                     pi_.ap[:, k * 128:(k + 1) * 128], ident),
                   reads=pi_.keys + ["ident"], writes=[pk(b)])
            evac_copy(pT.ap[:, :, tt * 128:(tt + 1) * 128], psb[b][:, 0:256].rearrange("p (k t) -> p k t", k=2),
                      reads=[pk(b)], writes=pT.keys)
            yield

    def ple_norm(layer, tbi):
        norm_to_hb(tbi, "g_ple%d" % layer)

    def ple_body(layer, tb_gen=None):
        wgt, win, pT = ple_views()
        pending = None
        for tbi, (t0, n) in enumerate(TBS):
            hr = [hk(k, tbi) for k in range(NCH)]
            for c in range(NCH):
                mm_gate = [(wgt.ap[:, k, c * 128:(c + 1) * 128], hb[:, k, HPAD + t0:HPAD + t0 + n], wgt.keys + hr)
                           for k in range(NCH)]
                mm_val = [(win.ap[:, k, c * 128:(c + 1) * 128], pT.ap[:, k, t0:t0 + n], win.keys + pT.keys)
                          for k in range(2)]
                gated_block(tbi, c, mm_val, mm_gate)
                if pending is not None and c in (0, 2, 4, 6):
                    next(pending, None)
            if pending is not None:
                for _ in pending:
                    pass
            if tb_gen is not None:
                pending = tb_gen(tbi)
        if pending is not None:
            for _ in pending:
                pass

    def pool_mixer(pre_hook=None, post_gen=None):
        wpool = av(24 * K1, 4 * K1, BF16, "p (g k n) -> p g k n", g=4, k=2)
        fw.dma("gpsimd", wpool.ap, pool_w.rearrange("g (k p) n -> p g k n", p=128), writes=wpool.keys)
        hs = av(28 * K1, 8 * NSEQ * 23 * 2, BF16, "p (c s t) -> p c s t", c=8, s=NSEQ)
        inv = av(34 * K1, 4 * 16 * 4, F32, "p (g t) -> p g t", g=4)
        wpos = av(35 * K1, 4 * K1, BF16, "p (g k n) -> p g k n", g=4, k=2)
        sp = [av(42 * K1, 4 * K1, F32), av(46 * K1, 4 * K1, F32)]
        h32t = av(50 * K1, 8 * 15 * 4, F32, "p (c t) -> p c t", c=8)
        h32s = av(51 * K1, 8 * 128 * 4, F32, "p (c t) -> p c t", c=8)
        ot = av(42 * K1, 4 * K1, F32)
        ot2 = av(46 * K1, 4 * K1, F32)
        p2s = [av(55 * K1, 2 * K1, F32), av(57 * K1, 2 * K1, F32)]
        t2b = [av(59 * K1, 2 * K1, F32), av(61 * K1, 2 * K1, F32)]
        for gi, w in enumerate(POOL_W):
            op("vector", lambda g, gi=gi, w=w: g.tensor_scalar(out=wpos.ap[:, gi], in0=wpool.ap[:, gi], scalar1=1.0 / w, scalar2=None,
                                                               op0=ALU.mult), reads=wpool.keys, writes=wpos.keys)
        op("vector", lambda g: g.tensor_scalar(out=wpool.ap, in0=wpool.ap, scalar1=-1.0, scalar2=None, op0=ALU.mult),
           reads=wpool.keys + wpos.keys, writes=wpool.keys)
        for gi, w in enumerate(POOL_W):
            op("vector", lambda g, gi=gi: g.memset(inv.ap[:, gi, :], 1.0), writes=inv.keys)
            for t in range(w - 1):
                op("vector", lambda g, gi=gi, t=t, w=w: g.memset(inv.ap[:, gi, t:t + 1], float(w) / (t + 1)), writes=inv.keys)
        for hf in range(2):
            fw.dma("sync", sp[hf].ap[0:120, :], spool[hf * 120:(hf + 1) * 120, :], writes=sp[hf].keys)
            for half in range(2):
                b = fw.bank()
                for cc in range(4):
                    c = half * 4 + cc
                    op("tensor", lambda t, b=b, cc=cc, c=c, hf=hf: t.transpose(
                        psb[b][:, cc * 120:(cc + 1) * 120], sp[hf].ap[0:120, c * 128:(c + 1) * 128], ident[0:120, 0:120]),
                       reads=sp[hf].keys + ["ident"], writes=[pk(b)])
                for cc in range(4):
                    c = half * 4 + cc
                    evac_copy(hs.ap[:, c, hf * 8:(hf + 1) * 8, 0:15],
                              psb[b][:, cc * 120:(cc + 1) * 120].rearrange("p (s t) -> p s t", s=8),
                              reads=[pk(b)], writes=hs.keys)
        fw.dma("sync", pool_s.rearrange("(s r) d -> s r d", r=15)[:, 0:7, :],
               spool.rearrange("(s r) d -> s r d", r=15)[:, 8:15, :])
        def pool_norm_tb(tbi):
            t0, n = TBS[tbi]
            box = []
            if tbi < 4:
                g_ = norm_gen(tbi, "g_mix0", out=box)
            else:
                g_ = norm_gen(tbi, "g_mix0", dst_fn=lambda c, rstd: (
                    hs.ap[:, c, :, 15:23], x[:, c, t0:t0 + n].rearrange("p (s t) -> p s t", s=NSEQ),
                    rstd.ap[:, 0:n].rearrange("p (s t) -> p s t", s=NSEQ), hs.keys), out=box)
            next(g_)
            yield
            for _ in g_:
                pass
            rstd = box[0]
            if tbi == 3:
                for c in range(NCH):
                    op("vector", lambda v, c=c, rstd=rstd: v.scalar_tensor_tensor(
                        out=h32t.ap[:, c, :], in0=x[:, c, TP - 15:TP], scalar=gs("g_mix0", c), in1=rstd.ap[:, 512 - 15:512],
                        op0=ALU.mult, op1=ALU.mult), reads=[xk(c, 3), "gvec"] + rstd.keys, writes=h32t.keys)
                for half in range(2):
                    b = fw.bank()
                    for cc in range(4):
                        c = half * 4 + cc
                        op("tensor", lambda t, b=b, cc=cc, c=c: t.transpose(psb[b][0:15, cc * 128:(cc + 1) * 128],
                                                                         h32t.ap[:, c, :], ident),
                           reads=h32t.keys + ["ident"], writes=[pk(b)])
                    evac_copy(ot.ap[0:15, half * 512:(half + 1) * 512], psb[b][0:15, :], reads=[pk(b)], writes=ot.keys)
                fw.dma("sync", pool_p, ot.ap[0:15, :], reads=ot.keys)
            if tbi == 4:
                for c in range(NCH):
                    op("vector", lambda v, c=c, rstd=rstd: v.scalar_tensor_tensor(
                        out=h32s.ap[:, c, :], in0=x[:, c, TP:NT], scalar=gs("g_mix0", c), in1=rstd.ap[:, 0:128],
                        op0=ALU.mult, op1=ALU.mult), reads=[xk(c, 4), "gvec"] + rstd.keys, writes=h32s.keys)
                for half in range(2):
                    b = fw.bank()
                    for cc in range(4):
                        c = half * 4 + cc
                        op("tensor", lambda t, b=b, cc=cc, c=c: t.transpose(psb[b][:, cc * 128:(cc + 1) * 128],
                                                                         h32s.ap[:, c, :], ident),
                           reads=h32s.keys + ["ident"], writes=[pk(b)])
                    evac_copy(ot2.ap[:, half * 512:(half + 1) * 512], psb[b], reads=[pk(b)], writes=ot2.keys)
                for s in range(NSEQ):
                    fw.dma("sync", pool_s[s * 15 + 7:s * 15 + 15, :], ot2.ap[s * 8:(s + 1) * 8, :], reads=ot2.keys)
        pctr_ = [0]

        def pool_mm_tb(tbi, pending=None):
            t0, n = TBS[tbi]
            pctr = pctr_[0]
            for gi, w in enumerate(POOL_W):
                for oc in range(2):
                    c = 2 * gi + oc
                    if pending is not None and c == 3:
                        for _ in pending:
                            pass
                    if tbi < 4:
                        rk = [hk(2 * gi + k, tbi) for k in range(2)] + ([hk(2 * gi + k, tbi - 1) for k in range(2)] if tbi else ["hpad"])
                        rhs = lambda k, d: hb[:, 2 * gi + k, HPAD + t0 - d:HPAD + t0 - d + n]
                    else:
                        rk = hs.keys
                        rhs = lambda k, d: hs.ap[:, 2 * gi + k, :, 15 - d:23 - d]
                    p1 = fw.bank()
                    po1 = psb[p1][:, 0:n] if tbi < 4 else psb[p1][:, 0:n].rearrange("p (s t) -> p s t", s=NSEQ)
                    fused = tbi > 0
                    nmm = 2 * w + (2 if fused else 0)
                    i = 0
                    for d in range(w):
                        for k in range(2):
                            op("tensor", lambda t, k=k, d=d, i=i, po1=po1, rhs=rhs: t.matmul(
                                po1, lhsT=wpos.ap[:, gi, k, oc * 128:(oc + 1) * 128], rhs=rhs(k, d),
                                start=(i == 0), stop=(i == nmm - 1)), reads=wpos.keys + rk, writes=[pk(p1)])
                            i += 1
                    pb_ = pctr % 2
                    pctr += 1
                    t2 = t2b[pb_]
                    if fused:
                        for k in range(2):
                            op("tensor", lambda t, k=k, i=i, po1=po1, rhs=rhs: t.matmul(
                                po1, lhsT=wpool.ap[:, gi, k, oc * 128:(oc + 1) * 128], rhs=rhs(k, 0),
                                start=False, stop=(i == nmm - 1)), reads=wpool.keys + rk, writes=[pk(p1)])
                            i += 1
                        op("vector", lambda v, p1=p1, c=c: v.scalar_tensor_tensor(
                            out=x[:, c, t0:t0 + n], in0=psb[p1][:, 0:n], scalar=gs("pool_scale", c), in1=x[:, c, t0:t0 + n],
                            op0=ALU.mult, op1=ALU.add), reads=[pk(p1), xk(c, tbi), "gvec"], writes=[xk(c, tbi)])
                        continue
                    p2 = fw.bank()
                    po2 = psb[p2][:, 0:n]
                    for k in range(2):
                        op("tensor", lambda t, k=k, po2=po2, rhs=rhs: t.matmul(
                            po2, lhsT=wpool.ap[:, gi, k, oc * 128:(oc + 1) * 128], rhs=rhs(k, 0),
                            start=(k == 0), stop=(k == 1)), reads=wpool.keys + rk, writes=[pk(p2)])
                    s2 = p2s[pb_]
                    tf = av(67 * K1, 64, F32)
                    op("scalar", lambda s, p2=p2, s2=s2: s.copy(out=s2.ap[:, 0:n], in_=psb[p2][:, 0:n]),
                       reads=[pk(p2)], writes=s2.keys)
                    op("vector", lambda v, p1=p1, t2=t2, s2=s2: v.tensor_tensor(
                        out=t2.ap[:, 0:n], in0=psb[p1][:, 0:n], in1=s2.ap[:, 0:n], op=ALU.add),
                       reads=[pk(p1)] + s2.keys, writes=t2.keys)
                    op("vector", lambda v, p1=p1, tf=tf: v.tensor_tensor(
                        out=tf.ap[:, 0:16], in0=psb[p1][:, 0:16], in1=inv.ap[:, gi, :], op=ALU.mult),
                       reads=[pk(p1)] + inv.keys, writes=tf.keys)
                    op("vector", lambda v, t2=t2, s2=s2, tf=tf: v.tensor_tensor(
                        out=t2.ap[:, 0:16], in0=tf.ap[:, 0:16], in1=s2.ap[:, 0:16], op=ALU.add),
                       reads=tf.keys + s2.keys + t2.keys, writes=t2.keys)
                    op("vector", lambda v, t2=t2, c=c: v.scalar_tensor_tensor(
                        out=x[:, c, t0:t0 + n], in0=t2.ap[:, 0:n], scalar=gs("pool_scale", c), in1=x[:, c, t0:t0 + n],
                        op0=ALU.mult, op1=ALU.add), reads=t2.keys + [xk(c, tbi), "gvec"], writes=[xk(c, tbi)])
            pctr_[0] = pctr

        if pre_hook is not None:
            pre_hook(0)
        for _ in pool_norm_tb(0):
            pass
        for tbi in range(5):
            pending = None
            if tbi + 1 < 5:
                if pre_hook is not None:
                    pre_hook(tbi + 1)
                pending = pool_norm_tb(tbi + 1)
                next(pending)
            fpend = post_gen(tbi) if post_gen is not None else None
            if fpend is not None:
                next(fpend, None)
            pool_mm_tb(tbi, pending)
            if pending is not None:
                for _ in pending:
                    pass
            if fpend is not None:
                for _ in fpend:
                    pass

    def ssm_mixer():
        TWO_S = 1.5957691216057308
        MUL, ADD, SUB = ALU.mult, ALU.add, ALU.subtract
        G = av(0, 36 * K1, F32, "p (n c) -> p n c", n=NBX)
        Bw = av(36 * K1, 8 * K1, BF16, "p (r e m) -> p r e m", r=2, e=16)
        Cw = av(36 * K1, 10 * K1, BF16, "p (r e b m) -> p r e b m", r=2, e=16, b=5)
        CB = [0, 1, 2, 4]
        ZB = av(46 * K1, 256, BF16, "p (r m) -> p r m", r=2)
        Lw = av(47 * K1, 4 * K1, BF16, "p (t m) -> p t m", t=16)
        Sc = av(51 * K1, 2 * 4 * NBX * 2, BF16, "p (r j n) -> p r j n", r=2, j=4)
        Pr = av(54 * K1, 2176, F32, "p (j e) -> p j e", j=32)
        Pi = av(54 * K1 + 2176, 2176, F32, "p (j e) -> p j e", j=32)
        Bbr = av(59 * K1, 2 * K1, F32, "p (j h) -> p j h", j=32)
        Bbi = av(61 * K1, 2 * K1, F32, "p (j h) -> p j h", j=32)
        Cr = av(63 * K1, 2 * K1, F32, "p (j h) -> p j h", j=32)
        Ci = av(65 * K1, 2 * K1, F32, "p (j h) -> p j h", j=32)
        Bbm = av(67 * K1, 4 * K1, BF16, "p (j r m) -> p j r m", j=32, r=2)
        C0m = av(71 * K1, 4 * K1, BF16, "p (j r m) -> p j r m", j=32, r=2)
        XBm = av(75 * K1, 8 * K1, F32, "p (e j m) -> p e j m", e=16, j=4)
        tAf = av(83 * K1, 4 * K1, F32)
        tBf = av(87 * K1, 4 * K1, F32)
        H0 = av(93 * K1, 4 * K1, F32, "p (s c) -> p s c", s=NSEQ)

        def hbu(c):
            return hb[:, c + 1, HPAD:HW].rearrange("p (r n) -> p r n", r=LCH)

        def hbz(c):
            return hb[:, c, HPAD:HW].rearrange("p (r n) -> p r n", r=LCH)

        def u_keys(c):
            return [hk(c + 1, t_) for t_ in range(5)]

        def z_keys(c):
            return [hk(c, t_) for t_ in range(5)]

        def sm(i):
            return av(91 * K1 + i * 128, 128, F32)

        def vop(out, in0, in1, o, reads, writes, eng="vector"):
            op(eng, lambda v: v.tensor_tensor(out=out, in0=in0, in1=in1, op=o), reads=reads, writes=writes)

        def ssm_norm():
            for c in range(NCH):
                op("gpsimd", lambda g, c=c: g.memset(hbu(c)[:, 0:8, NB:NBX], 0.0), writes=u_keys(c))
            for tbi, (t0, n) in enumerate(TBS):
                if tbi < 4:
                    n0 = t0 // LCH
                    norm_to_hb(tbi, "g_mix1", dst_fn=lambda c, rstd: (
                        hbu(c)[:, :, n0:n0 + 32], x[:, c, t0:t0 + n].rearrange("p (n r) -> p r n", r=LCH),
                        rstd.ap[:, 0:n].rearrange("p (n r) -> p r n", r=LCH), u_keys(c)))
                else:
                    norm_to_hb(tbi, "g_mix1", dst_fn=lambda c, rstd: (
                        hbu(c)[:, 8:16, NB:NBX], x[:, c, t0:t0 + n].rearrange("p (s t) -> p t s", t=TSEQ),
                        rstd.ap[:, 0:n].rearrange("p (s t) -> p t s", t=TSEQ), u_keys(c)))

        lrow = [av(75 * K1 + i * 512, 512, F32) for i in range(2)]
        ldt = av(76 * K1, 64, F32)
        ldx = av(76 * K1 + 512, 512, F32)
        fw.dma("sync", lrow[0].ap[0:32, :], lam_re, writes=lrow[0].keys)
        fw.dma("sync", lrow[1].ap[0:32, :], lam_im, writes=lrow[1].keys)
        fw.dma("sync", ldt.ap[0:32, 0:2], log_dt, writes=ldt.keys)
        inj_t = [[av(83 * K1 + (ri * 4 + jb) * 512, 512, F32) for jb in range(4)] for ri in range(2)]
        for ri, csrc in enumerate([c_re, c_im]):
            for jb in range(4):
                for g_ in range(2):
                    fw.dma("sync", inj_t[ri][jb].ap[:, g_ * 64:(g_ + 1) * 64], csrc[jb * 8:(jb + 1) * 8, g_], writes=inj_t[ri][jb].keys)
        braw = [av(87 * K1, 2 * K1, F32, "p (j h) -> p j h", j=32), av(89 * K1, 2 * K1, F32, "p (j h) -> p j h", j=32)]
        fw.dma("sync", braw[0].ap, b_re.rearrange("j q h -> q j h"), writes=braw[0].keys)
        fw.dma("sync", braw[1].ap, b_im.rearrange("j q h -> q j h"), writes=braw[1].keys)
        hl_t = [[av(67 * K1 + (ri * 4 + sb) * 512, 512, F32) for sb in range(4)] for ri in range(2)]
        for ri, hsrc in enumerate([sre, sim]):
            for sb in range(4):
                fw.dma("sync", hl_t[ri][sb].ap, hsrc[sb * 128:(sb + 1) * 128, :], writes=hl_t[ri][sb].keys)
        yield
        op("vector", lambda v: v.tensor_copy(out=ldx.ap[0:32, :].rearrange("j (g p) -> j g p", g=2),
                                             in_=ldt.ap[0:32, 0:2].unsqueeze(2).broadcast_to([32, 2, 64])),
           reads=ldt.keys, writes=ldx.keys)
        b = fw.bank()
        for i, s_ in enumerate([lrow[0], lrow[1], ldx]):
            op("tensor", lambda t, i=i, s_=s_: t.transpose(psb[b][:, i * 32:(i + 1) * 32], s_.ap[0:32, :], ident[0:32, 0:32]),
               reads=s_.keys + ["ident"], writes=[pk(b)])
        lr, li, dt = sm(0), sm(1), sm(2)
        op("vector", lambda v: v.tensor_copy(out=lr.ap, in_=psb[b][:, 0:32]), reads=[pk(b)], writes=lr.keys)
        op("vector", lambda v: v.tensor_copy(out=li.ap, in_=psb[b][:, 32:64]), reads=[pk(b)], writes=li.keys)
        op("scalar", lambda s: s.activation(out=dt.ap, in_=psb[b][:, 64:96], func=AF.Exp), reads=[pk(b)], writes=dt.keys)
        ar, ai, mg, cs, sn, zr, zi = sm(3), sm(4), sm(5), sm(6), sm(7), sm(8), sm(9)
        t1, t2, t3, t4, t5 = sm(10), sm(11), sm(12), sm(13), sm(14)
        vop(ar.ap, lr.ap, dt.ap, MUL, lr.keys, ar.keys)
        vop(ai.ap, li.ap, dt.ap, MUL, li.keys, ai.keys)
        op("scalar", lambda s: s.activation(out=mg.ap, in_=ar.ap, func=AF.Exp, scale=1.0 / 16), reads=ar.keys, writes=mg.keys)
        op("scalar", lambda s: s.activation(out=cs.ap, in_=ai.ap, func=AF.Sin, scale=1.0 / 32, bias=0.0),
           reads=ai.keys, writes=cs.keys)
        op("vector", lambda v: v.tensor_tensor(out=cs.ap, in0=cs.ap, in1=cs.ap, op=MUL), reads=cs.keys, writes=cs.keys)
        op("vector", lambda v: v.tensor_scalar(out=cs.ap, in0=cs.ap, scalar1=-2.0, scalar2=1.0, op0=MUL, op1=ADD),
           reads=cs.keys, writes=cs.keys)
        op("scalar", lambda s: s.activation(out=sn.ap, in_=ai.ap, func=AF.Sin, scale=1.0 / 16, bias=0.0),
           reads=ai.keys, writes=sn.keys)
        vop(zr.ap, mg.ap, cs.ap, MUL, mg.keys, zr.keys)
        vop(zi.ap, mg.ap, sn.ap, MUL, mg.keys, zi.keys)
        sk = sm(0).keys + sm(15).keys
        for _ in range(4):
            vop(t1.ap, zr.ap, zr.ap, MUL, sk, sk)
            vop(t2.ap, zi.ap, zi.ap, MUL, sk, sk)
            vop(t3.ap, zr.ap, zi.ap, MUL, sk, sk)
            vop(zr.ap, t1.ap, t2.ap, SUB, sk, sk)
            vop(zi.ap, t3.ap, t3.ap, ADD, sk, sk)
        lbr, lbi = zr, zi
        vop(t1.ap, lr.ap, lr.ap, MUL, sk, sk)
        vop(t2.ap, li.ap, li.ap, MUL, sk, sk)
        vop(t1.ap, t1.ap, t2.ap, ADD, sk, sk)
        op("vector", lambda v: v.reciprocal(out=t1.ap, in_=t1.ap), reads=sk, writes=sk)
        op("vector", lambda v: v.tensor_scalar(out=t2.ap, in0=lbr.ap, scalar1=-1.0, scalar2=None, op0=ADD), reads=sk, writes=sk)
        vop(t3.ap, t2.ap, lr.ap, MUL, sk, sk)
        vop(t4.ap, lbi.ap, li.ap, MUL, sk, sk)
        vop(t3.ap, t3.ap, t4.ap, ADD, sk, sk)
        vop(t3.ap, t3.ap, t1.ap, MUL, sk, sk)
        vop(t4.ap, lbi.ap, lr.ap, MUL, sk, sk)
        vop(t5.ap, t2.ap, li.ap, MUL, sk, sk)
        vop(t4.ap, t4.ap, t5.ap, SUB, sk, sk)
        vop(t4.ap, t4.ap, t1.ap, MUL, sk, sk)
        fre, fim = t3, t4
        pk_ = Pr.keys + Pi.keys
        op("vector", lambda v: v.memset(Pr.ap[:, :, 0:1], 1.0), writes=pk_)
        op("vector", lambda v: v.memset(Pi.ap[:, :, 0:1], 0.0), writes=pk_)
        op("vector", lambda v: v.tensor_copy(out=Pr.ap[:, :, 1:2], in_=lbr.ap.unsqueeze(2)), reads=sk, writes=pk_)
        op("vector", lambda v: v.tensor_copy(out=Pi.ap[:, :, 1:2], in_=lbi.ap.unsqueeze(2)), reads=sk, writes=pk_)
        tk = tAf.keys + tBf.keys
        pta, ptb = av(79 * K1, 1 * K1, F32), av(80 * K1, 1 * K1, F32)
        ptk = pta.keys + ptb.keys
        m = 1
        while m < 16:
            ta = pta.ap[:, 0:32 * m].rearrange("p (j e) -> p j e", j=32)
            tb_ = ptb.ap[:, 0:32 * m].rearrange("p (j e) -> p j e", j=32)
            prs, pis = Pr.ap[:, :, 1:m + 1], Pi.ap[:, :, 1:m + 1]
            prm = Pr.ap[:, :, m:m + 1].broadcast_to([128, 32, m])
            pim = Pi.ap[:, :, m:m + 1].broadcast_to([128, 32, m])
            vop(ta, prs, prm, MUL, pk_, ptk)
            vop(tb_, pis, pim, MUL, pk_, ptk)
            vop(Pr.ap[:, :, m + 1:2 * m + 1], ta, tb_, SUB, ptk + pk_, pk_)
            vop(ta, prs, pim, MUL, pk_, ptk)
            vop(tb_, pis, prm, MUL, pk_, ptk)
            vop(Pi.ap[:, :, m + 1:2 * m + 1], ta, tb_, ADD, ptk + pk_, pk_)
            m *= 2
        for ri, (csrc, cdst) in enumerate([(c_re, Cr), (c_im, Ci)]):
            b = fw.bank()
            for jb in range(4):
                inj = inj_t[ri][jb]
                op("tensor", lambda t, jb=jb, inj=inj, b=b: t.transpose(psb[b][:, jb * 128:(jb + 1) * 128], inj.ap, ident),
                   reads=inj.keys + ["ident"], writes=[pk(b)])
            op("vector", lambda v, b=b, cdst=cdst, ri=ri: v.tensor_scalar(
                out=cdst.ap, in0=psb[b].rearrange("p (j h) -> p j h", j=32), scalar1=(1.0 if ri == 0 else -1.0), scalar2=None,
                op0=MUL), reads=[pk(b)], writes=cdst.keys)
        ua = av(77 * K1, 2 * K1, F32, "p (j h) -> p j h", j=32)
        ub = av(79 * K1, 2 * K1, F32, "p (j h) -> p j h", j=32)
        freb = fre.ap.unsqueeze(2).broadcast_to([128, 32, 16])
        fimb = fim.ap.unsqueeze(2).broadcast_to([128, 32, 16])
        bk = braw[0].keys + braw[1].keys + sk
        uk = ua.keys + ub.keys
        vop(ua.ap, braw[0].ap, freb, MUL, bk, uk)
        vop(ub.ap, braw[1].ap, fimb, MUL, bk, uk)
        vop(Bbr.ap, ua.ap, ub.ap, SUB, uk, Bbr.keys)
        vop(ua.ap, braw[1].ap, freb, MUL, bk, uk)
        vop(ub.ap, braw[0].ap, fimb, MUL, bk, uk)
        vop(Bbi.ap, ua.ap, ub.ap, ADD, uk, Bbi.keys)
        def build_masked():
            op("gpsimd", lambda g: g.memset(Bbm.ap, 0.0), writes=Bbm.keys)
            op("gpsimd", lambda g: g.memset(C0m.ap, 0.0), writes=C0m.keys)
            for hf in range(2):
                ps_, cs_ = slice(hf * 64, (hf + 1) * 64), slice(hf * 16, (hf + 1) * 16)
                op("vector", lambda v, ps_=ps_, cs_=cs_: v.tensor_copy(out=Bbm.ap[ps_, :, 0, cs_], in_=Bbr.ap[ps_]),
                   reads=Bbr.keys, writes=Bbm.keys)
                op("vector", lambda v, ps_=ps_, cs_=cs_: v.tensor_copy(out=Bbm.ap[ps_, :, 1, cs_], in_=Bbi.ap[ps_]),
                   reads=Bbi.keys, writes=Bbm.keys)
                op("vector", lambda v, ps_=ps_, cs_=cs_: v.tensor_copy(out=C0m.ap[ps_, :, 0, cs_], in_=Cr.ap[ps_]),
                   reads=Cr.keys, writes=C0m.keys)
                op("vector", lambda v, ps_=ps_, cs_=cs_: v.tensor_copy(out=C0m.ap[ps_, :, 1, cs_], in_=Ci.ap[ps_]),
                   reads=Ci.keys, writes=C0m.keys)
        for ri, hsrc in enumerate([sre, sim]):
            b = fw.bank()
            for sb in range(4):
                hl = hl_t[ri][sb]
                op("tensor", lambda t, sb=sb, hl=hl, b=b: t.transpose(psb[b][:, sb * 128:(sb + 1) * 128], hl.ap, ident),
                   reads=hl.keys + ["ident"], writes=[pk(b)])
            op("vector", lambda v, b=b, ri=ri: v.tensor_copy(out=H0.ap[:, :, ri * 32:(ri + 1) * 32],
                                                             in_=psb[b].rearrange("p (s j) -> p s j", s=NSEQ)),
               reads=[pk(b)], writes=H0.keys)
        ssm_norm()
        XBmb = [XBm, av(67 * K1, 8 * K1, F32, "p (e j m) -> p e j m", e=16, j=4)]
        Bwb = [Bw, av(44 * K1, 8 * K1, BF16, "p (r e m) -> p r e m", r=2, e=16)]
        for X_ in XBmb:
            op("gpsimd", lambda g, X_=X_: g.memset(X_.ap, 0.0), writes=X_.keys)
        tA4 = tAf.ap.rearrange("p (a b h) -> p a b h", a=16, b=4)
        tB4 = tBf.ap.rearrange("p (a b h) -> p a b h", a=16, b=4)

        def bgen(c, ri):
            j0 = 4 * c
            X_ = XBmb[ri]
            prb = Pr.ap[:, j0:j0 + 4, 0:16].rearrange("p j e -> p e j").unsqueeze(3).broadcast_to([128, 16, 4, 16])
            pib = Pi.ap[:, j0:j0 + 4, 0:16].rearrange("p j e -> p e j").unsqueeze(3).broadcast_to([128, 16, 4, 16])
            bbr = Bbr.ap[:, j0:j0 + 4, :].unsqueeze(1).broadcast_to([128, 16, 4, 16])
            bbi = Bbi.ap[:, j0:j0 + 4, :].unsqueeze(1).broadcast_to([128, 16, 4, 16])
            rk_ = pk_ + Bbr.keys + Bbi.keys
            if ri == 0:
                vop(tA4, prb, bbr, MUL, rk_, tAf.keys)
                vop(tB4, pib, bbi, MUL, rk_, tBf.keys, eng="gpsimd")
                o_ = SUB
            else:
                vop(tA4, prb, bbi, MUL, rk_, tAf.keys)
                vop(tB4, pib, bbr, MUL, rk_, tBf.keys, eng="gpsimd")
                o_ = ADD
            for hf in range(2):
                ps_, cs_ = slice(hf * 64, (hf + 1) * 64), slice(hf * 16, (hf + 1) * 16)
                vop(X_.ap[ps_, :, :, cs_], tA4[ps_], tB4[ps_], o_, tk, X_.keys)

        def bbanks(c):
            return [0, 1, 2, 3] if c % 2 == 0 else [4, 5, 6, 7]

        def btrans(c, ri):
            X_, W_ = XBmb[ri], Bwb[c % 2]
            for q4 in range(4):
                b = bbanks(c)[q4]
                for ee in range(4):
                    e = q4 * 4 + ee
                    op("tensor", lambda t, b=b, ee=ee, e=e: t.transpose(
                        psb[b][:, ee * 128:(ee + 1) * 128], X_.ap[:, e].rearrange("p j m -> p (j m)"), ident),
                       reads=X_.keys + ["ident"], writes=[pk(b)])
                evac_copy(W_.ap[:, ri, q4 * 4:(q4 + 1) * 4, :], psb[b].rearrange("p (e m) -> p e m", e=4),
                          reads=[pk(b)], writes=W_.keys, eng="scalar")

        def bside_mm(c, jjs):
            W_, uv, hr, bk4 = Bwb[c % 2], hbu(c), u_keys(c), bbanks(c)
            for ri in range(2):
                for k in range(LCH):
                    for jj in jjs:
                        rs = slice(32 * jj, 32 * jj + 32) if jj < 3 else slice(64, 128)
                        pb = bk4[jj]
                        op("tensor", lambda t, k=k, pb=pb, ri=ri, rs=rs: t.matmul(
                            psb[pb][:, ri * NBX:(ri + 1) * NBX], lhsT=W_.ap[rs, ri, 15 - k, :],
                            rhs=uv[rs, k, :], start=(k == 0), stop=(k == LCH - 1)),
                           reads=W_.keys + hr, writes=[pk(pb)])

        for ri in range(2):
            bgen(0, ri)
            btrans(0, ri)
        for c in range(NCH):
            j0 = 4 * c
            if c + 1 < NCH:
                bgen(c + 1, 0)
                bgen(c + 1, 1)
            bside_mm(c, [0, 1, 2])
            op("vector", lambda v: v.memset(Bwb[c % 2].ap[64:96, 0], 0.0), reads=[], writes=Bwb[c % 2].keys)
            op("gpsimd", lambda g: g.memset(Bwb[c % 2].ap[64:96, 1], 0.0), reads=[], writes=Bwb[c % 2].keys)
            if c + 1 < NCH:
                btrans(c + 1, 0)
                btrans(c + 1, 1)
            bside_mm(c, [3])
            bk4 = bbanks(c)
            for jj in range(4):
                evac_copy(G.ap[:, :, j0 + jj:64:32], psb[bk4[jj]][:, 0:2 * NBX].rearrange("p (r n) -> p n r", r=2),
                          reads=[pk(bk4[jj])], writes=G.keys)

        Cwb = [Cw, av(18 * K1, 10 * K1, BF16, "p (r e b m) -> p r e b m", r=2, e=16, b=5)]
        Lwb = [Lw, av(59 * K1, 4 * K1, BF16, "p (t m) -> p t m", t=16)]
        Scb = [Sc, av(28 * K1, 2 * 4 * NBX * 2, BF16, "p (r j n) -> p r j n", r=2, j=4)]
        ZBb = [ZB, av(31 * K1, 256, BF16, "p (r m) -> p r m", r=2)]
        build_masked()
        for t_ in [Cwb[0], Lwb[0], Lwb[1], Scb[0], ZBb[0]]:
            op("gpsimd", lambda g, t_=t_: g.memset(t_.ap, 0.0), reads=Bbm.keys + C0m.keys, writes=t_.keys)
        fa = av(87 * K1, 2 * K1, F32, "p (s j) -> p s j", s=NSEQ)
        fb = av(89 * K1, 2 * K1, F32, "p (s j) -> p s j", s=NSEQ)
        fk = fa.keys + fb.keys
        H0r, H0i = H0.ap[:, :, 0:32], H0.ap[:, :, 32:64]
        q1, q2, q3 = sm(10), sm(11), sm(12)
        vop(q1.ap, Pr.ap[:, :, 8], Pr.ap[:, :, 8], MUL, pk_, sk)
        vop(q2.ap, Pi.ap[:, :, 8], Pi.ap[:, :, 8], MUL, pk_, sk)
        vop(q1.ap, q1.ap, q2.ap, ADD, sk, sk)
        op("vector", lambda v: v.reciprocal(out=q1.ap, in_=q1.ap), reads=sk, writes=sk)
        vop(q2.ap, Pr.ap[:, :, 8], q1.ap, MUL, pk_ + sk, sk)
        op("vector", lambda v: v.scalar_tensor_tensor(out=q3.ap, in0=Pi.ap[:, :, 8], scalar=-1.0, in1=q1.ap, op0=MUL, op1=MUL),
           reads=pk_ + sk, writes=sk)
        m8r = q2.ap.unsqueeze(1).broadcast_to([128, NSEQ, 32])
        m8i = q3.ap.unsqueeze(1).broadcast_to([128, NSEQ, 32])
        fc = av(85 * K1, 2 * K1, F32, "p (s j) -> p s j", s=NSEQ)
        vop(fa.ap, H0r, m8r, MUL, H0.keys + sk, fk)
        vop(fb.ap, H0i, m8i, MUL, H0.keys + sk, fk)
        vop(fc.ap, fa.ap, fb.ap, SUB, fk, fc.keys)
        vop(fa.ap, H0i, m8r, MUL, H0.keys + sk, fk)
        vop(fb.ap, H0r, m8i, MUL, H0.keys + sk, fk)
        vop(H0i, fa.ap, fb.ap, ADD, fk, H0.keys)
        op("vector", lambda v: v.tensor_copy(out=H0r, in_=fc.ap), reads=fc.keys, writes=H0.keys)

        A1 = av(91 * K1, 256, F32); A2 = av(91 * K1 + 256, 256, F32)
        A1q = av(91 * K1 + 512, 256, F32); A2q = av(91 * K1 + 768, 256, F32)
        s1 = av(92 * K1, 256, F32); s2 = av(92 * K1 + 256, 256, F32)
        QTr = av(75 * K1, 1 * K1, F32, "p (k j) -> p k j", k=8)
        QTi = av(76 * K1, 1 * K1, F32, "p (k j) -> p k j", k=8)
        qa = av(77 * K1, 512, F32); qb = av(77 * K1 + 512, 512, F32)
        qk = QTr.keys + QTi.keys + qa.keys
        ak = A1.keys + s1.keys
        op("vector", lambda v: v.tensor_copy(out=QTr.ap[:, 0, :], in_=Pr.ap[:, :, 16]), reads=pk_, writes=qk)
        op("vector", lambda v: v.tensor_copy(out=QTi.ap[:, 0, :], in_=Pi.ap[:, :, 16]), reads=pk_, writes=qk)
        m = 1
        while m < 8:
            ta = qa.ap[:, 0:32 * m].rearrange("p (k j) -> p k j", j=32)
            tb_ = qb.ap[:, 0:32 * m].rearrange("p (k j) -> p k j", j=32)
            qrs, qis = QTr.ap[:, 0:m, :], QTi.ap[:, 0:m, :]
            qrm = QTr.ap[:, m - 1:m, :].broadcast_to([128, m, 32])
            qim = QTi.ap[:, m - 1:m, :].broadcast_to([128, m, 32])
            vop(ta, qrs, qrm, MUL, qk, qk)
            vop(tb_, qis, qim, MUL, qk, qk)
            vop(QTr.ap[:, m:2 * m, :], ta, tb_, SUB, qk, qk)
            vop(ta, qrs, qim, MUL, qk, qk)
            vop(tb_, qis, qrm, MUL, qk, qk)
            vop(QTi.ap[:, m:2 * m, :], ta, tb_, ADD, qk, qk)
            m *= 2
        for (A1_, A2_, kq) in ((A1, A2, 0), (A1q, A2q, 7)):
            for hf in range(2):
                op("vector", lambda v, hf=hf, A1_=A1_, kq=kq: v.tensor_copy(out=A1_.ap[:, hf * 32:(hf + 1) * 32], in_=QTr.ap[:, kq, :]),
                   reads=qk, writes=ak)
            op("vector", lambda v, A2_=A2_, kq=kq: v.tensor_scalar(out=A2_.ap[:, 0:32], in0=QTi.ap[:, kq, :], scalar1=-1.0, scalar2=None,
                                                                  op0=MUL), reads=qk, writes=ak)
            op("vector", lambda v, A2_=A2_, kq=kq: v.tensor_copy(out=A2_.ap[:, 32:64], in_=QTi.ap[:, kq, :]), reads=qk, writes=ak)
        NBLK, BL = 16, 8
        Gv = G.ap[:, 0:NB, :].rearrange("p (a b) c -> p a b c", b=BL)
        S1 = tAf.ap.rearrange("p (a c) -> p a c", a=NBLK)
        S2 = tBf.ap.rearrange("p (a c) -> p a c", a=NBLK)
        A1b = A1.ap.unsqueeze(1).broadcast_to([128, NBLK, 64])
        A2lo = A2.ap[:, 0:32].unsqueeze(1).broadcast_to([128, NBLK, 32])
        A2hi = A2.ap[:, 32:64].unsqueeze(1).broadcast_to([128, NBLK, 32])
        sk2 = tAf.keys + tBf.keys
        for b_ in range(1, BL):
            rd = G.keys + ak if b_ == 1 else ["gscan"] + ak
            src_, dst_ = Gv[:, :, b_ - 1, :], Gv[:, :, b_, :]
            vop(S1, A1b, src_, MUL, rd, sk2)
            vop(S2[:, :, 0:32], A2lo, src_[:, :, 32:64], MUL, rd, sk2)
            vop(S2[:, :, 32:64], A2hi, src_[:, :, 0:32], MUL, rd, sk2)
            vop(dst_, dst_, S1, ADD, sk2 + ["gscan"], ["gscan"])
            vop(dst_, dst_, S2, ADD, sk2 + ["gscan"], ["gscan"])
        for a_ in range(1, NBLK):
            prev, cur = Gv[:, a_ - 1, BL - 1, :], Gv[:, a_, BL - 1, :]
            vop(s1.ap, A1q.ap, prev, MUL, ["gscan"] + ak, ak)
            vop(s2.ap[:, 0:32], A2q.ap[:, 0:32], prev[:, 32:64], MUL, ["gscan"] + ak, ak)
            vop(s2.ap[:, 32:64], A2q.ap[:, 32:64], prev[:, 0:32], MUL, ["gscan"] + ak, ak)
            vop(cur, cur, s1.ap, ADD, ak + ["gscan"], ["gscan"])
            vop(cur, cur, s2.ap, ADD, ak + ["gscan"], ["gscan"])
        for a0 in range(1, NBLK, 4):
            na = min(4, NBLK - a0)
            shp = [128, na, BL - 1, 32]
            cre = Gv[:, a0 - 1:a0 - 1 + na, BL - 1, 0:32].unsqueeze(2).broadcast_to(shp)
            cim = Gv[:, a0 - 1:a0 - 1 + na, BL - 1, 32:64].unsqueeze(2).broadcast_to(shp)
            qr_ = QTr.ap[:, 0:BL - 1, :].unsqueeze(1).broadcast_to(shp)
            qi_ = QTi.ap[:, 0:BL - 1, :].unsqueeze(1).broadcast_to(shp)
            dre = Gv[:, a0:a0 + na, 0:BL - 1, 0:32]
            dim_ = Gv[:, a0:a0 + na, 0:BL - 1, 32:64]
            tre = tAf.ap[:, 0:na * 7 * 32].rearrange("p (a b j) -> p a b j", a=na, b=BL - 1)
            tim = tBf.ap[:, 0:na * 7 * 32].rearrange("p (a b j) -> p a b j", a=na, b=BL - 1)
            last = (a0 + 4 >= NBLK)
            wre = (G.keys if last else []) + ["gscan_re"]
            wim = (G.keys if last else []) + ["gscan_im"]
            vop(tre, qr_, cre, MUL, ["gscan"] + qk, tAf.keys)
            vop(dre, dre, tre, ADD, tAf.keys + ["gscan", "gscan_re"], ["gscan_re"])
            vop(tre, qi_, cim, MUL, ["gscan"] + qk, tAf.keys)
            vop(dre, dre, tre, SUB, tAf.keys + ["gscan", "gscan_re"], wre)
            ie = "gpsimd" if a0 < 9 else "vector"
            vop(tim, qr_, cim, MUL, ["gscan"] + qk, tBf.keys, eng=ie)
            vop(dim_, dim_, tim, ADD, tBf.keys + ["gscan", "gscan_im"], ["gscan_im"], eng=ie)
            vop(tim, qi_, cre, MUL, ["gscan"] + qk, tBf.keys, eng=ie)
            vop(dim_, dim_, tim, ADD, tBf.keys + ["gscan", "gscan_im"], wim, eng=ie)
        Sall = av(0, 18 * K1, BF16, "p (n c) -> p n c", n=NBX)
        stmp = av(83 * K1, 9 * 64 * 2, BF16, "p (n c) -> p n c", n=9)
        op("vector", lambda v: v.tensor_copy(out=stmp.ap, in_=G.ap[:, 0:9, :]), reads=G.keys, writes=stmp.keys)
        op("vector", lambda v: v.tensor_copy(out=Sall.ap[:, 0:9, :], in_=stmp.ap), reads=stmp.keys, writes=G.keys)
        for (a_, b_) in ((9, 18), (18, 36), (36, 72), (72, 144)):
            op("vector", lambda v, a_=a_, b_=b_: v.tensor_copy(out=Sall.ap[:, a_:b_, :], in_=G.ap[:, a_:b_, :]),
               reads=G.keys, writes=G.keys)
        for t_ in [Cwb[1], Scb[1], ZBb[1]]:
            op("gpsimd", lambda g, t_=t_: g.memset(t_.ap, 0.0), writes=t_.keys)
        GT = [[av(75 * K1, 2 * K1, F32), av(77 * K1, 2 * K1, F32)], [av(79 * K1, 2 * K1, F32), av(81 * K1, 2 * K1, F32)]]
        gctr2 = [0]

        def gelu_bank(pbank, width, dst, in_view, wkeys):
            g1, g2 = GT[gctr2[0] % 2]
            gctr2[0] += 1
            ps_ = psb[pbank][:, 0:width]
            op("scalar", lambda s: s.activation(out=g1.ap[:, 0:width], in_=ps_, func=AF.Square), reads=[pk(pbank)], writes=g1.keys)
            op("vector", lambda v: v.tensor_scalar(out=g1.ap[:, 0:width], in0=g1.ap[:, 0:width], scalar1=0.044715, scalar2=1.0,
                                                   op0=MUL, op1=ADD), reads=g1.keys, writes=g1.keys)
            op("vector", lambda v: v.tensor_tensor(out=g2.ap[:, 0:width], in0=ps_, in1=g1.ap[:, 0:width], op=MUL),
               reads=[pk(pbank)] + g1.keys, writes=g2.keys)
            op("scalar", lambda s: s.activation(out=g2.ap[:, 0:width], in_=g2.ap[:, 0:width], func=AF.Sigmoid, scale=TWO_S),
               reads=g2.keys, writes=g2.keys)
            op("vector", lambda v: v.tensor_tensor(out=dst, in0=in_view(ps_), in1=in_view(g2.ap[:, 0:width]), op=MUL),
               reads=[pk(pbank)] + g2.keys, writes=wkeys)

        tA5 = tAf.ap.rearrange("p (a b h) -> p a b h", a=4, b=16)
        tB5 = tBf.ap.rearrange("p (a b h) -> p a b h", a=4, b=16)

        def cgen(c):
            j0, Cw_ = 4 * c, Cwb[c % 2]
            prb = Pr.ap[:, j0:j0 + 4, 1:17].unsqueeze(3).broadcast_to([128, 4, 16, 16])
            pib = Pi.ap[:, j0:j0 + 4, 1:17].unsqueeze(3).broadcast_to([128, 4, 16, 16])
            crb = Cr.ap[:, j0:j0 + 4, :].unsqueeze(2).broadcast_to([128, 4, 16, 16])
            cib = Ci.ap[:, j0:j0 + 4, :].unsqueeze(2).broadcast_to([128, 4, 16, 16])
            rk_ = pk_ + Cr.keys + Ci.keys

            def cw_write(ri, o2):
                for hf in range(2):
                    ps_, cs_ = slice(hf * 64, (hf + 1) * 64), slice(hf * 16, (hf + 1) * 16)
                    for (jsl, bsl) in ((slice(0, 3), slice(0, 3)), (slice(3, 4), slice(4, 5))):
                        o_ = Cw_.ap[ps_, ri, :, bsl, cs_]
                        a_ = tA5[ps_, jsl].rearrange("p j r h -> p r j h")
                        b_ = tB5[ps_, jsl].rearrange("p j r h -> p r j h")
                        vop(o_, a_, b_, o2, tk, Cw_.keys, eng="gpsimd")

            vop(tA5, prb, crb, MUL, rk_, tAf.keys, eng="gpsimd")
            vop(tB5, pib, cib, MUL, rk_, tBf.keys, eng="gpsimd")
            cw_write(0, ADD)
            vop(tA5, prb, cib, MUL, rk_, tAf.keys, eng="gpsimd")
            vop(tB5, pib, crb, MUL, rk_, tBf.keys, eng="gpsimd")
            cw_write(1, SUB)

        def clags(c):
            j0, Cw_, Lw_, ZB_, Sc_ = 4 * c, Cwb[c % 2], Lwb[c % 2], ZBb[c % 2], Scb[c % 2]
            op("vector", lambda v: v.tensor_copy(out=ZB_.ap[:, :, 32:64], in_=Bbm.ap[:, j0 + 3, :, :]), reads=Bbm.keys, writes=ZB_.keys)
            lb_ = fw.bank()
            lb3 = fw.bank()
            for jj in range(4):
                if jj < 3:
                    rs, bank_ = slice(32 * jj, 32 * jj + 32), lb_
                    lt = lambda ri: Bbm.ap[:, j0 + jj, ri, :]
                    lk = Bbm.keys
                else:
                    rs, bank_ = slice(64, 128), lb3
                    lt = lambda ri: ZB_.ap[:, ri, :]
                    lk = ZB_.keys
                for ri in range(2):
                    op("tensor", lambda t, ri=ri: t.matmul(
                        psb[bank_][rs, 0:32], lhsT=lt(ri), rhs=C0m.ap[:, j0 + jj, ri, :],
                        start=(ri == 0), stop=(ri == 1)), reads=lk + C0m.keys, writes=[pk(bank_)])
                for ri in range(2):
                    op("tensor", lambda t, ri=ri: t.matmul(
                        psb[bank_][rs, 32:512].rearrange("p (e m) -> p e m", e=15), lhsT=lt(ri),
                        rhs=Cw_.ap[:, ri, 0:15, CB[jj], :],
                        start=(ri == 0), stop=(ri == 1)), reads=lk + Cw_.keys, writes=[pk(bank_)])
            for jj in range(3):
                rs = slice(32 * jj, 32 * jj + 32)
                evac_copy(Lw_.ap[rs, :, 32 * jj:32 * jj + 32], psb[lb_][rs, :].rearrange("p (t m) -> p t m", t=16),
                          reads=[pk(lb_)], writes=Lw_.keys)
            evac_copy(Lw_.ap[64:128, :, 96:128], psb[lb3][64:128, :].rearrange("p (t m) -> p t m", t=16),
                      reads=[pk(lb3)], writes=Lw_.keys)
            op("vector", lambda v: v.scalar_tensor_tensor(out=Lw_.ap[:, 0, :], in0=ident, scalar=gs("ssm_d", c), in1=Lw_.ap[:, 0, :],
                                                          op0=MUL, op1=ADD), reads=Lw_.keys + ["ident", "gvec"], writes=Lw_.keys)
            op("vector", lambda v: v.tensor_copy(
                out=Sc_.ap[:, :, :, 1:NB],
                in_=Sall.ap[:, 0:NB - 1, :].rearrange("p n (r j) -> p r j n", r=2)[:, :, j0:j0 + 4, :]),
               reads=Sall.keys, writes=Sc_.keys)
            op("vector", lambda v: v.tensor_copy(
                out=Sc_.ap[:, :, :, NB:NBX],
                in_=H0.ap.rearrange("p s (r j) -> p r j s", r=2)[:, :, j0:j0 + 4, :]),
               reads=H0.keys, writes=Sc_.keys)

        RG = [(0, 3), (3, 3), (6, 3), (9, 3), (12, 3), (15, 1)]

        def cy(c, groups):
            Cw_, Lw_, Sc_ = Cwb[c % 2], Lwb[c % 2], Scb[c % 2]
            hr, uv, zv = u_keys(c), hbu(c), hbz(c)
            for (r0, nr) in groups:
                yb_ = fw.bank()
                for tau in range(r0 + nr):
                    ra = max(r0, tau)
                    nrow = r0 + nr - ra
                    op("tensor", lambda t, tau=tau, ra=ra, nrow=nrow, yb_=yb_: t.matmul(
                        psb[yb_][:, (ra - r0) * NBX:(ra - r0 + nrow) * NBX], lhsT=Lw_.ap[:, tau, :],
                        rhs=uv[:, ra - tau:ra - tau + nrow, :].rearrange("p r n -> p (r n)"),
                        start=(tau == 0), stop=False, skip_group_check=True), reads=Lw_.keys + hr, writes=[pk(yb_)])
                for rr in range(nr):
                    r = r0 + rr
                    cols = slice(rr * NBX, (rr + 1) * NBX)
                    for jj in range(4):
                        rs = slice(32 * jj, 32 * jj + 32) if jj < 3 else slice(64, 128)
                        for ri in range(2):
                            lt_ = Cw_.ap[:, ri, r, CB[jj], :] if jj < 3 else Cw_.ap[:, ri, r, 3:5, :].rearrange("p b m -> p (b m)")
                            op("tensor", lambda t, jj=jj, ri=ri, rs=rs, r=r, cols=cols, yb_=yb_, lt_=lt_: t.matmul(
                                psb[yb_][rs, cols], lhsT=lt_, rhs=Sc_.ap[:, ri, jj, :],
                                start=False, stop=(jj == 3 and ri == 1), skip_group_check=True),
                               reads=Cw_.keys + Sc_.keys, writes=[pk(yb_)])
                gelu_bank(yb_, nr * NBX, zv[:, r0:r0 + nr, :].rearrange("p r n -> p (r n)"), lambda a: a, z_keys(c))

        wa = av(0, 16 * K1, BF16, "p (k n) -> p k n", k=8)
        wg2 = av(36 * K1, 16 * K1, BF16, "p (k n) -> p k n", k=8)
        cgen(0)
        clags(0)
        for c in range(NCH):
            if c + 1 < NCH:
                cgen(c + 1)
            else:
                fw.dma("gpsimd", wa.ap, w_glu[:, 0:D].rearrange("(k p) n -> p k n", p=128), writes=wa.keys)
                fw.dma("gpsimd", wg2.ap, w_glu[:, D:2 * D].rearrange("(k p) n -> p k n", p=128), writes=wg2.keys)
            cy(c, RG[0:5])
            if c + 1 < NCH:
                clags(c + 1)
            cy(c, RG[5:6])

        stp = av(91 * K1 - 1 * K1, 1 * K1, F32)
        b = fw.bank()
        for ri in range(2):
            op("tensor", lambda t, ri=ri, b=b: t.transpose(psb[b][0:32, ri * 128:(ri + 1) * 128],
                                                          G.ap[:, NB - 1, ri * 32:(ri + 1) * 32], ident),
               reads=G.keys + ["ident"], writes=[pk(b)])
        evac_copy(stp.ap[0:32, 0:256], psb[b][0:32, 0:256], reads=[pk(b)], writes=stp.keys)
        fw.dma("sync", sre_p, stp.ap[0:32, 0:128], reads=stp.keys)
        fw.dma("sync", sim_p, stp.ap[0:32, 128:256], reads=stp.keys)
        fa = av(87 * K1, 2 * K1, F32, "p (s j) -> p s j", s=NSEQ)
        fb = av(89 * K1, 2 * K1, F32, "p (s j) -> p s j", s=NSEQ)
        p8r = Pr.ap[:, :, 16].unsqueeze(1).broadcast_to([128, NSEQ, 32])
        p8i = Pi.ap[:, :, 16].unsqueeze(1).broadcast_to([128, NSEQ, 32])
        fk = fa.keys + fb.keys
        Gre, Gim = G.ap[:, NB:NBX, 0:32], G.ap[:, NB:NBX, 32:64]
        H0r, H0i = H0.ap[:, :, 0:32], H0.ap[:, :, 32:64]
        vop(fa.ap, H0r, p8r, MUL, H0.keys + pk_, fk)
        vop(fb.ap, H0i, p8i, MUL, H0.keys + pk_, fk)
        vop(Gre, Gre, fa.ap, ADD, fk + G.keys, G.keys)
        vop(Gre, Gre, fb.ap, SUB, fk + G.keys, G.keys)
        vop(fa.ap, H0i, p8r, MUL, H0.keys + pk_, fk)
        vop(fb.ap, H0r, p8i, MUL, H0.keys + pk_, fk)
        vop(Gim, Gim, fa.ap, ADD, fk + G.keys, G.keys)
        vop(Gim, Gim, fb.ap, ADD, fk + G.keys, G.keys)
        fst = av(83 * K1, 4 * K1, F32, "p (r s j) -> p r s j", r=2, s=NSEQ)
        op("vector", lambda v: v.tensor_copy(out=fst.ap, in_=G.ap[:, NB:NBX, :].rearrange("p s (r j) -> p r s j", r=2)),
           reads=G.keys, writes=fst.keys)
        for ri, dst in enumerate([sre_s, sim_s]):
            b = fw.bank()
            for sb in range(4):
                op("tensor", lambda t, sb=sb, b=b, ri=ri: t.transpose(
                    psb[b][:, sb * 128:(sb + 1) * 128], fst.ap[:, ri, sb * 4:(sb + 1) * 4, :].rearrange("p s j -> p (s j)"), ident),
                   reads=fst.keys + ["ident"], writes=[pk(b)])
            so = av(75 * K1 + ri * 2 * K1, 2 * K1, F32)
            evac_copy(so.ap, psb[b], reads=[pk(b)], writes=so.keys)
            fw.dma("sync", dst.rearrange("(sb r) q -> r sb q", r=128), so.ap.rearrange("p (sb q) -> p sb q", sb=4), reads=so.keys)
        for tbi, (t0, n) in enumerate(TBS):
            hr = [hk(k, t_) for k in range(NCH) for t_ in range(5)]
            if tbi < 4:
                n0 = t0 // LCH
                zr_ = lambda k: hbz(k)[:, :, n0:n0 + 32]
                pview = lambda a: a.rearrange("p (r n) -> p r n", r=LCH)
                xview = lambda a: a.rearrange("p (n r) -> p r n", r=LCH)
            else:
                zr_ = lambda k: hbz(k)[:, 8:16, NB:NBX]
                pview = lambda a: a.rearrange("p (t s) -> p t s", t=TSEQ)
                xview = lambda a: a.rearrange("p (s t) -> p t s", t=TSEQ)
            for c in range(NCH):
                mm_val = [(wa.ap[:, k, c * 128:(c + 1) * 128], zr_(k), wa.keys + hr) for k in range(NCH)]
                mm_gate = [(wg2.ap[:, k, c * 128:(c + 1) * 128], zr_(k), wg2.keys + hr) for k in range(NCH)]
                gated_block(tbi, c, mm_val, mm_gate, pview=pview, xview=xview, tmp_base=75 * K1)

    octr_ = [0]

    def final_tb(tbi):
        yfin = av(28 * K1, 16 * K1, F32, "p (c t) -> p c t", c=8)
        yo = [av(20 * K1, 4 * K1, F32), av(24 * K1, 4 * K1, F32)]
        t0, n = TBS[tbi]
        b = nctr[0] % 2
        nctr[0] += 1
        sq = av(SQB[b], 8 * K1, BF16, "p (c t) -> p c t", c=8)
        for c in range(NCH):
            op("scalar", lambda s, c=c: s.activation(out=sq.ap[:, c, 0:n], in_=x[:, c, t0:t0 + n], func=AF.Square),
               reads=[xk(c, tbi)], writes=sq.keys)
        yield
        pb = fw.bank()
        for c in range(NCH):
            op("tensor", lambda t, c=c: t.matmul(psb[pb][:, 0:n], lhsT=ones_bf, rhs=sq.ap[:, c, 0:n],
                                                 start=(c == 0), stop=(c == NCH - 1)),
               reads=sq.keys + ["ones"], writes=[pk(pb)])
        rt = av(RT, 2 * K1, F32)
        rstd = av(RSTD[b], 2 * K1, F32)
        op("scalar", lambda s: s.activation(out=rt.ap[:, 0:n], in_=psb[pb][:, 0:n], func=AF.Ln, scale=1.0 / D, bias=EPS),
           reads=[pk(pb)], writes=rt.keys)
        op("scalar", lambda s: s.activation(out=rstd.ap[:, 0:n], in_=rt.ap[:, 0:n], func=AF.Exp, scale=-0.5),
           reads=rt.keys, writes=rstd.keys)
        for c in range(NCH):
            op("vector", lambda v, c=c: v.scalar_tensor_tensor(
                out=yfin.ap[:, c, 0:n], in0=x[:, c, t0:t0 + n], scalar=gs("g_final", c), in1=rstd.ap[:, 0:n],
                op0=ALU.mult, op1=ALU.mult), reads=[xk(c, tbi), "gvec"] + rstd.keys, writes=yfin.keys)
        yield
        for q in range(n // 128):
            tt = t0 // 128 + q
            o = yo[octr_[0] % 2]
            octr_[0] += 1
            for half in range(2):
                b_ = fw.bank()
                for cc in range(4):
                    c = half * 4 + cc
                    op("tensor", lambda t, b_=b_, cc=cc, c=c, q=q: t.transpose(
                        psb[b_][:, cc * 128:(cc + 1) * 128], yfin.ap[:, c, q * 128:(q + 1) * 128], ident),
                       reads=yfin.keys + ["ident"], writes=[pk(b_)])
                evac_copy(o.ap[:, half * 512:(half + 1) * 512], psb[b_], reads=[pk(b_)], writes=o.keys)
            dst = y_p[tt * 128:(tt + 1) * 128, :] if tt < 16 else y_s
            fw.dma("sync", dst, o.ap, reads=o.keys)
            if q % 2 == 1:
                yield

    load_ffn_slice(0, 0)
    pool_mixer(pre_hook=x_load_tb,
               post_gen=lambda tbi: norm_gen(tbi - 1, "g_ffn0") if tbi >= 1 else None)
    norm_to_hb(4, "g_ffn0")
    ffn(0, pre_normed=True, each_gen=ple_ptrans(0), mid_hook=lambda: ple_weights(0),
        tail_hook=lambda tbi: ple_norm(0, tbi))
    ssm_gen = ssm_mixer() if enable_ssm else iter(())
    next(ssm_gen, None)
    ple_body(0)
    for _ in ssm_gen:
        pass
    load_ffn_slice(1, 0)
    ffn(1, each_gen=ple_ptrans(1), mid_hook=lambda: ple_weights(1), tail_hook=lambda tbi: ple_norm(1, tbi))
    ple_body(1, tb_gen=final_tb)
    fw.finish("sync")
    return nc


_NC_CACHE = {}


def _get_program(enable_ssm=True):
    if enable_ssm not in _NC_CACHE:
        _NC_CACHE[enable_ssm] = build_program(enable_ssm)
    return _NC_CACHE[enable_ssm]


def kernel(x_prompt, x_sample, state_pool, state_ssm_re, state_ssm_im, p_prompt, p_sample,
           g_mix, g_ffn, g_ple, g_final, pool_w, pool_scale,
           ssm_lambda_re, ssm_lambda_im, ssm_log_dt, ssm_b_re, ssm_b_im, ssm_c_re, ssm_c_im,
           ssm_d, ssm_w_glu, ffn_w_gate, ffn_w_up, ffn_w_down, ple_w_in, ple_w_gate, _enable_ssm=True):
    f = lambda a: np.ascontiguousarray(np.asarray(a, dtype=np.float32))
    x_prompt, x_sample, state_pool, state_ssm_re, state_ssm_im, p_prompt, p_sample = map(
        f, (x_prompt, x_sample, state_pool, state_ssm_re, state_ssm_im, p_prompt, p_sample))
    gvecs = np.ascontiguousarray(np.concatenate([f(g_mix), f(g_ffn), f(g_ple), f(g_final)[None, :], f(pool_scale), f(ssm_d)], axis=0))
    shared = {
        "gvecs": gvecs,
        "pool_w": f(pool_w)[0],
        "lam_re": f(ssm_lambda_re)[0].reshape(32, 128), "lam_im": f(ssm_lambda_im)[0].reshape(32, 128),
        "log_dt": f(ssm_log_dt)[0].reshape(32, 2),
        "b_re": f(ssm_b_re)[0].reshape(32, 128, 16), "b_im": f(ssm_b_im)[0].reshape(32, 128, 16),
        "c_re": f(ssm_c_re)[0].reshape(32, 2, 16, 64), "c_im": f(ssm_c_im)[0].reshape(32, 2, 16, 64),
        "w_glu": f(ssm_w_glu)[0],
        "w_gate": f(ffn_w_gate), "w_up": f(ffn_w_up), "w_down": f(ffn_w_down),
        "ple_w_in": f(ple_w_in), "ple_w_gate": f(ple_w_gate),
    }
    in_maps = []
    for i in range(NCORES):
        sl = slice(i * NSEQ, (i + 1) * NSEQ)
        m = dict(shared)
        m["xp"] = x_prompt[i]
        m["xs"] = x_sample[sl].reshape(TS, D)
        m["pp"] = np.ascontiguousarray(p_prompt[:, i])
        m["psm"] = np.ascontiguousarray(p_sample[:, sl].reshape(2, TS, PLE))
        m["spool"] = state_pool[0, sl].reshape(NSEQ * 15, D)
        m["sre"] = state_ssm_re[0, sl].reshape(NSEQ * 32, 128)
        m["sim"] = state_ssm_im[0, sl].reshape(NSEQ * 32, 128)
        in_maps.append(m)
    nc = _get_program(_enable_ssm)
    res = run_bass_kernel_spmd(nc, in_maps, core_ids=list(range(NCORES)))
    r = res.results
    y_prompt = np.stack([r[i]["y_p"] for i in range(NCORES)], 0)
    y_sample = np.concatenate([r[i]["y_s"].reshape(NSEQ, TSEQ, D) for i in range(NCORES)], 0)
    pool_prompt = np.stack([r[i]["pool_p"] for i in range(NCORES)], 0)[None]
    pool_sample = np.concatenate([r[i]["pool_s"].reshape(NSEQ, 15, D) for i in range(NCORES)], 0)[None]
    sre_p = np.stack([r[i]["sre_p"].reshape(64, 64) for i in range(NCORES)], 0)[None]
    sim_p = np.stack([r[i]["sim_p"].reshape(64, 64) for i in range(NCORES)], 0)[None]
    sre_s = np.concatenate([r[i]["sre_s"].reshape(NSEQ, 64, 64) for i in range(NCORES)], 0)[None]
    sim_s = np.concatenate([r[i]["sim_s"].reshape(NSEQ, 64, 64) for i in range(NCORES)], 0)[None]
    return (y_prompt, y_sample, pool_prompt, pool_sample, sre_p, sim_p, sre_s, sim_s)
```

```python
import numpy as np
import concourse.bass as bass
import concourse.mybir as mybir
from concourse.bass_utils import run_bass_kernel_spmd

F32 = mybir.dt.float32
BF16 = mybir.dt.bfloat16
I32 = mybir.dt.int32
AF = mybir.ActivationFunctionType
ALU = mybir.AluOpType

NCORES = 8
D = 1024
NCH = 8
TP = 2048
NSEQ = 16
TSEQ = 8
TS = NSEQ * TSEQ
NT = TP + TS
DFF = 2816
PLE = 256
EPS = 1e-6
HPAD = 16
NBX = 144
HW = HPAD + 16 * NBX
TBS = [(0, 512), (512, 512), (1024, 512), (1536, 512), (2048, 128)]
POOL_W = (2, 4, 8, 16)
FF_SLICES = [(0, 4), (4, 4), (8, 4), (12, 4), (16, 3), (19, 3)]
LCH = 16
NB = TP // LCH
GV = {"g_mix0": 0, "g_mix1": 1, "g_ffn0": 2, "g_ffn1": 3, "g_ple0": 4, "g_ple1": 5, "g_final": 6,
      "pool_scale": 7, "ssm_d": 8}
ARENA = 97 * 1024
GRAN = 1024


class FW:
    ENG = ("tensor", "vector", "scalar", "gpsimd", "sync")

    def __init__(self, nc, n_dma_sems=32):
        self.nc = nc
        self.eng = {e: getattr(nc, e) for e in self.ENG}
        self.esem = {e: nc.alloc_semaphore("es_" + e) for e in self.ENG}
        self.ecount = {e: 0 for e in self.ENG}
        self.waited = {e: {} for e in self.ENG}
        self.dsems = [nc.alloc_semaphore("ds%d" % i) for i in range(n_dma_sems)]
        self.dcount = [0] * n_dma_sems
        self.dnext = 0
        self.last_write = {}
        self.reads_since = {}
        self.ps_next = 0
        self.gsems = []

    def _wait(self, e, dep):
        sem, val = dep
        key = id(sem)
        if self.waited[e].get(key, 0) >= val:
            return
        self.eng[e].wait_ge(sem, val)
        self.waited[e][key] = val

    def _deps(self, reads, writes):
        d = []
        for k in reads:
            if k in self.last_write:
                d.append(self.last_write[k])
        for k in writes:
            if k in self.last_write:
                d.append(self.last_write[k])
            d.extend(self.reads_since.get(k, ()))
        return d

    def _commit(self, dep, reads, writes):
        for k in reads:
            lst = self.reads_since.setdefault(k, [])
            lst[:] = [x for x in lst if x[0] is not dep[0]]
            lst.append(dep)
        for k in writes:
            self.last_write[k] = dep
            self.reads_since[k] = []

    def op(self, e, fn, reads=(), writes=()):
        own = self.esem[e]
        for dep in self._deps(reads, writes):
            if e == "tensor" and dep[0] is own:
                continue
            self._wait(e, dep)
        inst = fn(self.eng[e])
        self.ecount[e] += 1
        inst.then_inc(own, 1)
        dep = (own, self.ecount[e])
        self._commit(dep, reads, writes)
        return dep

    def dma(self, e, out, in_, reads=(), writes=(), **kw):
        if e == "gpsimd":
            sem = self.nc.alloc_semaphore("gd%d" % len(self.gsems))
            self.gsems.append(sem)
            for dep in self._deps(reads, writes):
                self._wait(e, dep)
            self.eng[e].dma_start(out=out, in_=in_, **kw).then_inc(sem, 16)
            dep = (sem, 16)
            self._commit(dep, reads, writes)
            return dep
        i = self.dnext
        self.dnext = (self.dnext + 1) % len(self.dsems)
        sem = self.dsems[i]
        if self.dcount[i] > 0:
            self._wait(e, (sem, self.dcount[i]))
        for dep in self._deps(reads, writes):
            self._wait(e, dep)
        self.eng[e].dma_start(out=out, in_=in_, **kw).then_inc(sem, 16)
        self.dcount[i] += 16
        dep = (sem, self.dcount[i])
        self._commit(dep, reads, writes)
        return dep

    def finish(self, e="sync"):
        for i, sem in enumerate(self.dsems):
            if self.dcount[i] > 0:
                self._wait(e, (sem, self.dcount[i]))
        for sem in self.gsems:
            self._wait(e, (sem, 16))

    def bank(self):
        b = self.ps_next
        self.ps_next = (self.ps_next + 1) % 8
        return b


class V:
    def __init__(self, ap, keys):
        self.ap = ap
        self.keys = keys


def build_program(enable_ssm=True):
    nc = bass.Bass("TRN2", target_bir_lowering=False)
    fw = FW(nc)

    def din(name, shape):
        return nc.dram_tensor(name, list(shape), F32, kind="ExternalInput").ap()

    def dout(name, shape):
        return nc.dram_tensor(name, list(shape), F32, kind="ExternalOutput").ap()

    xp = din("xp", [TP, D]); xs = din("xs", [TS, D])
    pp = din("pp", [2, TP, PLE]); psm = din("psm", [2, TS, PLE])
    spool = din("spool", [NSEQ * 15, D])
    sre = din("sre", [NSEQ * 32, 128]); sim = din("sim", [NSEQ * 32, 128])
    gvecs = din("gvecs", [9, D])
    pool_w = din("pool_w", [4, 256, 256])
    lam_re = din("lam_re", [32, 128]); lam_im = din("lam_im", [32, 128]); log_dt = din("log_dt", [32, 2])
    b_re = din("b_re", [32, 128, 16]); b_im = din("b_im", [32, 128, 16])
    c_re = din("c_re", [32, 2, 16, 64]); c_im = din("c_im", [32, 2, 16, 64])
    w_glu = din("w_glu", [D, 2 * D])
    w_gate = din("w_gate", [2, D, DFF]); w_up = din("w_up", [2, D, DFF]); w_down = din("w_down", [2, DFF, D])
    ple_w_in = din("ple_w_in", [2, PLE, D]); ple_w_gate = din("ple_w_gate", [2, D, D])

    y_p = dout("y_p", [TP, D]); y_s = dout("y_s", [TS, D])
    pool_p = dout("pool_p", [15, D]); pool_s = dout("pool_s", [NSEQ * 15, D])
    sre_p = dout("sre_p", [32, 128]); sim_p = dout("sim_p", [32, 128])
    sre_s = dout("sre_s", [NSEQ * 32, 128]); sim_s = dout("sim_s", [NSEQ * 32, 128])

    x = nc.alloc_sbuf_tensor("x", [128, NCH, NT], F32).ap()
    hb = nc.alloc_sbuf_tensor("hb", [128, NCH + 1, HW], BF16).ap()
    ident = nc.alloc_sbuf_tensor("ident", [128, 128], F32).ap()
    ones_bf = nc.alloc_sbuf_tensor("ones_bf", [128, 128], BF16).ap()
    gvec = nc.alloc_sbuf_tensor("gvec", [128, NCH, 16], F32).ap()
    iot = nc.alloc_sbuf_tensor("iot", [128, 128], I32).ap()
    R = nc.alloc_sbuf_tensor("arena", [128, ARENA // 2], BF16).ap()
    psb = [nc.alloc_psum_tensor("ps%d" % i, [128, 512], F32).ap() for i in range(8)]

    def av(off, nbytes, dtype=BF16, pat=None, **dims):
        assert off % 4 == 0 and nbytes % 4 == 0 and off + nbytes <= ARENA, (off, nbytes)
        ap = R[:, off // 2:(off + nbytes) // 2]
        if dtype != BF16:
            ap = ap.bitcast(dtype)
        if pat is not None:
            ap = ap.rearrange(pat, **dims)
        keys = [("R", s) for s in range(off // GRAN, (off + nbytes + GRAN - 1) // GRAN)]
        return V(ap, keys)

    def xk(c, tbi):
        return ("x", c, tbi)

    def hk(c, tbi):
        return ("h", c, tbi)

    def pk(b):
        return ("ps", b)

    op = fw.op
    K1 = 1024

    op("gpsimd", lambda g: g.iota(iot, [[1, 128]], base=0, channel_multiplier=-1), writes=["iot"])
    op("vector", lambda v: v.tensor_scalar(out=ident, in0=iot, scalar1=0, scalar2=None, op0=ALU.is_equal),
       reads=["iot"], writes=["ident"])
    op("vector", lambda v: v.memset(ones_bf, 1.0), writes=["ones"])
    op("gpsimd", lambda g: g.memset(hb[:, :, 0:HPAD], 0.0), writes=["hpad"])

    gv_rows = av(88 * K1, 4096, F32)
    for i in range(9):
        fw.dma("sync", gv_rows.ap[i:i + 1, :], gvecs[i:i + 1, :], writes=gv_rows.keys)
    b0 = fw.bank()
    for c in range(NCH):
        op("tensor", lambda t, c=c: t.transpose(psb[b0][:, c * 16:c * 16 + 9], gv_rows.ap[0:9, c * 128:(c + 1) * 128],
                                                 ident[0:9, 0:9]),
           reads=gv_rows.keys + ["ident"], writes=[pk(b0)])
    op("vector", lambda v: v.tensor_copy(out=gvec[:, :, 0:9], in_=psb[b0][:, 0:128].rearrange("p (c i) -> p c i", i=16)[:, :, 0:9]),
       reads=[pk(b0)], writes=["gvec"])

    def gs(name, c):
        i = GV[name]
        return gvec[:, c, i:i + 1]

    WFF = [0, 24 * K1]

    def ffn_views(b, nf):
        o = WFF[b]
        wg = av(o, 8 * K1, BF16, "p (k n) -> p k n", k=8)
        wu = av(o + 8 * K1, 8 * K1, BF16, "p (k n) -> p k n", k=8)
        wd = av(o + 16 * K1, 8 * K1, BF16, "p (f n) -> p f n", f=4)
        return wg, wu, wd

    def load_ffn_slice(layer, si):
        f0, nf = FF_SLICES[si]
        b = si % 2
        wg, wu, wd = ffn_views(b, nf)
        c0, c1 = f0 * 128, (f0 + nf) * 128
        fw.dma("gpsimd", wg.ap[:, :, 0:nf * 128], w_gate[layer, :, c0:c1].rearrange("(k p) n -> p k n", p=128),
               writes=wg.keys)
        fw.dma("gpsimd", wu.ap[:, :, 0:nf * 128], w_up[layer, :, c0:c1].rearrange("(k p) n -> p k n", p=128),
               writes=wu.keys)
        fw.dma("gpsimd", wd.ap[:, 0:nf, :], w_down[layer, c0:c1, :].rearrange("(f p) n -> p f n", p=128),
               writes=wd.keys)

    xin = [av(63 * K1, 4 * K1, F32), av(93 * K1, 4 * K1, F32),
           V(hb[:, NCH, HPAD:HPAD + 2 * D].bitcast(F32), [hk(NCH, t_) for t_ in range(5)])]
    ev = [0]

    def evac_copy(out, in_, reads, writes, eng=None):
        ev[0] += 1
        if eng == "vector" or (eng is None and ev[0] % 2 == 0):
            op("vector", lambda v: v.tensor_copy(out=out, in_=in_), reads=reads, writes=writes)
        else:
            op("scalar", lambda s: s.copy(out=out, in_=in_), reads=reads, writes=writes)

    def x_load_tb(tbi):
        t0b, nb_ = TBS[tbi]
        for tt in range(t0b // 128, (t0b + nb_) // 128):
            xi = xin[tt % 3]
            src = xp[tt * 128:(tt + 1) * 128, :] if tt < 16 else xs
            fw.dma("sync", xi.ap, src, writes=xi.keys)
            t0 = tt * 128
            for half in range(2):
                b = fw.bank()
                for cc in range(4):
                    c = half * 4 + cc
                    op("tensor", lambda t, b=b, cc=cc, c=c, xi=xi: t.transpose(psb[b][:, cc * 128:(cc + 1) * 128],
                                                                             xi.ap[:, c * 128:(c + 1) * 128], ident),
                       reads=xi.keys + ["ident"], writes=[pk(b)])
                evac_copy(x[:, half * 4:half * 4 + 4, t0:t0 + 128], psb[b].rearrange("p (c t) -> p c t", c=4),
                          reads=[pk(b)], writes=[xk(c, tbi) for c in range(half * 4, half * 4 + 4)])

    SQB = [68 * K1, 76 * K1]
    RT = 84 * K1
    RSTD = [86 * K1, 88 * K1]
    nctr = [0]

    def norm_stats(tbi):
        t0, n = TBS[tbi]
        b = nctr[0] % 2
        nctr[0] += 1
        sq = av(SQB[b], 8 * K1, BF16, "p (c t) -> p c t", c=8)
        for c in range(NCH):
            op("scalar", lambda s, c=c: s.activation(out=sq.ap[:, c, 0:n], in_=x[:, c, t0:t0 + n], func=AF.Square),
               reads=[xk(c, tbi)], writes=sq.keys)
        pb = fw.bank()
        for c in range(NCH):
            op("tensor", lambda t, c=c: t.matmul(psb[pb][:, 0:n], lhsT=ones_bf, rhs=sq.ap[:, c, 0:n],
                                                 start=(c == 0), stop=(c == NCH - 1)),
               reads=sq.keys + ["ones"], writes=[pk(pb)])
        rt = av(RT, 2 * K1, F32)
        rstd = av(RSTD[b], 2 * K1, F32)
        op("scalar", lambda s: s.activation(out=rt.ap[:, 0:n], in_=psb[pb][:, 0:n], func=AF.Ln, scale=1.0 / D, bias=EPS),
           reads=[pk(pb)], writes=rt.keys)
        op("scalar", lambda s: s.activation(out=rstd.ap[:, 0:n], in_=rt.ap[:, 0:n], func=AF.Exp, scale=-0.5),
           reads=rt.keys, writes=rstd.keys)
        return rstd

    def norm_gen(tbi, gname, dst_fn=None, out=None):
        t0, n = TBS[tbi]
        b = nctr[0] % 2
        nctr[0] += 1
        sq = av(SQB[b], 8 * K1, BF16, "p (c t) -> p c t", c=8)
        for c in range(NCH):
            op("scalar", lambda s, c=c: s.activation(out=sq.ap[:, c, 0:n], in_=x[:, c, t0:t0 + n], func=AF.Square),
               reads=[xk(c, tbi)], writes=sq.keys)
        yield
        pb = fw.bank()
        for c in range(NCH):
            op("tensor", lambda t, c=c: t.matmul(psb[pb][:, 0:n], lhsT=ones_bf, rhs=sq.ap[:, c, 0:n],
                                                 start=(c == 0), stop=(c == NCH - 1)),
               reads=sq.keys + ["ones"], writes=[pk(pb)])
        rt = av(RT, 2 * K1, F32)
        rstd = av(RSTD[b], 2 * K1, F32)
        op("scalar", lambda s: s.activation(out=rt.ap[:, 0:n], in_=psb[pb][:, 0:n], func=AF.Ln, scale=1.0 / D, bias=EPS),
           reads=[pk(pb)], writes=rt.keys)
        op("scalar", lambda s: s.activation(out=rstd.ap[:, 0:n], in_=rt.ap[:, 0:n], func=AF.Exp, scale=-0.5),
           reads=rt.keys, writes=rstd.keys)
        for c in range(NCH):
            if dst_fn is None:
                o, in0, in1, wk = hb[:, c, HPAD + t0:HPAD + t0 + n], x[:, c, t0:t0 + n], rstd.ap[:, 0:n], [hk(c, tbi)]
            else:
                o, in0, in1, wk = dst_fn(c, rstd)
            op("vector", lambda v, o=o, in0=in0, in1=in1, c=c: v.scalar_tensor_tensor(
                out=o, in0=in0, scalar=gs(gname, c), in1=in1, op0=ALU.mult, op1=ALU.mult),
               reads=[xk(c, tbi), "gvec"] + rstd.keys, writes=wk)
        if out is not None:
            out.append(rstd)

    def norm_to_hb(tbi, gname, dst_fn=None):
        t0, n = TBS[tbi]
        rstd = norm_stats(tbi)
        for c in range(NCH):
            if dst_fn is None:
                o, in0, in1, wk = hb[:, c, HPAD + t0:HPAD + t0 + n], x[:, c, t0:t0 + n], rstd.ap[:, 0:n], [hk(c, tbi)]
            else:
                o, in0, in1, wk = dst_fn(c, rstd)
            op("vector", lambda v, o=o, in0=in0, in1=in1, c=c: v.scalar_tensor_tensor(
                out=o, in0=in0, scalar=gs(gname, c), in1=in1, op0=ALU.mult, op1=ALU.mult),
               reads=[xk(c, tbi), "gvec"] + rstd.keys, writes=wk)
        return rstd

    ACT_B = [48 * K1, 52 * K1]
    SL_B = [56 * K1, 57 * K1]

    def ffn(layer, pre_normed=False, start_hook=None, mid_hook=None, tail_hook=None, each_gen=None):
        gname = "g_ffn%d" % layer
        if not pre_normed:
            for tbi in range(2):
                norm_to_hb(tbi, gname)
        if start_hook is not None:
            start_hook()
        ctr = 0
        for si, (f0, nf) in enumerate(FF_SLICES):
            if si + 1 < len(FF_SLICES):
                load_ffn_slice(layer, si + 1)
            wg, wu, wd = ffn_views(si % 2, nf)

            def gate_up(tbi, actv):
                t0, n = TBS[tbi]
                hr = [hk(k, tbi) for k in range(NCH)]
                for f in range(nf):
                    pg = fw.bank()
                    for k in range(NCH):
                        op("tensor", lambda t, k=k, f=f, pg=pg: t.matmul(
                            psb[pg][:, 0:n], lhsT=wg.ap[:, k, f * 128:(f + 1) * 128], rhs=hb[:, k, HPAD + t0:HPAD + t0 + n],
                            start=(k == 0), stop=(k == NCH - 1)), reads=wg.keys + hr, writes=[pk(pg)])
                    pu = fw.bank()
                    for k in range(NCH):
                        op("tensor", lambda t, k=k, f=f, pu=pu: t.matmul(
                            psb[pu][:, 0:n], lhsT=wu.ap[:, k, f * 128:(f + 1) * 128], rhs=hb[:, k, HPAD + t0:HPAD + t0 + n],
                            start=(k == 0), stop=(k == NCH - 1)), reads=wu.keys + hr, writes=[pk(pu)])
                    sl = av(SL_B[f % 2], 1 * K1, BF16)
                    op("scalar", lambda s, pg=pg, sl=sl: s.activation(out=sl.ap[:, 0:n], in_=psb[pg][:, 0:n], func=AF.Silu),
                       reads=[pk(pg)], writes=sl.keys)
                    op("vector", lambda v, pu=pu, sl=sl, f=f: v.tensor_tensor(
                        out=actv.ap[:, f, 0:n], in0=psb[pu][:, 0:n], in1=sl.ap[:, 0:n], op=ALU.mult),
                       reads=[pk(pu)] + sl.keys, writes=actv.keys)

            def down(tbi, actv):
                t0, n = TBS[tbi]
                for c in range(NCH):
                    pd = fw.bank()
                    for f in range(nf):
                        op("tensor", lambda t, f=f, c=c, pd=pd: t.matmul(
                            psb[pd][:, 0:n], lhsT=wd.ap[:, f, c * 128:(c + 1) * 128], rhs=actv.ap[:, f, 0:n],
                            start=(f == 0), stop=(f == nf - 1)), reads=wd.keys + actv.keys, writes=[pk(pd)])
                    op("vector", lambda v, c=c, pd=pd: v.tensor_tensor(
                        out=x[:, c, t0:t0 + n], in0=psb[pd][:, 0:n], in1=x[:, c, t0:t0 + n], op=ALU.add),
                       reads=[pk(pd), xk(c, tbi)], writes=[xk(c, tbi)])
                if each_gen is not None:
                    next(each_gen, None)
                if si == len(FF_SLICES) - 1 and tail_hook is not None:
                    if tbi >= 1:
                        tail_hook(tbi - 1)
                    if tbi == len(TBS) - 1:
                        tail_hook(tbi)

            prev = None
            for tbi in range(len(TBS)):
                actv = av(ACT_B[ctr % 2], 4 * K1, BF16, "p (f t) -> p f t", f=4)
                ctr += 1
                gate_up(tbi, actv)
                if not pre_normed and si == 0 and tbi + 2 < len(TBS):
                    norm_to_hb(tbi + 2, gname)
                if prev is not None:
                    down(*prev)
                prev = (tbi, actv)
            down(*prev)
            if si == len(FF_SLICES) - 2 and mid_hook is not None:
                mid_hook()

    SG_B = [48 * K1, 50 * K1]
    T2_B = [52 * K1, 54 * K1]
    gctr = [0]

    def gated_block(tbi, c, mm_val, mm_gate, pview=None, xview=None, tmp_base=None):
        t0, n = TBS[tbi]
        pview = pview or (lambda a: a)
        xview = xview or (lambda a: a)
        pg = fw.bank()
        for i, (l, r, rk) in enumerate(mm_gate):
            op("tensor", lambda t, l=l, r=r, i=i: t.matmul(pview(psb[pg][:, 0:n]), lhsT=l, rhs=r, start=(i == 0),
                                                          stop=(i == len(mm_gate) - 1)), reads=rk, writes=[pk(pg)])
        pv = fw.bank()
        for i, (l, r, rk) in enumerate(mm_val):
            op("tensor", lambda t, l=l, r=r, i=i: t.matmul(pview(psb[pv][:, 0:n]), lhsT=l, rhs=r, start=(i == 0),
                                                          stop=(i == len(mm_val) - 1)), reads=rk, writes=[pk(pv)])
        b = gctr[0] % 2
        gctr[0] += 1
        if tmp_base is None:
            sg = av(SG_B[b], 2 * K1, F32)
            t2 = av(T2_B[b], 2 * K1, F32)
        else:
            sg = av(tmp_base + b * 4 * K1, 2 * K1, F32)
            t2 = av(tmp_base + b * 4 * K1 + 2 * K1, 2 * K1, F32)
        op("scalar", lambda s: s.activation(out=sg.ap[:, 0:n], in_=psb[pg][:, 0:n], func=AF.Sigmoid),
           reads=[pk(pg)], writes=sg.keys)
        op("vector", lambda v: v.tensor_tensor(out=t2.ap[:, 0:n], in0=psb[pv][:, 0:n], in1=sg.ap[:, 0:n], op=ALU.mult),
           reads=[pk(pv)] + sg.keys, writes=t2.keys)
        xv = xview(x[:, c, t0:t0 + n])
        op("gpsimd", lambda g: g.tensor_tensor(out=xv, in0=xv, in1=pview(t2.ap[:, 0:n]), op=ALU.add),
           reads=t2.keys + [xk(c, tbi)], writes=[xk(c, tbi)])

    def ple_views():
        wgt = av(0, 16 * K1, BF16, "p (k n) -> p k n", k=8)
        win = av(16 * K1, 4 * K1, BF16, "p (k n) -> p k n", k=2)
        pT = av(58 * K1, 2 * NT * 2, BF16, "p (k t) -> p k t", k=2)
        return wgt, win, pT

    def ple_weights(layer):
        wgt, win, pT = ple_views()
        fw.dma("gpsimd", wgt.ap, ple_w_gate[layer].rearrange("(k p) n -> p k n", p=128), writes=wgt.keys)
        fw.dma("gpsimd", win.ap, ple_w_in[layer].rearrange("(k p) n -> p k n", p=128), writes=win.keys)

    def ple_ptrans(layer):
        wgt, win, pT = ple_views()
        pin = [av(90 * K1, 1 * K1, F32), av(91 * K1, 1 * K1, F32)]
        for tt in range(NT // 128):
            pi_ = pin[tt % 2]
            src = pp[layer, tt * 128:(tt + 1) * 128, :] if tt < 16 else psm[layer]
            fw.dma("sync", pi_.ap, src, writes=pi_.keys)
            b = fw.bank()
            for k in range(2):
                op("tensor", lambda t, k=k, b=b, pi_=pi_: t.transpose(psb[b][:, k * 128:(k + 1) * 128],
                                                                     pi_.ap[:, k * 128:(k + 1) * 128], ident),
                   reads=pi_.keys + ["ident"], writes=[pk(b)])
            evac_copy(pT.ap[:, :, tt * 128:(tt + 1) * 128], psb[b][:, 0:256].rearrange("p (k t) -> p k t", k=2),
                      reads=[pk(b)], writes=pT.keys)
            yield

    def ple_norm(layer, tbi):
        norm_to_hb(tbi, "g_ple%d" % layer)

    def ple_body(layer, tb_gen=None):
        wgt, win, pT = ple_views()
        pending = None
        for tbi, (t0, n) in enumerate(TBS):
            hr = [hk(k, tbi) for k in range(NCH)]
            for c in range(NCH):
                mm_gate = [(wgt.ap[:, k, c * 128:(c + 1) * 128], hb[:, k, HPAD + t0:HPAD + t0 + n], wgt.keys + hr)
                           for k in range(NCH)]
                mm_val = [(win.ap[:, k, c * 128:(c + 1) * 128], pT.ap[:, k, t0:t0 + n], win.keys + pT.keys)
                          for k in range(2)]
                gated_block(tbi, c, mm_val, mm_gate)
                if pending is not None and c in (0, 2, 4, 6):
                    next(pending, None)
            if pending is not None:
                for _ in pending:
                    pass
            if tb_gen is not None:
                pending = tb_gen(tbi)
        if pending is not None:
            for _ in pending:
                pass

    def pool_mixer(pre_hook=None, post_hook=None):
        wpool = av(24 * K1, 4 * K1, BF16, "p (g k n) -> p g k n", g=4, k=2)
        fw.dma("gpsimd", wpool.ap, pool_w.rearrange("g (k p) n -> p g k n", p=128), writes=wpool.keys)
        hs = av(28 * K1, 8 * NSEQ * 23 * 2, BF16, "p (c s t) -> p c s t", c=8, s=NSEQ)
        inv = av(34 * K1, 4 * 16 * 4, F32, "p (g t) -> p g t", g=4)
        wpos = av(35 * K1, 4 * K1, BF16, "p (g k n) -> p g k n", g=4, k=2)
        sp = [av(42 * K1, 4 * K1, F32), av(46 * K1, 4 * K1, F32)]
        h32t = av(50 * K1, 8 * 15 * 4, F32, "p (c t) -> p c t", c=8)
        h32s = av(51 * K1, 8 * 128 * 4, F32, "p (c t) -> p c t", c=8)
        ot = av(42 * K1, 4 * K1, F32)
        ot2 = av(46 * K1, 4 * K1, F32)
        p2s = [av(55 * K1, 2 * K1, F32), av(57 * K1, 2 * K1, F32)]
        t2b = [av(59 * K1, 2 * K1, F32), av(61 * K1, 2 * K1, F32)]
        for gi, w in enumerate(POOL_W):
            op("vector", lambda g, gi=gi, w=w: g.tensor_scalar(out=wpos.ap[:, gi], in0=wpool.ap[:, gi], scalar1=1.0 / w, scalar2=None,
                                                               op0=ALU.mult), reads=wpool.keys, writes=wpos.keys)
        op("vector", lambda g: g.tensor_scalar(out=wpool.ap, in0=wpool.ap, scalar1=-1.0, scalar2=None, op0=ALU.mult),
           reads=wpool.keys + wpos.keys, writes=wpool.keys)
        for gi, w in enumerate(POOL_W):
            op("vector", lambda g, gi=gi: g.memset(inv.ap[:, gi, :], 1.0), writes=inv.keys)
            for t in range(w - 1):
                op("vector", lambda g, gi=gi, t=t, w=w: g.memset(inv.ap[:, gi, t:t + 1], float(w) / (t + 1)), writes=inv.keys)
        for hf in range(2):
            fw.dma("sync", sp[hf].ap[0:120, :], spool[hf * 120:(hf + 1) * 120, :], writes=sp[hf].keys)
            for half in range(2):
                b = fw.bank()
                for cc in range(4):
                    c = half * 4 + cc
                    op("tensor", lambda t, b=b, cc=cc, c=c, hf=hf: t.transpose(
                        psb[b][:, cc * 120:(cc + 1) * 120], sp[hf].ap[0:120, c * 128:(c + 1) * 128], ident[0:120, 0:120]),
                       reads=sp[hf].keys + ["ident"], writes=[pk(b)])
                for cc in range(4):
                    c = half * 4 + cc
                    evac_copy(hs.ap[:, c, hf * 8:(hf + 1) * 8, 0:15],
                              psb[b][:, cc * 120:(cc + 1) * 120].rearrange("p (s t) -> p s t", s=8),
                              reads=[pk(b)], writes=hs.keys)
        fw.dma("sync", pool_s.rearrange("(s r) d -> s r d", r=15)[:, 0:7, :],
               spool.rearrange("(s r) d -> s r d", r=15)[:, 8:15, :])
        def pool_norm_tb(tbi):
            t0, n = TBS[tbi]
            box = []
            if tbi < 4:
                g_ = norm_gen(tbi, "g_mix0", out=box)
            else:
                g_ = norm_gen(tbi, "g_mix0", dst_fn=lambda c, rstd: (
                    hs.ap[:, c, :, 15:23], x[:, c, t0:t0 + n].rearrange("p (s t) -> p s t", s=NSEQ),
                    rstd.ap[:, 0:n].rearrange("p (s t) -> p s t", s=NSEQ), hs.keys), out=box)
            next(g_)
            yield
            for _ in g_:
                pass
            rstd = box[0]
            if tbi == 3:
                for c in range(NCH):
                    op("vector", lambda v, c=c, rstd=rstd: v.scalar_tensor_tensor(
                        out=h32t.ap[:, c, :], in0=x[:, c, TP - 15:TP], scalar=gs("g_mix0", c), in1=rstd.ap[:, 512 - 15:512],
                        op0=ALU.mult, op1=ALU.mult), reads=[xk(c, 3), "gvec"] + rstd.keys, writes=h32t.keys)
                for half in range(2):
                    b = fw.bank()
                    for cc in range(4):
                        c = half * 4 + cc
                        op("tensor", lambda t, b=b, cc=cc, c=c: t.transpose(psb[b][0:15, cc * 128:(cc + 1) * 128],
                                                                         h32t.ap[:, c, :], ident),
                           reads=h32t.keys + ["ident"], writes=[pk(b)])
                    evac_copy(ot.ap[0:15, half * 512:(half + 1) * 512], psb[b][0:15, :], reads=[pk(b)], writes=ot.keys)
                fw.dma("sync", pool_p, ot.ap[0:15, :], reads=ot.keys)
            if tbi == 4:
                for c in range(NCH):
                    op("vector", lambda v, c=c, rstd=rstd: v.scalar_tensor_tensor(
                        out=h32s.ap[:, c, :], in0=x[:, c, TP:NT], scalar=gs("g_mix0", c), in1=rstd.ap[:, 0:128],
                        op0=ALU.mult, op1=ALU.mult), reads=[xk(c, 4), "gvec"] + rstd.keys, writes=h32s.keys)
                for half in range(2):
                    b = fw.bank()
                    for cc in range(4):
                        c = half * 4 + cc
                        op("tensor", lambda t, b=b, cc=cc, c=c: t.transpose(psb[b][:, cc * 128:(cc + 1) * 128],
                                                                         h32s.ap[:, c, :], ident),
                           reads=h32s.keys + ["ident"], writes=[pk(b)])
                    evac_copy(ot2.ap[:, half * 512:(half + 1) * 512], psb[b], reads=[pk(b)], writes=ot2.keys)
                for s in range(NSEQ):
                    fw.dma("sync", pool_s[s * 15 + 7:s * 15 + 15, :], ot2.ap[s * 8:(s + 1) * 8, :], reads=ot2.keys)
        pctr_ = [0]

        def pool_mm_tb(tbi, pending=None):
            t0, n = TBS[tbi]
            pctr = pctr_[0]
            for gi, w in enumerate(POOL_W):
                for oc in range(2):
                    c = 2 * gi + oc
                    if pending is not None and c == 3:
                        for _ in pending:
                            pass
                    if tbi < 4:
                        rk = [hk(2 * gi + k, tbi) for k in range(2)] + ([hk(2 * gi + k, tbi - 1) for k in range(2)] if tbi else ["hpad"])
                        rhs = lambda k, d: hb[:, 2 * gi + k, HPAD + t0 - d:HPAD + t0 - d + n]
                    else:
                        rk = hs.keys
                        rhs = lambda k, d: hs.ap[:, 2 * gi + k, :, 15 - d:23 - d]
                    p1 = fw.bank()
                    po1 = psb[p1][:, 0:n] if tbi < 4 else psb[p1][:, 0:n].rearrange("p (s t) -> p s t", s=NSEQ)
                    fused = tbi > 0
                    nmm = 2 * w + (2 if fused else 0)
                    i = 0
                    for d in range(w):
                        for k in range(2):
                            op("tensor", lambda t, k=k, d=d, i=i, po1=po1, rhs=rhs: t.matmul(
                                po1, lhsT=wpos.ap[:, gi, k, oc * 128:(oc + 1) * 128], rhs=rhs(k, d),
                                start=(i == 0), stop=(i == nmm - 1)), reads=wpos.keys + rk, writes=[pk(p1)])
                            i += 1
                    pb_ = pctr % 2
                    pctr += 1
                    t2 = t2b[pb_]
                    if fused:
                        for k in range(2):
                            op("tensor", lambda t, k=k, i=i, po1=po1, rhs=rhs: t.matmul(
                                po1, lhsT=wpool.ap[:, gi, k, oc * 128:(oc + 1) * 128], rhs=rhs(k, 0),
                                start=False, stop=(i == nmm - 1)), reads=wpool.keys + rk, writes=[pk(p1)])
                            i += 1
                        op("vector", lambda v, p1=p1, c=c: v.scalar_tensor_tensor(
                            out=x[:, c, t0:t0 + n], in0=psb[p1][:, 0:n], scalar=gs("pool_scale", c), in1=x[:, c, t0:t0 + n],
                            op0=ALU.mult, op1=ALU.add), reads=[pk(p1), xk(c, tbi), "gvec"], writes=[xk(c, tbi)])
                        continue
                    p2 = fw.bank()
                    po2 = psb[p2][:, 0:n]
                    for k in range(2):
                        op("tensor", lambda t, k=k, po2=po2, rhs=rhs: t.matmul(
                            po2, lhsT=wpool.ap[:, gi, k, oc * 128:(oc + 1) * 128], rhs=rhs(k, 0),
                            start=(k == 0), stop=(k == 1)), reads=wpool.keys + rk, writes=[pk(p2)])
                    s2 = p2s[pb_]
                    tf = av(67 * K1, 64, F32)
                    op("scalar", lambda s, p2=p2, s2=s2: s.copy(out=s2.ap[:, 0:n], in_=psb[p2][:, 0:n]),
                       reads=[pk(p2)], writes=s2.keys)
                    op("vector", lambda v, p1=p1, t2=t2, s2=s2: v.tensor_tensor(
                        out=t2.ap[:, 0:n], in0=psb[p1][:, 0:n], in1=s2.ap[:, 0:n], op=ALU.add),
                       reads=[pk(p1)] + s2.keys, writes=t2.keys)
                    op("vector", lambda v, p1=p1, tf=tf: v.tensor_tensor(
                        out=tf.ap[:, 0:16], in0=psb[p1][:, 0:16], in1=inv.ap[:, gi, :], op=ALU.mult),
                       reads=[pk(p1)] + inv.keys, writes=tf.keys)
                    op("vector", lambda v, t2=t2, s2=s2, tf=tf: v.tensor_tensor(
                        out=t2.ap[:, 0:16], in0=tf.ap[:, 0:16], in1=s2.ap[:, 0:16], op=ALU.add),
                       reads=tf.keys + s2.keys + t2.keys, writes=t2.keys)
                    op("vector", lambda v, t2=t2, c=c: v.scalar_tensor_tensor(
                        out=x[:, c, t0:t0 + n], in0=t2.ap[:, 0:n], scalar=gs("pool_scale", c), in1=x[:, c, t0:t0 + n],
                        op0=ALU.mult, op1=ALU.add), reads=t2.keys + [xk(c, tbi), "gvec"], writes=[xk(c, tbi)])
            pctr_[0] = pctr

        if pre_hook is not None:
            pre_hook(0)
        for _ in pool_norm_tb(0):
            pass
        for tbi in range(5):
            pending = None
            if tbi + 1 < 5:
                if pre_hook is not None:
                    pre_hook(tbi + 1)
                pending = pool_norm_tb(tbi + 1)
                next(pending)
            pool_mm_tb(tbi, pending)
            if pending is not None:
                for _ in pending:
                    pass
            if post_hook is not None:
                post_hook(tbi)

    def ssm_mixer():
        TWO_S = 1.5957691216057308
        MUL, ADD, SUB = ALU.mult, ALU.add, ALU.subtract
        G = av(0, 36 * K1, F32, "p (n c) -> p n c", n=NBX)
        Bw = av(36 * K1, 8 * K1, BF16, "p (r e m) -> p r e m", r=2, e=16)
        Cw = av(36 * K1, 10 * K1, BF16, "p (r e b m) -> p r e b m", r=2, e=16, b=5)
        CB = [0, 1, 2, 4]
        ZB = av(46 * K1, 256, BF16, "p (r m) -> p r m", r=2)
        Lw = av(47 * K1, 4 * K1, BF16, "p (t m) -> p t m", t=16)
        Sc = av(51 * K1, 2 * 4 * NBX * 2, BF16, "p (r j n) -> p r j n", r=2, j=4)
        Pr = av(54 * K1, 2176, F32, "p (j e) -> p j e", j=32)
        Pi = av(54 * K1 + 2176, 2176, F32, "p (j e) -> p j e", j=32)
        Bbr = av(59 * K1, 2 * K1, F32, "p (j h) -> p j h", j=32)
        Bbi = av(61 * K1, 2 * K1, F32, "p (j h) -> p j h", j=32)
        Cr = av(63 * K1, 2 * K1, F32, "p (j h) -> p j h", j=32)
        Ci = av(65 * K1, 2 * K1, F32, "p (j h) -> p j h", j=32)
        Bbm = av(67 * K1, 4 * K1, BF16, "p (j r m) -> p j r m", j=32, r=2)
        C0m = av(71 * K1, 4 * K1, BF16, "p (j r m) -> p j r m", j=32, r=2)
        XBm = av(75 * K1, 8 * K1, F32, "p (e j m) -> p e j m", e=16, j=4)
        tAf = av(83 * K1, 4 * K1, F32)
        tBf = av(87 * K1, 4 * K1, F32)
        H0 = av(93 * K1, 4 * K1, F32, "p (s c) -> p s c", s=NSEQ)

        def hbu(c):
            return hb[:, c + 1, HPAD:HW].rearrange("p (r n) -> p r n", r=LCH)

        def hbz(c):
            return hb[:, c, HPAD:HW].rearrange("p (r n) -> p r n", r=LCH)

        def u_keys(c):
            return [hk(c + 1, t_) for t_ in range(5)]

        def z_keys(c):
            return [hk(c, t_) for t_ in range(5)]

        def sm(i):
            return av(91 * K1 + i * 128, 128, F32)

        def vop(out, in0, in1, o, reads, writes, eng="vector"):
            op(eng, lambda v: v.tensor_tensor(out=out, in0=in0, in1=in1, op=o), reads=reads, writes=writes)

        def ssm_norm():
            for c in range(NCH):
                op("gpsimd", lambda g, c=c: g.memset(hbu(c)[:, 0:8, NB:NBX], 0.0), writes=u_keys(c))
            for tbi, (t0, n) in enumerate(TBS):
                if tbi < 4:
                    n0 = t0 // LCH
                    norm_to_hb(tbi, "g_mix1", dst_fn=lambda c, rstd: (
                        hbu(c)[:, :, n0:n0 + 32], x[:, c, t0:t0 + n].rearrange("p (n r) -> p r n", r=LCH),
                        rstd.ap[:, 0:n].rearrange("p (n r) -> p r n", r=LCH), u_keys(c)))
                else:
                    norm_to_hb(tbi, "g_mix1", dst_fn=lambda c, rstd: (
                        hbu(c)[:, 8:16, NB:NBX], x[:, c, t0:t0 + n].rearrange("p (s t) -> p t s", t=TSEQ),
                        rstd.ap[:, 0:n].rearrange("p (s t) -> p t s", t=TSEQ), u_keys(c)))

        lrow = [av(75 * K1 + i * 512, 512, F32) for i in range(2)]
        ldt = av(76 * K1, 64, F32)
        ldx = av(76 * K1 + 512, 512, F32)
        fw.dma("sync", lrow[0].ap[0:32, :], lam_re, writes=lrow[0].keys)
        fw.dma("sync", lrow[1].ap[0:32, :], lam_im, writes=lrow[1].keys)
        fw.dma("sync", ldt.ap[0:32, 0:2], log_dt, writes=ldt.keys)
        inj_t = [[av(83 * K1 + (ri * 4 + jb) * 512, 512, F32) for jb in range(4)] for ri in range(2)]
        for ri, csrc in enumerate([c_re, c_im]):
            for jb in range(4):
                for g_ in range(2):
                    fw.dma("sync", inj_t[ri][jb].ap[:, g_ * 64:(g_ + 1) * 64], csrc[jb * 8:(jb + 1) * 8, g_], writes=inj_t[ri][jb].keys)
        braw = [av(87 * K1, 2 * K1, F32, "p (j h) -> p j h", j=32), av(89 * K1, 2 * K1, F32, "p (j h) -> p j h", j=32)]
        fw.dma("sync", braw[0].ap, b_re.rearrange("j q h -> q j h"), writes=braw[0].keys)
        fw.dma("sync", braw[1].ap, b_im.rearrange("j q h -> q j h"), writes=braw[1].keys)
        hl_t = [[av(67 * K1 + (ri * 4 + sb) * 512, 512, F32) for sb in range(4)] for ri in range(2)]
        for ri, hsrc in enumerate([sre, sim]):
            for sb in range(4):
                fw.dma("sync", hl_t[ri][sb].ap, hsrc[sb * 128:(sb + 1) * 128, :], writes=hl_t[ri][sb].keys)
        yield
        op("vector", lambda v: v.tensor_copy(out=ldx.ap[0:32, :].rearrange("j (g p) -> j g p", g=2),
                                             in_=ldt.ap[0:32, 0:2].unsqueeze(2).broadcast_to([32, 2, 64])),
           reads=ldt.keys, writes=ldx.keys)
        b = fw.bank()
        for i, s_ in enumerate([lrow[0], lrow[1], ldx]):
            op("tensor", lambda t, i=i, s_=s_: t.transpose(psb[b][:, i * 32:(i + 1) * 32], s_.ap[0:32, :], ident[0:32, 0:32]),
               reads=s_.keys + ["ident"], writes=[pk(b)])
        lr, li, dt = sm(0), sm(1), sm(2)
        op("vector", lambda v: v.tensor_copy(out=lr.ap, in_=psb[b][:, 0:32]), reads=[pk(b)], writes=lr.keys)
        op("vector", lambda v: v.tensor_copy(out=li.ap, in_=psb[b][:, 32:64]), reads=[pk(b)], writes=li.keys)
        op("scalar", lambda s: s.activation(out=dt.ap, in_=psb[b][:, 64:96], func=AF.Exp), reads=[pk(b)], writes=dt.keys)
        ar, ai, mg, cs, sn, zr, zi = sm(3), sm(4), sm(5), sm(6), sm(7), sm(8), sm(9)
        t1, t2, t3, t4, t5 = sm(10), sm(11), sm(12), sm(13), sm(14)
        vop(ar.ap, lr.ap, dt.ap, MUL, lr.keys, ar.keys)
        vop(ai.ap, li.ap, dt.ap, MUL, li.keys, ai.keys)
        op("scalar", lambda s: s.activation(out=mg.ap, in_=ar.ap, func=AF.Exp, scale=1.0 / 16), reads=ar.keys, writes=mg.keys)
        op("scalar", lambda s: s.activation(out=cs.ap, in_=ai.ap, func=AF.Sin, scale=1.0 / 32, bias=0.0),
           reads=ai.keys, writes=cs.keys)
        op("vector", lambda v: v.tensor_tensor(out=cs.ap, in0=cs.ap, in1=cs.ap, op=MUL), reads=cs.keys, writes=cs.keys)
        op("vector", lambda v: v.tensor_scalar(out=cs.ap, in0=cs.ap, scalar1=-2.0, scalar2=1.0, op0=MUL, op1=ADD),
           reads=cs.keys, writes=cs.keys)
        op("scalar", lambda s: s.activation(out=sn.ap, in_=ai.ap, func=AF.Sin, scale=1.0 / 16, bias=0.0),
           reads=ai.keys, writes=sn.keys)
        vop(zr.ap, mg.ap, cs.ap, MUL, mg.keys, zr.keys)
        vop(zi.ap, mg.ap, sn.ap, MUL, mg.keys, zi.keys)
        sk = sm(0).keys + sm(15).keys
        for _ in range(4):
            vop(t1.ap, zr.ap, zr.ap, MUL, sk, sk)
            vop(t2.ap, zi.ap, zi.ap, MUL, sk, sk)
            vop(t3.ap, zr.ap, zi.ap, MUL, sk, sk)
            vop(zr.ap, t1.ap, t2.ap, SUB, sk, sk)
            vop(zi.ap, t3.ap, t3.ap, ADD, sk, sk)
        lbr, lbi = zr, zi
        vop(t1.ap, lr.ap, lr.ap, MUL, sk, sk)
        vop(t2.ap, li.ap, li.ap, MUL, sk, sk)
        vop(t1.ap, t1.ap, t2.ap, ADD, sk, sk)
        op("vector", lambda v: v.reciprocal(out=t1.ap, in_=t1.ap), reads=sk, writes=sk)
        op("vector", lambda v: v.tensor_scalar(out=t2.ap, in0=lbr.ap, scalar1=-1.0, scalar2=None, op0=ADD), reads=sk, writes=sk)
        vop(t3.ap, t2.ap, lr.ap, MUL, sk, sk)
        vop(t4.ap, lbi.ap, li.ap, MUL, sk, sk)
        vop(t3.ap, t3.ap, t4.ap, ADD, sk, sk)
        vop(t3.ap, t3.ap, t1.ap, MUL, sk, sk)
        vop(t4.ap, lbi.ap, lr.ap, MUL, sk, sk)
        vop(t5.ap, t2.ap, li.ap, MUL, sk, sk)
        vop(t4.ap, t4.ap, t5.ap, SUB, sk, sk)
        vop(t4.ap, t4.ap, t1.ap, MUL, sk, sk)
        fre, fim = t3, t4
        pk_ = Pr.keys + Pi.keys
        op("vector", lambda v: v.memset(Pr.ap[:, :, 0:1], 1.0), writes=pk_)
        op("vector", lambda v: v.memset(Pi.ap[:, :, 0:1], 0.0), writes=pk_)
        op("vector", lambda v: v.tensor_copy(out=Pr.ap[:, :, 1:2], in_=lbr.ap.unsqueeze(2)), reads=sk, writes=pk_)
        op("vector", lambda v: v.tensor_copy(out=Pi.ap[:, :, 1:2], in_=lbi.ap.unsqueeze(2)), reads=sk, writes=pk_)
        tk = tAf.keys + tBf.keys
        pta, ptb = av(79 * K1, 1 * K1, F32), av(80 * K1, 1 * K1, F32)
        ptk = pta.keys + ptb.keys
        m = 1
        while m < 16:
            ta = pta.ap[:, 0:32 * m].rearrange("p (j e) -> p j e", j=32)
            tb_ = ptb.ap[:, 0:32 * m].rearrange("p (j e) -> p j e", j=32)
            prs, pis = Pr.ap[:, :, 1:m + 1], Pi.ap[:, :, 1:m + 1]
            prm = Pr.ap[:, :, m:m + 1].broadcast_to([128, 32, m])
            pim = Pi.ap[:, :, m:m + 1].broadcast_to([128, 32, m])
            vop(ta, prs, prm, MUL, pk_, ptk)
            vop(tb_, pis, pim, MUL, pk_, ptk)
            vop(Pr.ap[:, :, m + 1:2 * m + 1], ta, tb_, SUB, ptk + pk_, pk_)
            vop(ta, prs, pim, MUL, pk_, ptk)
            vop(tb_, pis, prm, MUL, pk_, ptk)
            vop(Pi.ap[:, :, m + 1:2 * m + 1], ta, tb_, ADD, ptk + pk_, pk_)
            m *= 2
        for ri, (csrc, cdst) in enumerate([(c_re, Cr), (c_im, Ci)]):
            b = fw.bank()
            for jb in range(4):
                inj = inj_t[ri][jb]
                op("tensor", lambda t, jb=jb, inj=inj, b=b: t.transpose(psb[b][:, jb * 128:(jb + 1) * 128], inj.ap, ident),
                   reads=inj.keys + ["ident"], writes=[pk(b)])
            op("vector", lambda v, b=b, cdst=cdst, ri=ri: v.tensor_scalar(
                out=cdst.ap, in0=psb[b].rearrange("p (j h) -> p j h", j=32), scalar1=(1.0 if ri == 0 else -1.0), scalar2=None,
                op0=MUL), reads=[pk(b)], writes=cdst.keys)
        ua = av(77 * K1, 2 * K1, F32, "p (j h) -> p j h", j=32)
        ub = av(79 * K1, 2 * K1, F32, "p (j h) -> p j h", j=32)
        freb = fre.ap.unsqueeze(2).broadcast_to([128, 32, 16])
        fimb = fim.ap.unsqueeze(2).broadcast_to([128, 32, 16])
        bk = braw[0].keys + braw[1].keys + sk
        uk = ua.keys + ub.keys
        vop(ua.ap, braw[0].ap, freb, MUL, bk, uk)
        vop(ub.ap, braw[1].ap, fimb, MUL, bk, uk)
        vop(Bbr.ap, ua.ap, ub.ap, SUB, uk, Bbr.keys)
        vop(ua.ap, braw[1].ap, freb, MUL, bk, uk)
        vop(ub.ap, braw[0].ap, fimb, MUL, bk, uk)
        vop(Bbi.ap, ua.ap, ub.ap, ADD, uk, Bbi.keys)
        def build_masked():
            op("gpsimd", lambda g: g.memset(Bbm.ap, 0.0), writes=Bbm.keys)
            op("gpsimd", lambda g: g.memset(C0m.ap, 0.0), writes=C0m.keys)
            for hf in range(2):
                ps_, cs_ = slice(hf * 64, (hf + 1) * 64), slice(hf * 16, (hf + 1) * 16)
                op("vector", lambda v, ps_=ps_, cs_=cs_: v.tensor_copy(out=Bbm.ap[ps_, :, 0, cs_], in_=Bbr.ap[ps_]),
                   reads=Bbr.keys, writes=Bbm.keys)
                op("vector", lambda v, ps_=ps_, cs_=cs_: v.tensor_copy(out=Bbm.ap[ps_, :, 1, cs_], in_=Bbi.ap[ps_]),
                   reads=Bbi.keys, writes=Bbm.keys)
                op("vector", lambda v, ps_=ps_, cs_=cs_: v.tensor_copy(out=C0m.ap[ps_, :, 0, cs_], in_=Cr.ap[ps_]),
                   reads=Cr.keys, writes=C0m.keys)
                op("vector", lambda v, ps_=ps_, cs_=cs_: v.tensor_copy(out=C0m.ap[ps_, :, 1, cs_], in_=Ci.ap[ps_]),
                   reads=Ci.keys, writes=C0m.keys)
        for ri, hsrc in enumerate([sre, sim]):
            b = fw.bank()
            for sb in range(4):
                hl = hl_t[ri][sb]
                op("tensor", lambda t, sb=sb, hl=hl, b=b: t.transpose(psb[b][:, sb * 128:(sb + 1) * 128], hl.ap, ident),
                   reads=hl.keys + ["ident"], writes=[pk(b)])
            op("vector", lambda v, b=b, ri=ri: v.tensor_copy(out=H0.ap[:, :, ri * 32:(ri + 1) * 32],
                                                             in_=psb[b].rearrange("p (s j) -> p s j", s=NSEQ)),
               reads=[pk(b)], writes=H0.keys)
        ssm_norm()
        XBmb = [XBm, av(67 * K1, 8 * K1, F32, "p (e j m) -> p e j m", e=16, j=4)]
        Bwb = [Bw, av(44 * K1, 8 * K1, BF16, "p (r e m) -> p r e m", r=2, e=16)]
        for X_ in XBmb:
            op("gpsimd", lambda g, X_=X_: g.memset(X_.ap, 0.0), writes=X_.keys)
        tA4 = tAf.ap.rearrange("p (a b h) -> p a b h", a=16, b=4)
        tB4 = tBf.ap.rearrange("p (a b h) -> p a b h", a=16, b=4)

        def bgen(c, ri):
            j0 = 4 * c
            X_ = XBmb[ri]
            prb = Pr.ap[:, j0:j0 + 4, 0:16].rearrange("p j e -> p e j").unsqueeze(3).broadcast_to([128, 16, 4, 16])
            pib = Pi.ap[:, j0:j0 + 4, 0:16].rearrange("p j e -> p e j").unsqueeze(3).broadcast_to([128, 16, 4, 16])
            bbr = Bbr.ap[:, j0:j0 + 4, :].unsqueeze(1).broadcast_to([128, 16, 4, 16])
            bbi = Bbi.ap[:, j0:j0 + 4, :].unsqueeze(1).broadcast_to([128, 16, 4, 16])
            rk_ = pk_ + Bbr.keys + Bbi.keys
            if ri == 0:
                vop(tA4, prb, bbr, MUL, rk_, tAf.keys)
                vop(tB4, pib, bbi, MUL, rk_, tBf.keys, eng="gpsimd")
                o_ = SUB
            else:
                vop(tA4, prb, bbi, MUL, rk_, tAf.keys)
                vop(tB4, pib, bbr, MUL, rk_, tBf.keys, eng="gpsimd")
                o_ = ADD
            for hf in range(2):
                ps_, cs_ = slice(hf * 64, (hf + 1) * 64), slice(hf * 16, (hf + 1) * 16)
                vop(X_.ap[ps_, :, :, cs_], tA4[ps_], tB4[ps_], o_, tk, X_.keys)

        def bbanks(c):
            return [0, 1, 2, 3] if c % 2 == 0 else [4, 5, 6, 7]

        def btrans(c, ri):
            X_, W_ = XBmb[ri], Bwb[c % 2]
            for q4 in range(4):
                b = bbanks(c)[q4]
                for ee in range(4):
                    e = q4 * 4 + ee
                    op("tensor", lambda t, b=b, ee=ee, e=e: t.transpose(
                        psb[b][:, ee * 128:(ee + 1) * 128], X_.ap[:, e].rearrange("p j m -> p (j m)"), ident),
                       reads=X_.keys + ["ident"], writes=[pk(b)])
                evac_copy(W_.ap[:, ri, q4 * 4:(q4 + 1) * 4, :], psb[b].rearrange("p (e m) -> p e m", e=4),
                          reads=[pk(b)], writes=W_.keys, eng="scalar")

        def bside_mm(c, jjs):
            W_, uv, hr, bk4 = Bwb[c % 2], hbu(c), u_keys(c), bbanks(c)
            for ri in range(2):
                for k in range(LCH):
                    for jj in jjs:
                        rs = slice(32 * jj, 32 * jj + 32) if jj < 3 else slice(64, 128)
                        pb = bk4[jj]
                        op("tensor", lambda t, k=k, pb=pb, ri=ri, rs=rs: t.matmul(
                            psb[pb][:, ri * NBX:(ri + 1) * NBX], lhsT=W_.ap[rs, ri, 15 - k, :],
                            rhs=uv[rs, k, :], start=(k == 0), stop=(k == LCH - 1)),
                           reads=W_.keys + hr, writes=[pk(pb)])

        for ri in range(2):
            bgen(0, ri)
            btrans(0, ri)
        for c in range(NCH):
            j0 = 4 * c
            if c + 1 < NCH:
                bgen(c + 1, 0)
                bgen(c + 1, 1)
            bside_mm(c, [0, 1, 2])
            op("vector", lambda v: v.memset(Bwb[c % 2].ap[64:96, 0], 0.0), reads=[], writes=Bwb[c % 2].keys)
            op("gpsimd", lambda g: g.memset(Bwb[c % 2].ap[64:96, 1], 0.0), reads=[], writes=Bwb[c % 2].keys)
            if c + 1 < NCH:
                btrans(c + 1, 0)
                btrans(c + 1, 1)
            bside_mm(c, [3])
            bk4 = bbanks(c)
            for jj in range(4):
                evac_copy(G.ap[:, :, j0 + jj:64:32], psb[bk4[jj]][:, 0:2 * NBX].rearrange("p (r n) -> p n r", r=2),
                          reads=[pk(bk4[jj])], writes=G.keys)

        Cwb = [Cw, av(18 * K1, 10 * K1, BF16, "p (r e b m) -> p r e b m", r=2, e=16, b=5)]
        Lwb = [Lw, av(59 * K1, 4 * K1, BF16, "p (t m) -> p t m", t=16)]
        Scb = [Sc, av(28 * K1, 2 * 4 * NBX * 2, BF16, "p (r j n) -> p r j n", r=2, j=4)]
        ZBb = [ZB, av(31 * K1, 256, BF16, "p (r m) -> p r m", r=2)]
        build_masked()
        for t_ in [Cwb[0], Lwb[0], Lwb[1], Scb[0], ZBb[0]]:
            op("gpsimd", lambda g, t_=t_: g.memset(t_.ap, 0.0), reads=Bbm.keys + C0m.keys, writes=t_.keys)
        fa = av(87 * K1, 2 * K1, F32, "p (s j) -> p s j", s=NSEQ)
        fb = av(89 * K1, 2 * K1, F32, "p (s j) -> p s j", s=NSEQ)
        fk = fa.keys + fb.keys
        H0r, H0i = H0.ap[:, :, 0:32], H0.ap[:, :, 32:64]
        q1, q2, q3 = sm(10), sm(11), sm(12)
        vop(q1.ap, Pr.ap[:, :, 8], Pr.ap[:, :, 8], MUL, pk_, sk)
        vop(q2.ap, Pi.ap[:, :, 8], Pi.ap[:, :, 8], MUL, pk_, sk)
        vop(q1.ap, q1.ap, q2.ap, ADD, sk, sk)
        op("vector", lambda v: v.reciprocal(out=q1.ap, in_=q1.ap), reads=sk, writes=sk)
        vop(q2.ap, Pr.ap[:, :, 8], q1.ap, MUL, pk_ + sk, sk)
        op("vector", lambda v: v.scalar_tensor_tensor(out=q3.ap, in0=Pi.ap[:, :, 8], scalar=-1.0, in1=q1.ap, op0=MUL, op1=MUL),
           reads=pk_ + sk, writes=sk)
        m8r = q2.ap.unsqueeze(1).broadcast_to([128, NSEQ, 32])
        m8i = q3.ap.unsqueeze(1).broadcast_to([128, NSEQ, 32])
        fc = av(85 * K1, 2 * K1, F32, "p (s j) -> p s j", s=NSEQ)
        vop(fa.ap, H0r, m8r, MUL, H0.keys + sk, fk)
        vop(fb.ap, H0i, m8i, MUL, H0.keys + sk, fk)
        vop(fc.ap, fa.ap, fb.ap, SUB, fk, fc.keys)
        vop(fa.ap, H0i, m8r, MUL, H0.keys + sk, fk)
        vop(fb.ap, H0r, m8i, MUL, H0.keys + sk, fk)
        vop(H0i, fa.ap, fb.ap, ADD, fk, H0.keys)
        op("vector", lambda v: v.tensor_copy(out=H0r, in_=fc.ap), reads=fc.keys, writes=H0.keys)

        A1 = av(91 * K1, 256, F32); A2 = av(91 * K1 + 256, 256, F32)
        A1q = av(91 * K1 + 512, 256, F32); A2q = av(91 * K1 + 768, 256, F32)
        s1 = av(92 * K1, 256, F32); s2 = av(92 * K1 + 256, 256, F32)
        QTr = av(75 * K1, 1 * K1, F32, "p (k j) -> p k j", k=8)
        QTi = av(76 * K1, 1 * K1, F32, "p (k j) -> p k j", k=8)
        qa = av(77 * K1, 512, F32); qb = av(77 * K1 + 512, 512, F32)
        qk = QTr.keys + QTi.keys + qa.keys
        ak = A1.keys + s1.keys
        op("vector", lambda v: v.tensor_copy(out=QTr.ap[:, 0, :], in_=Pr.ap[:, :, 16]), reads=pk_, writes=qk)
        op("vector", lambda v: v.tensor_copy(out=QTi.ap[:, 0, :], in_=Pi.ap[:, :, 16]), reads=pk_, writes=qk)
        m = 1
        while m < 8:
            ta = qa.ap[:, 0:32 * m].rearrange("p (k j) -> p k j", j=32)
            tb_ = qb.ap[:, 0:32 * m].rearrange("p (k j) -> p k j", j=32)
            qrs, qis = QTr.ap[:, 0:m, :], QTi.ap[:, 0:m, :]
            qrm = QTr.ap[:, m - 1:m, :].broadcast_to([128, m, 32])
            qim = QTi.ap[:, m - 1:m, :].broadcast_to([128, m, 32])
            vop(ta, qrs, qrm, MUL, qk, qk)
            vop(tb_, qis, qim, MUL, qk, qk)
            vop(QTr.ap[:, m:2 * m, :], ta, tb_, SUB, qk, qk)
            vop(ta, qrs, qim, MUL, qk, qk)
            vop(tb_, qis, qrm, MUL, qk, qk)
            vop(QTi.ap[:, m:2 * m, :], ta, tb_, ADD, qk, qk)
            m *= 2
        for (A1_, A2_, kq) in ((A1, A2, 0), (A1q, A2q, 7)):
            for hf in range(2):
                op("vector", lambda v, hf=hf, A1_=A1_, kq=kq: v.tensor_copy(out=A1_.ap[:, hf * 32:(hf + 1) * 32], in_=QTr.ap[:, kq, :]),
                   reads=qk, writes=ak)
            op("vector", lambda v, A2_=A2_, kq=kq: v.tensor_scalar(out=A2_.ap[:, 0:32], in0=QTi.ap[:, kq, :], scalar1=-1.0, scalar2=None,
                                                                  op0=MUL), reads=qk, writes=ak)
            op("vector", lambda v, A2_=A2_, kq=kq: v.tensor_copy(out=A2_.ap[:, 32:64], in_=QTi.ap[:, kq, :]), reads=qk, writes=ak)
        NBLK, BL = 16, 8
        Gv = G.ap[:, 0:NB, :].rearrange("p (a b) c -> p a b c", b=BL)
        S1 = tAf.ap.rearrange("p (a c) -> p a c", a=NBLK)
        S2 = tBf.ap.rearrange("p (a c) -> p a c", a=NBLK)
        A1b = A1.ap.unsqueeze(1).broadcast_to([128, NBLK, 64])
        A2lo = A2.ap[:, 0:32].unsqueeze(1).broadcast_to([128, NBLK, 32])
        A2hi = A2.ap[:, 32:64].unsqueeze(1).broadcast_to([128, NBLK, 32])
        sk2 = tAf.keys + tBf.keys
        for b_ in range(1, BL):
            rd = G.keys + ak if b_ == 1 else ["gscan"] + ak
            src_, dst_ = Gv[:, :, b_ - 1, :], Gv[:, :, b_, :]
            vop(S1, A1b, src_, MUL, rd, sk2)
            vop(S2[:, :, 0:32], A2lo, src_[:, :, 32:64], MUL, rd, sk2)
            vop(S2[:, :, 32:64], A2hi, src_[:, :, 0:32], MUL, rd, sk2)
            vop(dst_, dst_, S1, ADD, sk2 + ["gscan"], ["gscan"])
            vop(dst_, dst_, S2, ADD, sk2 + ["gscan"], ["gscan"])
        for a_ in range(1, NBLK):
            prev, cur = Gv[:, a_ - 1, BL - 1, :], Gv[:, a_, BL - 1, :]
            vop(s1.ap, A1q.ap, prev, MUL, ["gscan"] + ak, ak)
            vop(s2.ap[:, 0:32], A2q.ap[:, 0:32], prev[:, 32:64], MUL, ["gscan"] + ak, ak)
            vop(s2.ap[:, 32:64], A2q.ap[:, 32:64], prev[:, 0:32], MUL, ["gscan"] + ak, ak)
            vop(cur, cur, s1.ap, ADD, ak + ["gscan"], ["gscan"])
            vop(cur, cur, s2.ap, ADD, ak + ["gscan"], ["gscan"])
        for a0 in range(1, NBLK, 4):
            na = min(4, NBLK - a0)
            shp = [128, na, BL - 1, 32]
            cre = Gv[:, a0 - 1:a0 - 1 + na, BL - 1, 0:32].unsqueeze(2).broadcast_to(shp)
            cim = Gv[:, a0 - 1:a0 - 1 + na, BL - 1, 32:64].unsqueeze(2).broadcast_to(shp)
            qr_ = QTr.ap[:, 0:BL - 1, :].unsqueeze(1).broadcast_to(shp)
            qi_ = QTi.ap[:, 0:BL - 1, :].unsqueeze(1).broadcast_to(shp)
            dre = Gv[:, a0:a0 + na, 0:BL - 1, 0:32]
            dim_ = Gv[:, a0:a0 + na, 0:BL - 1, 32:64]
            tre = tAf.ap[:, 0:na * 7 * 32].rearrange("p (a b j) -> p a b j", a=na, b=BL - 1)
            tim = tBf.ap[:, 0:na * 7 * 32].rearrange("p (a b j) -> p a b j", a=na, b=BL - 1)
            last = (a0 + 4 >= NBLK)
            wre = (G.keys if last else []) + ["gscan_re"]
            wim = (G.keys if last else []) + ["gscan_im"]
            vop(tre, qr_, cre, MUL, ["gscan"] + qk, tAf.keys)
            vop(dre, dre, tre, ADD, tAf.keys + ["gscan", "gscan_re"], ["gscan_re"])
            vop(tre, qi_, cim, MUL, ["gscan"] + qk, tAf.keys)
            vop(dre, dre, tre, SUB, tAf.keys + ["gscan", "gscan_re"], wre)
            ie = "gpsimd" if a0 < 9 else "vector"
            vop(tim, qr_, cim, MUL, ["gscan"] + qk, tBf.keys, eng=ie)
            vop(dim_, dim_, tim, ADD, tBf.keys + ["gscan", "gscan_im"], ["gscan_im"], eng=ie)
            vop(tim, qi_, cre, MUL, ["gscan"] + qk, tBf.keys, eng=ie)
            vop(dim_, dim_, tim, ADD, tBf.keys + ["gscan", "gscan_im"], wim, eng=ie)
        Sall = av(0, 18 * K1, BF16, "p (n c) -> p n c", n=NBX)
        stmp = av(83 * K1, 9 * 64 * 2, BF16, "p (n c) -> p n c", n=9)
        op("vector", lambda v: v.tensor_copy(out=stmp.ap, in_=G.ap[:, 0:9, :]), reads=G.keys, writes=stmp.keys)
        op("vector", lambda v: v.tensor_copy(out=Sall.ap[:, 0:9, :], in_=stmp.ap), reads=stmp.keys, writes=G.keys)
        for (a_, b_) in ((9, 18), (18, 36), (36, 72), (72, 144)):
            op("vector", lambda v, a_=a_, b_=b_: v.tensor_copy(out=Sall.ap[:, a_:b_, :], in_=G.ap[:, a_:b_, :]),
               reads=G.keys, writes=G.keys)
        for t_ in [Cwb[1], Scb[1], ZBb[1]]:
            op("gpsimd", lambda g, t_=t_: g.memset(t_.ap, 0.0), writes=t_.keys)
        GT = [[av(75 * K1, 2 * K1, F32), av(77 * K1, 2 * K1, F32)], [av(79 * K1, 2 * K1, F32), av(81 * K1, 2 * K1, F32)]]
        gctr2 = [0]

        def gelu_bank(pbank, width, dst, in_view, wkeys):
            g1, g2 = GT[gctr2[0] % 2]
            gctr2[0] += 1
            ps_ = psb[pbank][:, 0:width]
            op("scalar", lambda s: s.activation(out=g1.ap[:, 0:width], in_=ps_, func=AF.Square), reads=[pk(pbank)], writes=g1.keys)
            op("vector", lambda v: v.tensor_scalar(out=g1.ap[:, 0:width], in0=g1.ap[:, 0:width], scalar1=0.044715, scalar2=1.0,
                                                   op0=MUL, op1=ADD), reads=g1.keys, writes=g1.keys)
            op("vector", lambda v: v.tensor_tensor(out=g2.ap[:, 0:width], in0=ps_, in1=g1.ap[:, 0:width], op=MUL),
               reads=[pk(pbank)] + g1.keys, writes=g2.keys)
            op("scalar", lambda s: s.activation(out=g2.ap[:, 0:width], in_=g2.ap[:, 0:width], func=AF.Sigmoid, scale=TWO_S),
               reads=g2.keys, writes=g2.keys)
            op("vector", lambda v: v.tensor_tensor(out=dst, in0=in_view(ps_), in1=in_view(g2.ap[:, 0:width]), op=MUL),
               reads=[pk(pbank)] + g2.keys, writes=wkeys)

        tA5 = tAf.ap.rearrange("p (a b h) -> p a b h", a=4, b=16)
        tB5 = tBf.ap.rearrange("p (a b h) -> p a b h", a=4, b=16)

        def cgen(c):
            j0, Cw_ = 4 * c, Cwb[c % 2]
            prb = Pr.ap[:, j0:j0 + 4, 1:17].unsqueeze(3).broadcast_to([128, 4, 16, 16])
            pib = Pi.ap[:, j0:j0 + 4, 1:17].unsqueeze(3).broadcast_to([128, 4, 16, 16])
            crb = Cr.ap[:, j0:j0 + 4, :].unsqueeze(2).broadcast_to([128, 4, 16, 16])
            cib = Ci.ap[:, j0:j0 + 4, :].unsqueeze(2).broadcast_to([128, 4, 16, 16])
            rk_ = pk_ + Cr.keys + Ci.keys

            def cw_write(ri, o2):
                for hf in range(2):
                    ps_, cs_ = slice(hf * 64, (hf + 1) * 64), slice(hf * 16, (hf + 1) * 16)
                    for (jsl, bsl) in ((slice(0, 3), slice(0, 3)), (slice(3, 4), slice(4, 5))):
                        o_ = Cw_.ap[ps_, ri, :, bsl, cs_]
                        a_ = tA5[ps_, jsl].rearrange("p j r h -> p r j h")
                        b_ = tB5[ps_, jsl].rearrange("p j r h -> p r j h")
                        vop(o_, a_, b_, o2, tk, Cw_.keys, eng="gpsimd")

            vop(tA5, prb, crb, MUL, rk_, tAf.keys, eng="gpsimd")
            vop(tB5, pib, cib, MUL, rk_, tBf.keys, eng="gpsimd")
            cw_write(0, ADD)
            vop(tA5, prb, cib, MUL, rk_, tAf.keys, eng="gpsimd")
            vop(tB5, pib, crb, MUL, rk_, tBf.keys, eng="gpsimd")
            cw_write(1, SUB)

        def clags(c):
            j0, Cw_, Lw_, ZB_, Sc_ = 4 * c, Cwb[c % 2], Lwb[c % 2], ZBb[c % 2], Scb[c % 2]
            op("vector", lambda v: v.tensor_copy(out=ZB_.ap[:, :, 32:64], in_=Bbm.ap[:, j0 + 3, :, :]), reads=Bbm.keys, writes=ZB_.keys)
            lb_ = fw.bank()
            lb3 = fw.bank()
            for jj in range(4):
                if jj < 3:
                    rs, bank_ = slice(32 * jj, 32 * jj + 32), lb_
                    lt = lambda ri: Bbm.ap[:, j0 + jj, ri, :]
                    lk = Bbm.keys
                else:
                    rs, bank_ = slice(64, 128), lb3
                    lt = lambda ri: ZB_.ap[:, ri, :]
                    lk = ZB_.keys
                for ri in range(2):
                    op("tensor", lambda t, ri=ri: t.matmul(
                        psb[bank_][rs, 0:32], lhsT=lt(ri), rhs=C0m.ap[:, j0 + jj, ri, :],
                        start=(ri == 0), stop=(ri == 1)), reads=lk + C0m.keys, writes=[pk(bank_)])
                for ri in range(2):
                    op("tensor", lambda t, ri=ri: t.matmul(
                        psb[bank_][rs, 32:512].rearrange("p (e m) -> p e m", e=15), lhsT=lt(ri),
                        rhs=Cw_.ap[:, ri, 0:15, CB[jj], :],
                        start=(ri == 0), stop=(ri == 1)), reads=lk + Cw_.keys, writes=[pk(bank_)])
            for jj in range(3):
                rs = slice(32 * jj, 32 * jj + 32)
                evac_copy(Lw_.ap[rs, :, 32 * jj:32 * jj + 32], psb[lb_][rs, :].rearrange("p (t m) -> p t m", t=16),
                          reads=[pk(lb_)], writes=Lw_.keys)
            evac_copy(Lw_.ap[64:128, :, 96:128], psb[lb3][64:128, :].rearrange("p (t m) -> p t m", t=16),
                      reads=[pk(lb3)], writes=Lw_.keys)
            op("vector", lambda v: v.scalar_tensor_tensor(out=Lw_.ap[:, 0, :], in0=ident, scalar=gs("ssm_d", c), in1=Lw_.ap[:, 0, :],
                                                          op0=MUL, op1=ADD), reads=Lw_.keys + ["ident", "gvec"], writes=Lw_.keys)
            op("vector", lambda v: v.tensor_copy(
                out=Sc_.ap[:, :, :, 1:NB],
                in_=Sall.ap[:, 0:NB - 1, :].rearrange("p n (r j) -> p r j n", r=2)[:, :, j0:j0 + 4, :]),
               reads=Sall.keys, writes=Sc_.keys)
            op("vector", lambda v: v.tensor_copy(
                out=Sc_.ap[:, :, :, NB:NBX],
                in_=H0.ap.rearrange("p s (r j) -> p r j s", r=2)[:, :, j0:j0 + 4, :]),
               reads=H0.keys, writes=Sc_.keys)

        RG = [(0, 3), (3, 3), (6, 3), (9, 3), (12, 3), (15, 1)]

        def cy(c, groups):
            Cw_, Lw_, Sc_ = Cwb[c % 2], Lwb[c % 2], Scb[c % 2]
            hr, uv, zv = u_keys(c), hbu(c), hbz(c)
            for (r0, nr) in groups:
                yb_ = fw.bank()
                for tau in range(r0 + nr):
                    ra = max(r0, tau)
                    nrow = r0 + nr - ra
                    op("tensor", lambda t, tau=tau, ra=ra, nrow=nrow, yb_=yb_: t.matmul(
                        psb[yb_][:, (ra - r0) * NBX:(ra - r0 + nrow) * NBX], lhsT=Lw_.ap[:, tau, :],
                        rhs=uv[:, ra - tau:ra - tau + nrow, :].rearrange("p r n -> p (r n)"),
                        start=(tau == 0), stop=False, skip_group_check=True), reads=Lw_.keys + hr, writes=[pk(yb_)])
                for rr in range(nr):
                    r = r0 + rr
                    cols = slice(rr * NBX, (rr + 1) * NBX)
                    for jj in range(4):
                        rs = slice(32 * jj, 32 * jj + 32) if jj < 3 else slice(64, 128)
                        for ri in range(2):
                            lt_ = Cw_.ap[:, ri, r, CB[jj], :] if jj < 3 else Cw_.ap[:, ri, r, 3:5, :].rearrange("p b m -> p (b m)")
                            op("tensor", lambda t, jj=jj, ri=ri, rs=rs, r=r, cols=cols, yb_=yb_, lt_=lt_: t.matmul(
                                psb[yb_][rs, cols], lhsT=lt_, rhs=Sc_.ap[:, ri, jj, :],
                                start=False, stop=(jj == 3 and ri == 1), skip_group_check=True),
                               reads=Cw_.keys + Sc_.keys, writes=[pk(yb_)])
                gelu_bank(yb_, nr * NBX, zv[:, r0:r0 + nr, :].rearrange("p r n -> p (r n)"), lambda a: a, z_keys(c))

        wa = av(0, 16 * K1, BF16, "p (k n) -> p k n", k=8)
        wg2 = av(36 * K1, 16 * K1, BF16, "p (k n) -> p k n", k=8)
        cgen(0)
        clags(0)
        for c in range(NCH):
            if c + 1 < NCH:
                cgen(c + 1)
            else:
                fw.dma("gpsimd", wa.ap, w_glu[:, 0:D].rearrange("(k p) n -> p k n", p=128), writes=wa.keys)
                fw.dma("gpsimd", wg2.ap, w_glu[:, D:2 * D].rearrange("(k p) n -> p k n", p=128), writes=wg2.keys)
            cy(c, RG[0:5])
            if c + 1 < NCH:
                clags(c + 1)
            cy(c, RG[5:6])

        stp = av(91 * K1 - 1 * K1, 1 * K1, F32)
        b = fw.bank()
        for ri in range(2):
            op("tensor", lambda t, ri=ri, b=b: t.transpose(psb[b][0:32, ri * 128:(ri + 1) * 128],
                                                          G.ap[:, NB - 1, ri * 32:(ri + 1) * 32], ident),
               reads=G.keys + ["ident"], writes=[pk(b)])
        evac_copy(stp.ap[0:32, 0:256], psb[b][0:32, 0:256], reads=[pk(b)], writes=stp.keys)
        fw.dma("sync", sre_p, stp.ap[0:32, 0:128], reads=stp.keys)
        fw.dma("sync", sim_p, stp.ap[0:32, 128:256], reads=stp.keys)
        fa = av(87 * K1, 2 * K1, F32, "p (s j) -> p s j", s=NSEQ)
        fb = av(89 * K1, 2 * K1, F32, "p (s j) -> p s j", s=NSEQ)
        p8r = Pr.ap[:, :, 16].unsqueeze(1).broadcast_to([128, NSEQ, 32])
        p8i = Pi.ap[:, :, 16].unsqueeze(1).broadcast_to([128, NSEQ, 32])
        fk = fa.keys + fb.keys
        Gre, Gim = G.ap[:, NB:NBX, 0:32], G.ap[:, NB:NBX, 32:64]
        H0r, H0i = H0.ap[:, :, 0:32], H0.ap[:, :, 32:64]
        vop(fa.ap, H0r, p8r, MUL, H0.keys + pk_, fk)
        vop(fb.ap, H0i, p8i, MUL, H0.keys + pk_, fk)
        vop(Gre, Gre, fa.ap, ADD, fk + G.keys, G.keys)
        vop(Gre, Gre, fb.ap, SUB, fk + G.keys, G.keys)
        vop(fa.ap, H0i, p8r, MUL, H0.keys + pk_, fk)
        vop(fb.ap, H0r, p8i, MUL, H0.keys + pk_, fk)
        vop(Gim, Gim, fa.ap, ADD, fk + G.keys, G.keys)
        vop(Gim, Gim, fb.ap, ADD, fk + G.keys, G.keys)
        fst = av(83 * K1, 4 * K1, F32, "p (r s j) -> p r s j", r=2, s=NSEQ)
        op("vector", lambda v: v.tensor_copy(out=fst.ap, in_=G.ap[:, NB:NBX, :].rearrange("p s (r j) -> p r s j", r=2)),
           reads=G.keys, writes=fst.keys)
        for ri, dst in enumerate([sre_s, sim_s]):
            b = fw.bank()
            for sb in range(4):
                op("tensor", lambda t, sb=sb, b=b, ri=ri: t.transpose(
                    psb[b][:, sb * 128:(sb + 1) * 128], fst.ap[:, ri, sb * 4:(sb + 1) * 4, :].rearrange("p s j -> p (s j)"), ident),
                   reads=fst.keys + ["ident"], writes=[pk(b)])
            so = av(75 * K1 + ri * 2 * K1, 2 * K1, F32)
            evac_copy(so.ap, psb[b], reads=[pk(b)], writes=so.keys)
            fw.dma("sync", dst.rearrange("(sb r) q -> r sb q", r=128), so.ap.rearrange("p (sb q) -> p sb q", sb=4), reads=so.keys)
        for tbi, (t0, n) in enumerate(TBS):
            hr = [hk(k, t_) for k in range(NCH) for t_ in range(5)]
            if tbi < 4:
                n0 = t0 // LCH
                zr_ = lambda k: hbz(k)[:, :, n0:n0 + 32]
                pview = lambda a: a.rearrange("p (r n) -> p r n", r=LCH)
                xview = lambda a: a.rearrange("p (n r) -> p r n", r=LCH)
            else:
                zr_ = lambda k: hbz(k)[:, 8:16, NB:NBX]
                pview = lambda a: a.rearrange("p (t s) -> p t s", t=TSEQ)
                xview = lambda a: a.rearrange("p (s t) -> p t s", t=TSEQ)
            for c in range(NCH):
                mm_val = [(wa.ap[:, k, c * 128:(c + 1) * 128], zr_(k), wa.keys + hr) for k in range(NCH)]
                mm_gate = [(wg2.ap[:, k, c * 128:(c + 1) * 128], zr_(k), wg2.keys + hr) for k in range(NCH)]
                gated_block(tbi, c, mm_val, mm_gate, pview=pview, xview=xview, tmp_base=75 * K1)

    octr_ = [0]

    def final_tb(tbi):
        yfin = av(28 * K1, 16 * K1, F32, "p (c t) -> p c t", c=8)
        yo = [av(20 * K1, 4 * K1, F32), av(24 * K1, 4 * K1, F32)]
        t0, n = TBS[tbi]
        b = nctr[0] % 2
        nctr[0] += 1
        sq = av(SQB[b], 8 * K1, BF16, "p (c t) -> p c t", c=8)
        for c in range(NCH):
            op("scalar", lambda s, c=c: s.activation(out=sq.ap[:, c, 0:n], in_=x[:, c, t0:t0 + n], func=AF.Square),
               reads=[xk(c, tbi)], writes=sq.keys)
        yield
        pb = fw.bank()
        for c in range(NCH):
            op("tensor", lambda t, c=c: t.matmul(psb[pb][:, 0:n], lhsT=ones_bf, rhs=sq.ap[:, c, 0:n],
                                                 start=(c == 0), stop=(c == NCH - 1)),
               reads=sq.keys + ["ones"], writes=[pk(pb)])
        rt = av(RT, 2 * K1, F32)
        rstd = av(RSTD[b], 2 * K1, F32)
        op("scalar", lambda s: s.activation(out=rt.ap[:, 0:n], in_=psb[pb][:, 0:n], func=AF.Ln, scale=1.0 / D, bias=EPS),
           reads=[pk(pb)], writes=rt.keys)
        op("scalar", lambda s: s.activation(out=rstd.ap[:, 0:n], in_=rt.ap[:, 0:n], func=AF.Exp, scale=-0.5),
           reads=rt.keys, writes=rstd.keys)
        for c in range(NCH):
            op("vector", lambda v, c=c: v.scalar_tensor_tensor(
                out=yfin.ap[:, c, 0:n], in0=x[:, c, t0:t0 + n], scalar=gs("g_final", c), in1=rstd.ap[:, 0:n],
                op0=ALU.mult, op1=ALU.mult), reads=[xk(c, tbi), "gvec"] + rstd.keys, writes=yfin.keys)
        yield
        for q in range(n // 128):
            tt = t0 // 128 + q
            o = yo[octr_[0] % 2]
            octr_[0] += 1
            for half in range(2):
                b_ = fw.bank()
                for cc in range(4):
                    c = half * 4 + cc
                    op("tensor", lambda t, b_=b_, cc=cc, c=c, q=q: t.transpose(
                        psb[b_][:, cc * 128:(cc + 1) * 128], yfin.ap[:, c, q * 128:(q + 1) * 128], ident),
                       reads=yfin.keys + ["ident"], writes=[pk(b_)])
                evac_copy(o.ap[:, half * 512:(half + 1) * 512], psb[b_], reads=[pk(b_)], writes=o.keys)
            dst = y_p[tt * 128:(tt + 1) * 128, :] if tt < 16 else y_s
            fw.dma("sync", dst, o.ap, reads=o.keys)
            if q % 2 == 1:
                yield

    load_ffn_slice(0, 0)
    pool_mixer(pre_hook=x_load_tb,
               post_hook=lambda tbi: norm_to_hb(tbi - 1, "g_ffn0") if tbi >= 1 else None)
    norm_to_hb(4, "g_ffn0")
    ffn(0, pre_normed=True, each_gen=ple_ptrans(0), mid_hook=lambda: ple_weights(0),
        tail_hook=lambda tbi: ple_norm(0, tbi))
    ssm_gen = ssm_mixer() if enable_ssm else iter(())
    next(ssm_gen, None)
    ple_body(0)
    for _ in ssm_gen:
        pass
    load_ffn_slice(1, 0)
    ffn(1, each_gen=ple_ptrans(1), mid_hook=lambda: ple_weights(1), tail_hook=lambda tbi: ple_norm(1, tbi))
    ple_body(1, tb_gen=final_tb)
    fw.finish("sync")
    return nc


_NC_CACHE = {}


def _get_program(enable_ssm=True):
    if enable_ssm not in _NC_CACHE:
        _NC_CACHE[enable_ssm] = build_program(enable_ssm)
    return _NC_CACHE[enable_ssm]


def kernel(x_prompt, x_sample, state_pool, state_ssm_re, state_ssm_im, p_prompt, p_sample,
           g_mix, g_ffn, g_ple, g_final, pool_w, pool_scale,
           ssm_lambda_re, ssm_lambda_im, ssm_log_dt, ssm_b_re, ssm_b_im, ssm_c_re, ssm_c_im,
           ssm_d, ssm_w_glu, ffn_w_gate, ffn_w_up, ffn_w_down, ple_w_in, ple_w_gate, _enable_ssm=True):
    f = lambda a: np.ascontiguousarray(np.asarray(a, dtype=np.float32))
    x_prompt, x_sample, state_pool, state_ssm_re, state_ssm_im, p_prompt, p_sample = map(
        f, (x_prompt, x_sample, state_pool, state_ssm_re, state_ssm_im, p_prompt, p_sample))
    gvecs = np.ascontiguousarray(np.concatenate([f(g_mix), f(g_ffn), f(g_ple), f(g_final)[None, :], f(pool_scale), f(ssm_d)], axis=0))
    shared = {
        "gvecs": gvecs,
        "pool_w": f(pool_w)[0],
        "lam_re": f(ssm_lambda_re)[0].reshape(32, 128), "lam_im": f(ssm_lambda_im)[0].reshape(32, 128),
        "log_dt": f(ssm_log_dt)[0].reshape(32, 2),
        "b_re": f(ssm_b_re)[0].reshape(32, 128, 16), "b_im": f(ssm_b_im)[0].reshape(32, 128, 16),
        "c_re": f(ssm_c_re)[0].reshape(32, 2, 16, 64), "c_im": f(ssm_c_im)[0].reshape(32, 2, 16, 64),
        "w_glu": f(ssm_w_glu)[0],
        "w_gate": f(ffn_w_gate), "w_up": f(ffn_w_up), "w_down": f(ffn_w_down),
        "ple_w_in": f(ple_w_in), "ple_w_gate": f(ple_w_gate),
    }
    in_maps = []
    for i in range(NCORES):
        sl = slice(i * NSEQ, (i + 1) * NSEQ)
        m = dict(shared)
        m["xp"] = x_prompt[i]
        m["xs"] = x_sample[sl].reshape(TS, D)
        m["pp"] = np.ascontiguousarray(p_prompt[:, i])
        m["psm"] = np.ascontiguousarray(p_sample[:, sl].reshape(2, TS, PLE))
        m["spool"] = state_pool[0, sl].reshape(NSEQ * 15, D)
        m["sre"] = state_ssm_re[0, sl].reshape(NSEQ * 32, 128)
        m["sim"] = state_ssm_im[0, sl].reshape(NSEQ * 32, 128)
        in_maps.append(m)
    nc = _get_program(_enable_ssm)
    res = run_bass_kernel_spmd(nc, in_maps, core_ids=list(range(NCORES)))
    r = res.results
    y_prompt = np.stack([r[i]["y_p"] for i in range(NCORES)], 0)
    y_sample = np.concatenate([r[i]["y_s"].reshape(NSEQ, TSEQ, D) for i in range(NCORES)], 0)
    pool_prompt = np.stack([r[i]["pool_p"] for i in range(NCORES)], 0)[None]
    pool_sample = np.concatenate([r[i]["pool_s"].reshape(NSEQ, 15, D) for i in range(NCORES)], 0)[None]
    sre_p = np.stack([r[i]["sre_p"].reshape(64, 64) for i in range(NCORES)], 0)[None]
    sim_p = np.stack([r[i]["sim_p"].reshape(64, 64) for i in range(NCORES)], 0)[None]
    sre_s = np.concatenate([r[i]["sre_s"].reshape(NSEQ, 64, 64) for i in range(NCORES)], 0)[None]
    sim_s = np.concatenate([r[i]["sim_s"].reshape(NSEQ, 64, 64) for i in range(NCORES)], 0)[None]
    return (y_prompt, y_sample, pool_prompt, pool_sample, sre_p, sim_p, sre_s, sim_s)
```

```python
import numpy as np
import concourse.bass as bass
import concourse.mybir as mybir
from concourse.bass_utils import run_bass_kernel_spmd

F32 = mybir.dt.float32
BF16 = mybir.dt.bfloat16
I32 = mybir.dt.int32
AF = mybir.ActivationFunctionType
ALU = mybir.AluOpType

NCORES = 8
D = 1024
NCH = 8
TP = 2048
NSEQ = 16
TSEQ = 8
TS = NSEQ * TSEQ
NT = TP + TS
DFF = 2816
PLE = 256
EPS = 1e-6
HPAD = 16
NBX = 144
HW = HPAD + 16 * NBX
TBS = [(0, 512), (512, 512), (1024, 512), (1536, 512), (2048, 128)]
POOL_W = (2, 4, 8, 16)
FF_SLICES = [(0, 4), (4, 4), (8, 4), (12, 4), (16, 3), (19, 3)]
LCH = 16
NB = TP // LCH
GV = {"g_mix0": 0, "g_mix1": 1, "g_ffn0": 2, "g_ffn1": 3, "g_ple0": 4, "g_ple1": 5, "g_final": 6,
      "pool_scale": 7, "ssm_d": 8}
ARENA = 97 * 1024
GRAN = 1024


class FW:
    ENG = ("tensor", "vector", "scalar", "gpsimd", "sync")

    def __init__(self, nc, n_dma_sems=32):
        self.nc = nc
        self.eng = {e: getattr(nc, e) for e in self.ENG}
        self.esem = {e: nc.alloc_semaphore("es_" + e) for e in self.ENG}
        self.ecount = {e: 0 for e in self.ENG}
        self.waited = {e: {} for e in self.ENG}
        self.dsems = [nc.alloc_semaphore("ds%d" % i) for i in range(n_dma_sems)]
        self.dcount = [0] * n_dma_sems
        self.dnext = 0
        self.last_write = {}
        self.reads_since = {}
        self.ps_next = 0
        self.gsems = []

    def _wait(self, e, dep):
        sem, val = dep
        key = id(sem)
        if self.waited[e].get(key, 0) >= val:
            return
        self.eng[e].wait_ge(sem, val)
        self.waited[e][key] = val

    def _deps(self, reads, writes):
        d = []
        for k in reads:
            if k in self.last_write:
                d.append(self.last_write[k])
        for k in writes:
            if k in self.last_write:
                d.append(self.last_write[k])
            d.extend(self.reads_since.get(k, ()))
        return d

    def _commit(self, dep, reads, writes):
        for k in reads:
            lst = self.reads_since.setdefault(k, [])
            lst[:] = [x for x in lst if x[0] is not dep[0]]
            lst.append(dep)
        for k in writes:
            self.last_write[k] = dep
            self.reads_since[k] = []

    def op(self, e, fn, reads=(), writes=()):
        own = self.esem[e]
        for dep in self._deps(reads, writes):
            if e == "tensor" and dep[0] is own:
                continue
            self._wait(e, dep)
        inst = fn(self.eng[e])
        self.ecount[e] += 1
        inst.then_inc(own, 1)
        dep = (own, self.ecount[e])
        self._commit(dep, reads, writes)
        return dep

    def dma(self, e, out, in_, reads=(), writes=(), **kw):
        if e == "gpsimd":
            sem = self.nc.alloc_semaphore("gd%d" % len(self.gsems))
            self.gsems.append(sem)
            for dep in self._deps(reads, writes):
                self._wait(e, dep)
            self.eng[e].dma_start(out=out, in_=in_, **kw).then_inc(sem, 16)
            dep = (sem, 16)
            self._commit(dep, reads, writes)
            return dep
        i = self.dnext
        self.dnext = (self.dnext + 1) % len(self.dsems)
        sem = self.dsems[i]
        if self.dcount[i] > 0:
            self._wait(e, (sem, self.dcount[i]))
        for dep in self._deps(reads, writes):
            self._wait(e, dep)
        self.eng[e].dma_start(out=out, in_=in_, **kw).then_inc(sem, 16)
        self.dcount[i] += 16
        dep = (sem, self.dcount[i])
        self._commit(dep, reads, writes)
        return dep

    def finish(self, e="sync"):
        for i, sem in enumerate(self.dsems):
            if self.dcount[i] > 0:
                self._wait(e, (sem, self.dcount[i]))
        for sem in self.gsems:
            self._wait(e, (sem, 16))

    def bank(self):
        b = self.ps_next
        self.ps_next = (self.ps_next + 1) % 8
        return b


class V:
    def __init__(self, ap, keys):
        self.ap = ap
        self.keys = keys


def build_program(enable_ssm=True):
    nc = bass.Bass("TRN2", target_bir_lowering=False)
    fw = FW(nc)

    def din(name, shape):
        return nc.dram_tensor(name, list(shape), F32, kind="ExternalInput").ap()

    def dout(name, shape):
        return nc.dram_tensor(name, list(shape), F32, kind="ExternalOutput").ap()

    xp = din("xp", [TP, D]); xs = din("xs", [TS, D])
    pp = din("pp", [2, TP, PLE]); psm = din("psm", [2, TS, PLE])
    spool = din("spool", [NSEQ * 15, D])
    sre = din("sre", [NSEQ * 32, 128]); sim = din("sim", [NSEQ * 32, 128])
    gvecs = din("gvecs", [9, D])
    pool_w = din("pool_w", [4, 256, 256])
    lam_re = din("lam_re", [32, 128]); lam_im = din("lam_im", [32, 128]); log_dt = din("log_dt", [32, 2])
    b_re = din("b_re", [32, 128, 16]); b_im = din("b_im", [32, 128, 16])
    c_re = din("c_re", [32, 2, 16, 64]); c_im = din("c_im", [32, 2, 16, 64])
    w_glu = din("w_glu", [D, 2 * D])
    w_gate = din("w_gate", [2, D, DFF]); w_up = din("w_up", [2, D, DFF]); w_down = din("w_down", [2, DFF, D])
    ple_w_in = din("ple_w_in", [2, PLE, D]); ple_w_gate = din("ple_w_gate", [2, D, D])

    y_p = dout("y_p", [TP, D]); y_s = dout("y_s", [TS, D])
    pool_p = dout("pool_p", [15, D]); pool_s = dout("pool_s", [NSEQ * 15, D])
    sre_p = dout("sre_p", [32, 128]); sim_p = dout("sim_p", [32, 128])
    sre_s = dout("sre_s", [NSEQ * 32, 128]); sim_s = dout("sim_s", [NSEQ * 32, 128])

    x = nc.alloc_sbuf_tensor("x", [128, NCH, NT], F32).ap()
    hb = nc.alloc_sbuf_tensor("hb", [128, NCH + 1, HW], BF16).ap()
    ident = nc.alloc_sbuf_tensor("ident", [128, 128], F32).ap()
    ones_bf = nc.alloc_sbuf_tensor("ones_bf", [128, 128], BF16).ap()
    gvec = nc.alloc_sbuf_tensor("gvec", [128, NCH, 16], F32).ap()
    iot = nc.alloc_sbuf_tensor("iot", [128, 128], I32).ap()
    R = nc.alloc_sbuf_tensor("arena", [128, ARENA // 2], BF16).ap()
    psb = [nc.alloc_psum_tensor("ps%d" % i, [128, 512], F32).ap() for i in range(8)]

    def av(off, nbytes, dtype=BF16, pat=None, **dims):
        assert off % 4 == 0 and nbytes % 4 == 0 and off + nbytes <= ARENA, (off, nbytes)
        ap = R[:, off // 2:(off + nbytes) // 2]
        if dtype != BF16:
            ap = ap.bitcast(dtype)
        if pat is not None:
            ap = ap.rearrange(pat, **dims)
        keys = [("R", s) for s in range(off // GRAN, (off + nbytes + GRAN - 1) // GRAN)]
        return V(ap, keys)

    def xk(c, tbi):
        return ("x", c, tbi)

    def hk(c, tbi):
        return ("h", c, tbi)

    def pk(b):
        return ("ps", b)

    op = fw.op
    K1 = 1024

    op("gpsimd", lambda g: g.iota(iot, [[1, 128]], base=0, channel_multiplier=-1), writes=["iot"])
    op("vector", lambda v: v.tensor_scalar(out=ident, in0=iot, scalar1=0, scalar2=None, op0=ALU.is_equal),
       reads=["iot"], writes=["ident"])
    op("vector", lambda v: v.memset(ones_bf, 1.0), writes=["ones"])
    op("gpsimd", lambda g: g.memset(hb[:, :, 0:HPAD], 0.0), writes=["hpad"])

    gv_rows = av(88 * K1, 4096, F32)
    for i in range(9):
        fw.dma("sync", gv_rows.ap[i:i + 1, :], gvecs[i:i + 1, :], writes=gv_rows.keys)
    b0 = fw.bank()
    for c in range(NCH):
        op("tensor", lambda t, c=c: t.transpose(psb[b0][:, c * 16:c * 16 + 9], gv_rows.ap[0:9, c * 128:(c + 1) * 128],
                                                 ident[0:9, 0:9]),
           reads=gv_rows.keys + ["ident"], writes=[pk(b0)])
    op("vector", lambda v: v.tensor_copy(out=gvec[:, :, 0:9], in_=psb[b0][:, 0:128].rearrange("p (c i) -> p c i", i=16)[:, :, 0:9]),
       reads=[pk(b0)], writes=["gvec"])

    def gs(name, c):
        i = GV[name]
        return gvec[:, c, i:i + 1]

    WFF = [0, 24 * K1]

    def ffn_views(b, nf):
        o = WFF[b]
        wg = av(o, 8 * K1, BF16, "p (k n) -> p k n", k=8)
        wu = av(o + 8 * K1, 8 * K1, BF16, "p (k n) -> p k n", k=8)
        wd = av(o + 16 * K1, 8 * K1, BF16, "p (f n) -> p f n", f=4)
        return wg, wu, wd

    def load_ffn_slice(layer, si):
        f0, nf = FF_SLICES[si]
        b = si % 2
        wg, wu, wd = ffn_views(b, nf)
        c0, c1 = f0 * 128, (f0 + nf) * 128
        fw.dma("gpsimd", wg.ap[:, :, 0:nf * 128], w_gate[layer, :, c0:c1].rearrange("(k p) n -> p k n", p=128),
               writes=wg.keys)
        fw.dma("gpsimd", wu.ap[:, :, 0:nf * 128], w_up[layer, :, c0:c1].rearrange("(k p) n -> p k n", p=128),
               writes=wu.keys)
        fw.dma("gpsimd", wd.ap[:, 0:nf, :], w_down[layer, c0:c1, :].rearrange("(f p) n -> p f n", p=128),
               writes=wd.keys)

    xin = [av(63 * K1, 4 * K1, F32), av(93 * K1, 4 * K1, F32),
           V(hb[:, NCH, HPAD:HPAD + 2 * D].bitcast(F32), [hk(NCH, t_) for t_ in range(5)])]
    ev = [0]

    def evac_copy(out, in_, reads, writes, eng=None):
        ev[0] += 1
        if eng == "vector" or (eng is None and ev[0] % 2 == 0):
            op("vector", lambda v: v.tensor_copy(out=out, in_=in_), reads=reads, writes=writes)
        else:
            op("scalar", lambda s: s.copy(out=out, in_=in_), reads=reads, writes=writes)

    def x_load_tb(tbi):
        t0b, nb_ = TBS[tbi]
        for tt in range(t0b // 128, (t0b + nb_) // 128):
            xi = xin[tt % 3]
            src = xp[tt * 128:(tt + 1) * 128, :] if tt < 16 else xs
            fw.dma("sync", xi.ap, src, writes=xi.keys)
            t0 = tt * 128
            for half in range(2):
                b = fw.bank()
                for cc in range(4):
                    c = half * 4 + cc
                    op("tensor", lambda t, b=b, cc=cc, c=c, xi=xi: t.transpose(psb[b][:, cc * 128:(cc + 1) * 128],
                                                                             xi.ap[:, c * 128:(c + 1) * 128], ident),
                       reads=xi.keys + ["ident"], writes=[pk(b)])
                evac_copy(x[:, half * 4:half * 4 + 4, t0:t0 + 128], psb[b].rearrange("p (c t) -> p c t", c=4),
                          reads=[pk(b)], writes=[xk(c, tbi) for c in range(half * 4, half * 4 + 4)])

    SQB = [68 * K1, 76 * K1]
    RT = 84 * K1
    RSTD = [86 * K1, 88 * K1]
    nctr = [0]

    def norm_stats(tbi):
        t0, n = TBS[tbi]
        b = nctr[0] % 2
        nctr[0] += 1
        sq = av(SQB[b], 8 * K1, BF16, "p (c t) -> p c t", c=8)
        for c in range(NCH):
            op("scalar", lambda s, c=c: s.activation(out=sq.ap[:, c, 0:n], in_=x[:, c, t0:t0 + n], func=AF.Square),
               reads=[xk(c, tbi)], writes=sq.keys)
        pb = fw.bank()
        for c in range(NCH):
            op("tensor", lambda t, c=c: t.matmul(psb[pb][:, 0:n], lhsT=ones_bf, rhs=sq.ap[:, c, 0:n],
                                                 start=(c == 0), stop=(c == NCH - 1)),
               reads=sq.keys + ["ones"], writes=[pk(pb)])
        rt = av(RT, 2 * K1, F32)
        rstd = av(RSTD[b], 2 * K1, F32)
        op("scalar", lambda s: s.activation(out=rt.ap[:, 0:n], in_=psb[pb][:, 0:n], func=AF.Ln, scale=1.0 / D, bias=EPS),
           reads=[pk(pb)], writes=rt.keys)
        op("scalar", lambda s: s.activation(out=rstd.ap[:, 0:n], in_=rt.ap[:, 0:n], func=AF.Exp, scale=-0.5),
           reads=rt.keys, writes=rstd.keys)
        return rstd

    def norm_gen(tbi, gname, dst_fn=None, out=None):
        t0, n = TBS[tbi]
        b = nctr[0] % 2
        nctr[0] += 1
        sq = av(SQB[b], 8 * K1, BF16, "p (c t) -> p c t", c=8)
        for c in range(NCH):
            op("scalar", lambda s, c=c: s.activation(out=sq.ap[:, c, 0:n], in_=x[:, c, t0:t0 + n], func=AF.Square),
               reads=[xk(c, tbi)], writes=sq.keys)
        yield
        pb = fw.bank()
        for c in range(NCH):
            op("tensor", lambda t, c=c: t.matmul(psb[pb][:, 0:n], lhsT=ones_bf, rhs=sq.ap[:, c, 0:n],
                                                 start=(c == 0), stop=(c == NCH - 1)),
               reads=sq.keys + ["ones"], writes=[pk(pb)])
        rt = av(RT, 2 * K1, F32)
        rstd = av(RSTD[b], 2 * K1, F32)
        op("scalar", lambda s: s.activation(out=rt.ap[:, 0:n], in_=psb[pb][:, 0:n], func=AF.Ln, scale=1.0 / D, bias=EPS),
           reads=[pk(pb)], writes=rt.keys)
        op("scalar", lambda s: s.activation(out=rstd.ap[:, 0:n], in_=rt.ap[:, 0:n], func=AF.Exp, scale=-0.5),
           reads=rt.keys, writes=rstd.keys)
        for c in range(NCH):
            if dst_fn is None:
                o, in0, in1, wk = hb[:, c, HPAD + t0:HPAD + t0 + n], x[:, c, t0:t0 + n], rstd.ap[:, 0:n], [hk(c, tbi)]
            else:
                o, in0, in1, wk = dst_fn(c, rstd)
            op("vector", lambda v, o=o, in0=in0, in1=in1, c=c: v.scalar_tensor_tensor(
                out=o, in0=in0, scalar=gs(gname, c), in1=in1, op0=ALU.mult, op1=ALU.mult),
               reads=[xk(c, tbi), "gvec"] + rstd.keys, writes=wk)
        if out is not None:
            out.append(rstd)

    def norm_to_hb(tbi, gname, dst_fn=None):
        t0, n = TBS[tbi]
        rstd = norm_stats(tbi)
        for c in range(NCH):
            if dst_fn is None:
                o, in0, in1, wk = hb[:, c, HPAD + t0:HPAD + t0 + n], x[:, c, t0:t0 + n], rstd.ap[:, 0:n], [hk(c, tbi)]
            else:
                o, in0, in1, wk = dst_fn(c, rstd)
            op("vector", lambda v, o=o, in0=in0, in1=in1, c=c: v.scalar_tensor_tensor(
                out=o, in0=in0, scalar=gs(gname, c), in1=in1, op0=ALU.mult, op1=ALU.mult),
               reads=[xk(c, tbi), "gvec"] + rstd.keys, writes=wk)
        return rstd

    ACT_B = [48 * K1, 52 * K1]
    SL_B = [56 * K1, 57 * K1]

    def ffn(layer, pre_normed=False, start_hook=None, mid_hook=None, tail_hook=None, each_gen=None):
        gname = "g_ffn%d" % layer
        if not pre_normed:
            for tbi in range(2):
                norm_to_hb(tbi, gname)
        if start_hook is not None:
            start_hook()
        ctr = 0
        for si, (f0, nf) in enumerate(FF_SLICES):
            if si + 1 < len(FF_SLICES):
                load_ffn_slice(layer, si + 1)
            wg, wu, wd = ffn_views(si % 2, nf)

            def gate_up(tbi, actv):
                t0, n = TBS[tbi]
                hr = [hk(k, tbi) for k in range(NCH)]
                for f in range(nf):
                    pg = fw.bank()
                    for k in range(NCH):
                        op("tensor", lambda t, k=k, f=f, pg=pg: t.matmul(
                            psb[pg][:, 0:n], lhsT=wg.ap[:, k, f * 128:(f + 1) * 128], rhs=hb[:, k, HPAD + t0:HPAD + t0 + n],
                            start=(k == 0), stop=(k == NCH - 1)), reads=wg.keys + hr, writes=[pk(pg)])
                    pu = fw.bank()
                    for k in range(NCH):
                        op("tensor", lambda t, k=k, f=f, pu=pu: t.matmul(
                            psb[pu][:, 0:n], lhsT=wu.ap[:, k, f * 128:(f + 1) * 128], rhs=hb[:, k, HPAD + t0:HPAD + t0 + n],
                            start=(k == 0), stop=(k == NCH - 1)), reads=wu.keys + hr, writes=[pk(pu)])
                    sl = av(SL_B[f % 2], 1 * K1, BF16)
                    op("scalar", lambda s, pg=pg, sl=sl: s.activation(out=sl.ap[:, 0:n], in_=psb[pg][:, 0:n], func=AF.Silu),
                       reads=[pk(pg)], writes=sl.keys)
                    op("vector", lambda v, pu=pu, sl=sl, f=f: v.tensor_tensor(
                        out=actv.ap[:, f, 0:n], in0=psb[pu][:, 0:n], in1=sl.ap[:, 0:n], op=ALU.mult),
                       reads=[pk(pu)] + sl.keys, writes=actv.keys)

            def down(tbi, actv):
                t0, n = TBS[tbi]
                for c in range(NCH):
                    pd = fw.bank()
                    for f in range(nf):
                        op("tensor", lambda t, f=f, c=c, pd=pd: t.matmul(
                            psb[pd][:, 0:n], lhsT=wd.ap[:, f, c * 128:(c + 1) * 128], rhs=actv.ap[:, f, 0:n],
                            start=(f == 0), stop=(f == nf - 1)), reads=wd.keys + actv.keys, writes=[pk(pd)])
                    op("vector", lambda v, c=c, pd=pd: v.tensor_tensor(
                        out=x[:, c, t0:t0 + n], in0=psb[pd][:, 0:n], in1=x[:, c, t0:t0 + n], op=ALU.add),
                       reads=[pk(pd), xk(c, tbi)], writes=[xk(c, tbi)])
                if each_gen is not None:
                    next(each_gen, None)
                if si == len(FF_SLICES) - 1 and tail_hook is not None:
                    if tbi >= 1:
                        tail_hook(tbi - 1)
                    if tbi == len(TBS) - 1:
                        tail_hook(tbi)

            prev = None
            for tbi in range(len(TBS)):
                actv = av(ACT_B[ctr % 2], 4 * K1, BF16, "p (f t) -> p f t", f=4)
                ctr += 1
                gate_up(tbi, actv)
                if not pre_normed and si == 0 and tbi + 2 < len(TBS):
                    norm_to_hb(tbi + 2, gname)
                if prev is not None:
                    down(*prev)
                prev = (tbi, actv)
            down(*prev)
            if si == len(FF_SLICES) - 2 and mid_hook is not None:
                mid_hook()

    SG_B = [48 * K1, 50 * K1]
    T2_B = [52 * K1, 54 * K1]
    gctr = [0]

    def gated_block(tbi, c, mm_val, mm_gate, pview=None, xview=None, tmp_base=None):
        t0, n = TBS[tbi]
        pview = pview or (lambda a: a)
        xview = xview or (lambda a: a)
        pg = fw.bank()
        for i, (l, r, rk) in enumerate(mm_gate):
            op("tensor", lambda t, l=l, r=r, i=i: t.matmul(pview(psb[pg][:, 0:n]), lhsT=l, rhs=r, start=(i == 0),
                                                          stop=(i == len(mm_gate) - 1)), reads=rk, writes=[pk(pg)])
        pv = fw.bank()
        for i, (l, r, rk) in enumerate(mm_val):
            op("tensor", lambda t, l=l, r=r, i=i: t.matmul(pview(psb[pv][:, 0:n]), lhsT=l, rhs=r, start=(i == 0),
                                                          stop=(i == len(mm_val) - 1)), reads=rk, writes=[pk(pv)])
        b = gctr[0] % 2
        gctr[0] += 1
        if tmp_base is None:
            sg = av(SG_B[b], 2 * K1, F32)
            t2 = av(T2_B[b], 2 * K1, F32)
        else:
            sg = av(tmp_base + b * 4 * K1, 2 * K1, F32)
            t2 = av(tmp_base + b * 4 * K1 + 2 * K1, 2 * K1, F32)
        op("scalar", lambda s: s.activation(out=sg.ap[:, 0:n], in_=psb[pg][:, 0:n], func=AF.Sigmoid),
           reads=[pk(pg)], writes=sg.keys)
        op("vector", lambda v: v.tensor_tensor(out=t2.ap[:, 0:n], in0=psb[pv][:, 0:n], in1=sg.ap[:, 0:n], op=ALU.mult),
           reads=[pk(pv)] + sg.keys, writes=t2.keys)
        xv = xview(x[:, c, t0:t0 + n])
        op("gpsimd", lambda g: g.tensor_tensor(out=xv, in0=xv, in1=pview(t2.ap[:, 0:n]), op=ALU.add),
           reads=t2.keys + [xk(c, tbi)], writes=[xk(c, tbi)])

    def ple_views():
        wgt = av(0, 16 * K1, BF16, "p (k n) -> p k n", k=8)
        win = av(16 * K1, 4 * K1, BF16, "p (k n) -> p k n", k=2)
        pT = av(58 * K1, 2 * NT * 2, BF16, "p (k t) -> p k t", k=2)
        return wgt, win, pT

    def ple_weights(layer):
        wgt, win, pT = ple_views()
        fw.dma("gpsimd", wgt.ap, ple_w_gate[layer].rearrange("(k p) n -> p k n", p=128), writes=wgt.keys)
        fw.dma("gpsimd", win.ap, ple_w_in[layer].rearrange("(k p) n -> p k n", p=128), writes=win.keys)

    def ple_ptrans(layer):
        wgt, win, pT = ple_views()
        pin = [av(90 * K1, 1 * K1, F32), av(91 * K1, 1 * K1, F32)]
        for tt in range(NT // 128):
            pi_ = pin[tt % 2]
            src = pp[layer, tt * 128:(tt + 1) * 128, :] if tt < 16 else psm[layer]
            fw.dma("sync", pi_.ap, src, writes=pi_.keys)
            b = fw.bank()
            for k in range(2):
                op("tensor", lambda t, k=k, b=b, pi_=pi_: t.transpose(psb[b][:, k * 128:(k + 1) * 128],
                                                                     pi_.ap[:, k * 128:(k + 1) * 128], ident),
                   reads=pi_.keys + ["ident"], writes=[pk(b)])
            evac_copy(pT.ap[:, :, tt * 128:(tt + 1) * 128], psb[b][:, 0:256].rearrange("p (k t) -> p k t", k=2),
                      reads=[pk(b)], writes=pT.keys)
            yield

    def ple_norm(layer, tbi):
        norm_to_hb(tbi, "g_ple%d" % layer)

    def ple_body(layer, tb_gen=None):
        wgt, win, pT = ple_views()
        pending = None
        for tbi, (t0, n) in enumerate(TBS):
            hr = [hk(k, tbi) for k in range(NCH)]
            for c in range(NCH):
                mm_gate = [(wgt.ap[:, k, c * 128:(c + 1) * 128], hb[:, k, HPAD + t0:HPAD + t0 + n], wgt.keys + hr)
                           for k in range(NCH)]
                mm_val = [(win.ap[:, k, c * 128:(c + 1) * 128], pT.ap[:, k, t0:t0 + n], win.keys + pT.keys)
                          for k in range(2)]
                gated_block(tbi, c, mm_val, mm_gate)
                if pending is not None and c in (0, 2, 4, 6):
                    next(pending, None)
            if pending is not None:
                for _ in pending:
                    pass
            if tb_gen is not None:
                pending = tb_gen(tbi)
        if pending is not None:
            for _ in pending:
                pass

    def pool_mixer(pre_hook=None, post_hook=None):
        wpool = av(24 * K1, 4 * K1, BF16, "p (g k n) -> p g k n", g=4, k=2)
        fw.dma("gpsimd", wpool.ap, pool_w.rearrange("g (k p) n -> p g k n", p=128), writes=wpool.keys)
        hs = av(28 * K1, 8 * NSEQ * 23 * 2, BF16, "p (c s t) -> p c s t", c=8, s=NSEQ)
        inv = av(34 * K1, 4 * 16 * 4, F32, "p (g t) -> p g t", g=4)
        wpos = av(35 * K1, 4 * K1, BF16, "p (g k n) -> p g k n", g=4, k=2)
        sp = [av(42 * K1, 4 * K1, F32), av(46 * K1, 4 * K1, F32)]
        h32t = av(50 * K1, 8 * 15 * 4, F32, "p (c t) -> p c t", c=8)
        h32s = av(51 * K1, 8 * 128 * 4, F32, "p (c t) -> p c t", c=8)
        ot = av(42 * K1, 4 * K1, F32)
        ot2 = av(46 * K1, 4 * K1, F32)
        p2s = [av(55 * K1, 2 * K1, F32), av(57 * K1, 2 * K1, F32)]
        t2b = [av(59 * K1, 2 * K1, F32), av(61 * K1, 2 * K1, F32)]
        for gi, w in enumerate(POOL_W):
            op("vector", lambda g, gi=gi, w=w: g.tensor_scalar(out=wpos.ap[:, gi], in0=wpool.ap[:, gi], scalar1=1.0 / w, scalar2=None,
                                                               op0=ALU.mult), reads=wpool.keys, writes=wpos.keys)
        op("vector", lambda g: g.tensor_scalar(out=wpool.ap, in0=wpool.ap, scalar1=-1.0, scalar2=None, op0=ALU.mult),
           reads=wpool.keys + wpos.keys, writes=wpool.keys)
        for gi, w in enumerate(POOL_W):
            op("vector", lambda g, gi=gi: g.memset(inv.ap[:, gi, :], 1.0), writes=inv.keys)
            for t in range(w - 1):
                op("vector", lambda g, gi=gi, t=t, w=w: g.memset(inv.ap[:, gi, t:t + 1], float(w) / (t + 1)), writes=inv.keys)
        for hf in range(2):
            fw.dma("sync", sp[hf].ap[0:120, :], spool[hf * 120:(hf + 1) * 120, :], writes=sp[hf].keys)
            for half in range(2):
                b = fw.bank()
                for cc in range(4):
                    c = half * 4 + cc
                    op("tensor", lambda t, b=b, cc=cc, c=c, hf=hf: t.transpose(
                        psb[b][:, cc * 120:(cc + 1) * 120], sp[hf].ap[0:120, c * 128:(c + 1) * 128], ident[0:120, 0:120]),
                       reads=sp[hf].keys + ["ident"], writes=[pk(b)])
                for cc in range(4):
                    c = half * 4 + cc
                    evac_copy(hs.ap[:, c, hf * 8:(hf + 1) * 8, 0:15],
                              psb[b][:, cc * 120:(cc + 1) * 120].rearrange("p (s t) -> p s t", s=8),
                              reads=[pk(b)], writes=hs.keys)
        fw.dma("sync", pool_s.rearrange("(s r) d -> s r d", r=15)[:, 0:7, :],
               spool.rearrange("(s r) d -> s r d", r=15)[:, 8:15, :])
        def pool_norm_tb(tbi):
            t0, n = TBS[tbi]
            box = []
            if tbi < 4:
                g_ = norm_gen(tbi, "g_mix0", out=box)
            else:
                g_ = norm_gen(tbi, "g_mix0", dst_fn=lambda c, rstd: (
                    hs.ap[:, c, :, 15:23], x[:, c, t0:t0 + n].rearrange("p (s t) -> p s t", s=NSEQ),
                    rstd.ap[:, 0:n].rearrange("p (s t) -> p s t", s=NSEQ), hs.keys), out=box)
            next(g_)
            yield
            for _ in g_:
                pass
            rstd = box[0]
            if tbi == 3:
                for c in range(NCH):
                    op("vector", lambda v, c=c, rstd=rstd: v.scalar_tensor_tensor(
                        out=h32t.ap[:, c, :], in0=x[:, c, TP - 15:TP], scalar=gs("g_mix0", c), in1=rstd.ap[:, 512 - 15:512],
                        op0=ALU.mult, op1=ALU.mult), reads=[xk(c, 3), "gvec"] + rstd.keys, writes=h32t.keys)
                for half in range(2):
                    b = fw.bank()
                    for cc in range(4):
                        c = half * 4 + cc
                        op("tensor", lambda t, b=b, cc=cc, c=c: t.transpose(psb[b][0:15, cc * 128:(cc + 1) * 128],
                                                                         h32t.ap[:, c, :], ident),
                           reads=h32t.keys + ["ident"], writes=[pk(b)])
                    evac_copy(ot.ap[0:15, half * 512:(half + 1) * 512], psb[b][0:15, :], reads=[pk(b)], writes=ot.keys)
                fw.dma("sync", pool_p, ot.ap[0:15, :], reads=ot.keys)
            if tbi == 4:
                for c in range(NCH):
                    op("vector", lambda v, c=c, rstd=rstd: v.scalar_tensor_tensor(
                        out=h32s.ap[:, c, :], in0=x[:, c, TP:NT], scalar=gs("g_mix0", c), in1=rstd.ap[:, 0:128],
                        op0=ALU.mult, op1=ALU.mult), reads=[xk(c, 4), "gvec"] + rstd.keys, writes=h32s.keys)
                for half in range(2):
                    b = fw.bank()
                    for cc in range(4):
                        c = half * 4 + cc
                        op("tensor", lambda t, b=b, cc=cc, c=c: t.transpose(psb[b][:, cc * 128:(cc + 1) * 128],
                                                                         h32s.ap[:, c, :], ident),
                           reads=h32s.keys + ["ident"], writes=[pk(b)])
                    evac_copy(ot2.ap[:, half * 512:(half + 1) * 512], psb[b], reads=[pk(b)], writes=ot2.keys)
                for s in range(NSEQ):
                    fw.dma("sync", pool_s[s * 15 + 7:s * 15 + 15, :], ot2.ap[s * 8:(s + 1) * 8, :], reads=ot2.keys)
        pctr_ = [0]

        def pool_mm_tb(tbi, pending=None):
            t0, n = TBS[tbi]
            pctr = pctr_[0]
            for gi, w in enumerate(POOL_W):
                for oc in range(2):
                    c = 2 * gi + oc
                    if pending is not None and c == 3:
                        for _ in pending:
                            pass
                    if tbi < 4:
                        rk = [hk(2 * gi + k, tbi) for k in range(2)] + ([hk(2 * gi + k, tbi - 1) for k in range(2)] if tbi else ["hpad"])
                        rhs = lambda k, d: hb[:, 2 * gi + k, HPAD + t0 - d:HPAD + t0 - d + n]
                    else:
                        rk = hs.keys
                        rhs = lambda k, d: hs.ap[:, 2 * gi + k, :, 15 - d:23 - d]
                    p1 = fw.bank()
                    po1 = psb[p1][:, 0:n] if tbi < 4 else psb[p1][:, 0:n].rearrange("p (s t) -> p s t", s=NSEQ)
                    fused = tbi > 0
                    nmm = 2 * w + (2 if fused else 0)
                    i = 0
                    for d in range(w):
                        for k in range(2):
                            op("tensor", lambda t, k=k, d=d, i=i, po1=po1, rhs=rhs: t.matmul(
                                po1, lhsT=wpos.ap[:, gi, k, oc * 128:(oc + 1) * 128], rhs=rhs(k, d),
                                start=(i == 0), stop=(i == nmm - 1)), reads=wpos.keys + rk, writes=[pk(p1)])
                            i += 1
                    pb_ = pctr % 2
                    pctr += 1
                    t2 = t2b[pb_]
                    if fused:
                        for k in range(2):
                            op("tensor", lambda t, k=k, i=i, po1=po1, rhs=rhs: t.matmul(
                                po1, lhsT=wpool.ap[:, gi, k, oc * 128:(oc + 1) * 128], rhs=rhs(k, 0),
                                start=False, stop=(i == nmm - 1)), reads=wpool.keys + rk, writes=[pk(p1)])
                            i += 1
                        op("vector", lambda v, p1=p1, c=c: v.scalar_tensor_tensor(
                            out=x[:, c, t0:t0 + n], in0=psb[p1][:, 0:n], scalar=gs("pool_scale", c), in1=x[:, c, t0:t0 + n],
                            op0=ALU.mult, op1=ALU.add), reads=[pk(p1), xk(c, tbi), "gvec"], writes=[xk(c, tbi)])
                        continue
                    p2 = fw.bank()
                    po2 = psb[p2][:, 0:n]
                    for k in range(2):
                        op("tensor", lambda t, k=k, po2=po2, rhs=rhs: t.matmul(
                            po2, lhsT=wpool.ap[:, gi, k, oc * 128:(oc + 1) * 128], rhs=rhs(k, 0),
                            start=(k == 0), stop=(k == 1)), reads=wpool.keys + rk, writes=[pk(p2)])
                    s2 = p2s[pb_]
                    tf = av(67 * K1, 64, F32)
                    op("scalar", lambda s, p2=p2, s2=s2: s.copy(out=s2.ap[:, 0:n], in_=psb[p2][:, 0:n]),
                       reads=[pk(p2)], writes=s2.keys)
                    op("vector", lambda v, p1=p1, t2=t2, s2=s2: v.tensor_tensor(
                        out=t2.ap[:, 0:n], in0=psb[p1][:, 0:n], in1=s2.ap[:, 0:n], op=ALU.add),
                       reads=[pk(p1)] + s2.keys, writes=t2.keys)
                    op("vector", lambda v, p1=p1, tf=tf: v.tensor_tensor(
                        out=tf.ap[:, 0:16], in0=psb[p1][:, 0:16], in1=inv.ap[:, gi, :], op=ALU.mult),
                       reads=[pk(p1)] + inv.keys, writes=tf.keys)
                    op("vector", lambda v, t2=t2, s2=s2, tf=tf: v.tensor_tensor(
                        out=t2.ap[:, 0:16], in0=tf.ap[:, 0:16], in1=s2.ap[:, 0:16], op=ALU.add),
                       reads=tf.keys + s2.keys + t2.keys, writes=t2.keys)
                    op("vector", lambda v, t2=t2, c=c: v.scalar_tensor_tensor(
                        out=x[:, c, t0:t0 + n], in0=t2.ap[:, 0:n], scalar=gs("pool_scale", c), in1=x[:, c, t0:t0 + n],
                        op0=ALU.mult, op1=ALU.add), reads=t2.keys + [xk(c, tbi), "gvec"], writes=[xk(c, tbi)])
            pctr_[0] = pctr

        if pre_hook is not None:
            pre_hook(0)
        for _ in pool_norm_tb(0):
            pass
        for tbi in range(5):
            pending = None
            if tbi + 1 < 5:
                if pre_hook is not None:
                    pre_hook(tbi + 1)
                pending = pool_norm_tb(tbi + 1)
                next(pending)
            pool_mm_tb(tbi, pending)
            if pending is not None:
                for _ in pending:
                    pass
            if post_hook is not None:
                post_hook(tbi)

    def ssm_mixer():
        TWO_S = 1.5957691216057308
        MUL, ADD, SUB = ALU.mult, ALU.add, ALU.subtract
        G = av(0, 36 * K1, F32, "p (n c) -> p n c", n=NBX)
        Bw = av(36 * K1, 8 * K1, BF16, "p (r e m) -> p r e m", r=2, e=16)
        Cw = av(36 * K1, 10 * K1, BF16, "p (r e b m) -> p r e b m", r=2, e=16, b=5)
        CB = [0, 1, 2, 4]
        ZB = av(46 * K1, 256, BF16, "p (r m) -> p r m", r=2)
        Lw = av(47 * K1, 4 * K1, BF16, "p (t m) -> p t m", t=16)
        Sc = av(51 * K1, 2 * 4 * NBX * 2, BF16, "p (r j n) -> p r j n", r=2, j=4)
        Pr = av(54 * K1, 2176, F32, "p (j e) -> p j e", j=32)
        Pi = av(54 * K1 + 2176, 2176, F32, "p (j e) -> p j e", j=32)
        Bbr = av(59 * K1, 2 * K1, F32, "p (j h) -> p j h", j=32)
        Bbi = av(61 * K1, 2 * K1, F32, "p (j h) -> p j h", j=32)
        Cr = av(63 * K1, 2 * K1, F32, "p (j h) -> p j h", j=32)
        Ci = av(65 * K1, 2 * K1, F32, "p (j h) -> p j h", j=32)
        Bbm = av(67 * K1, 4 * K1, BF16, "p (j r m) -> p j r m", j=32, r=2)
        C0m = av(71 * K1, 4 * K1, BF16, "p (j r m) -> p j r m", j=32, r=2)
        XBm = av(75 * K1, 8 * K1, F32, "p (e j m) -> p e j m", e=16, j=4)
        tAf = av(83 * K1, 4 * K1, F32)
        tBf = av(87 * K1, 4 * K1, F32)
        H0 = av(93 * K1, 4 * K1, F32, "p (s c) -> p s c", s=NSEQ)

        def hbu(c):
            return hb[:, c + 1, HPAD:HW].rearrange("p (r n) -> p r n", r=LCH)

        def hbz(c):
            return hb[:, c, HPAD:HW].rearrange("p (r n) -> p r n", r=LCH)

        def u_keys(c):
            return [hk(c + 1, t_) for t_ in range(5)]

        def z_keys(c):
            return [hk(c, t_) for t_ in range(5)]

        def sm(i):
            return av(91 * K1 + i * 128, 128, F32)

        def vop(out, in0, in1, o, reads, writes, eng="vector"):
            op(eng, lambda v: v.tensor_tensor(out=out, in0=in0, in1=in1, op=o), reads=reads, writes=writes)

        def ssm_norm():
            for c in range(NCH):
                op("gpsimd", lambda g, c=c: g.memset(hbu(c)[:, 0:8, NB:NBX], 0.0), writes=u_keys(c))
            for tbi, (t0, n) in enumerate(TBS):
                if tbi < 4:
                    n0 = t0 // LCH
                    norm_to_hb(tbi, "g_mix1", dst_fn=lambda c, rstd: (
                        hbu(c)[:, :, n0:n0 + 32], x[:, c, t0:t0 + n].rearrange("p (n r) -> p r n", r=LCH),
                        rstd.ap[:, 0:n].rearrange("p (n r) -> p r n", r=LCH), u_keys(c)))
                else:
                    norm_to_hb(tbi, "g_mix1", dst_fn=lambda c, rstd: (
                        hbu(c)[:, 8:16, NB:NBX], x[:, c, t0:t0 + n].rearrange("p (s t) -> p t s", t=TSEQ),
                        rstd.ap[:, 0:n].rearrange("p (s t) -> p t s", t=TSEQ), u_keys(c)))

        lrow = [av(75 * K1 + i * 512, 512, F32) for i in range(2)]
        ldt = av(76 * K1, 64, F32)
        ldx = av(76 * K1 + 512, 512, F32)
        fw.dma("sync", lrow[0].ap[0:32, :], lam_re, writes=lrow[0].keys)
        fw.dma("sync", lrow[1].ap[0:32, :], lam_im, writes=lrow[1].keys)
        fw.dma("sync", ldt.ap[0:32, 0:2], log_dt, writes=ldt.keys)
        inj_t = [[av(83 * K1 + (ri * 4 + jb) * 512, 512, F32) for jb in range(4)] for ri in range(2)]
        for ri, csrc in enumerate([c_re, c_im]):
            for jb in range(4):
                for g_ in range(2):
                    fw.dma("sync", inj_t[ri][jb].ap[:, g_ * 64:(g_ + 1) * 64], csrc[jb * 8:(jb + 1) * 8, g_], writes=inj_t[ri][jb].keys)
        braw = [av(87 * K1, 2 * K1, F32, "p (j h) -> p j h", j=32), av(89 * K1, 2 * K1, F32, "p (j h) -> p j h", j=32)]
        fw.dma("sync", braw[0].ap, b_re.rearrange("j q h -> q j h"), writes=braw[0].keys)
        fw.dma("sync", braw[1].ap, b_im.rearrange("j q h -> q j h"), writes=braw[1].keys)
        hl_t = [[av(67 * K1 + (ri * 4 + sb) * 512, 512, F32) for sb in range(4)] for ri in range(2)]
        for ri, hsrc in enumerate([sre, sim]):
            for sb in range(4):
                fw.dma("sync", hl_t[ri][sb].ap, hsrc[sb * 128:(sb + 1) * 128, :], writes=hl_t[ri][sb].keys)
        yield
        op("vector", lambda v: v.tensor_copy(out=ldx.ap[0:32, :].rearrange("j (g p) -> j g p", g=2),
                                             in_=ldt.ap[0:32, 0:2].unsqueeze(2).broadcast_to([32, 2, 64])),
           reads=ldt.keys, writes=ldx.keys)
        b = fw.bank()
        for i, s_ in enumerate([lrow[0], lrow[1], ldx]):
            op("tensor", lambda t, i=i, s_=s_: t.transpose(psb[b][:, i * 32:(i + 1) * 32], s_.ap[0:32, :], ident[0:32, 0:32]),
               reads=s_.keys + ["ident"], writes=[pk(b)])
        lr, li, dt = sm(0), sm(1), sm(2)
        op("vector", lambda v: v.tensor_copy(out=lr.ap, in_=psb[b][:, 0:32]), reads=[pk(b)], writes=lr.keys)
        op("vector", lambda v: v.tensor_copy(out=li.ap, in_=psb[b][:, 32:64]), reads=[pk(b)], writes=li.keys)
        op("scalar", lambda s: s.activation(out=dt.ap, in_=psb[b][:, 64:96], func=AF.Exp), reads=[pk(b)], writes=dt.keys)
        ar, ai, mg, cs, sn, zr, zi = sm(3), sm(4), sm(5), sm(6), sm(7), sm(8), sm(9)
        t1, t2, t3, t4, t5 = sm(10), sm(11), sm(12), sm(13), sm(14)
        vop(ar.ap, lr.ap, dt.ap, MUL, lr.keys, ar.keys)
        vop(ai.ap, li.ap, dt.ap, MUL, li.keys, ai.keys)
        op("scalar", lambda s: s.activation(out=mg.ap, in_=ar.ap, func=AF.Exp, scale=1.0 / 16), reads=ar.keys, writes=mg.keys)
        op("scalar", lambda s: s.activation(out=cs.ap, in_=ai.ap, func=AF.Sin, scale=1.0 / 32, bias=0.0),
           reads=ai.keys, writes=cs.keys)
        op("vector", lambda v: v.tensor_tensor(out=cs.ap, in0=cs.ap, in1=cs.ap, op=MUL), reads=cs.keys, writes=cs.keys)
        op("vector", lambda v: v.tensor_scalar(out=cs.ap, in0=cs.ap, scalar1=-2.0, scalar2=1.0, op0=MUL, op1=ADD),
           reads=cs.keys, writes=cs.keys)
        op("scalar", lambda s: s.activation(out=sn.ap, in_=ai.ap, func=AF.Sin, scale=1.0 / 16, bias=0.0),
           reads=ai.keys, writes=sn.keys)
        vop(zr.ap, mg.ap, cs.ap, MUL, mg.keys, zr.keys)
        vop(zi.ap, mg.ap, sn.ap, MUL, mg.keys, zi.keys)
        sk = sm(0).keys + sm(15).keys
        for _ in range(4):
            vop(t1.ap, zr.ap, zr.ap, MUL, sk, sk)
            vop(t2.ap, zi.ap, zi.ap, MUL, sk, sk)
            vop(t3.ap, zr.ap, zi.ap, MUL, sk, sk)
            vop(zr.ap, t1.ap, t2.ap, SUB, sk, sk)
            vop(zi.ap, t3.ap, t3.ap, ADD, sk, sk)
        lbr, lbi = zr, zi
        vop(t1.ap, lr.ap, lr.ap, MUL, sk, sk)
        vop(t2.ap, li.ap, li.ap, MUL, sk, sk)
        vop(t1.ap, t1.ap, t2.ap, ADD, sk, sk)
        op("vector", lambda v: v.reciprocal(out=t1.ap, in_=t1.ap), reads=sk, writes=sk)
        op("vector", lambda v: v.tensor_scalar(out=t2.ap, in0=lbr.ap, scalar1=-1.0, scalar2=None, op0=ADD), reads=sk, writes=sk)
        vop(t3.ap, t2.ap, lr.ap, MUL, sk, sk)
        vop(t4.ap, lbi.ap, li.ap, MUL, sk, sk)
        vop(t3.ap, t3.ap, t4.ap, ADD, sk, sk)
        vop(t3.ap, t3.ap, t1.ap, MUL, sk, sk)
        vop(t4.ap, lbi.ap, lr.ap, MUL, sk, sk)
        vop(t5.ap, t2.ap, li.ap, MUL, sk, sk)
        vop(t4.ap, t4.ap, t5.ap, SUB, sk, sk)
        vop(t4.ap, t4.ap, t1.ap, MUL, sk, sk)
        fre, fim = t3, t4
        pk_ = Pr.keys + Pi.keys
        op("vector", lambda v: v.memset(Pr.ap[:, :, 0:1], 1.0), writes=pk_)
        op("vector", lambda v: v.memset(Pi.ap[:, :, 0:1], 0.0), writes=pk_)
        op("vector", lambda v: v.tensor_copy(out=Pr.ap[:, :, 1:2], in_=lbr.ap.unsqueeze(2)), reads=sk, writes=pk_)
        op("vector", lambda v: v.tensor_copy(out=Pi.ap[:, :, 1:2], in_=lbi.ap.unsqueeze(2)), reads=sk, writes=pk_)
        tk = tAf.keys + tBf.keys
        pta, ptb = av(79 * K1, 1 * K1, F32), av(80 * K1, 1 * K1, F32)
        ptk = pta.keys + ptb.keys
        m = 1
        while m < 16:
            ta = pta.ap[:, 0:32 * m].rearrange("p (j e) -> p j e", j=32)
            tb_ = ptb.ap[:, 0:32 * m].rearrange("p (j e) -> p j e", j=32)
            prs, pis = Pr.ap[:, :, 1:m + 1], Pi.ap[:, :, 1:m + 1]
            prm = Pr.ap[:, :, m:m + 1].broadcast_to([128, 32, m])
            pim = Pi.ap[:, :, m:m + 1].broadcast_to([128, 32, m])
            vop(ta, prs, prm, MUL, pk_, ptk)
            vop(tb_, pis, pim, MUL, pk_, ptk)
            vop(Pr.ap[:, :, m + 1:2 * m + 1], ta, tb_, SUB, ptk + pk_, pk_)
            vop(ta, prs, pim, MUL, pk_, ptk)
            vop(tb_, pis, prm, MUL, pk_, ptk)
            vop(Pi.ap[:, :, m + 1:2 * m + 1], ta, tb_, ADD, ptk + pk_, pk_)
            m *= 2
        for ri, (csrc, cdst) in enumerate([(c_re, Cr), (c_im, Ci)]):
            b = fw.bank()
            for jb in range(4):
                inj = inj_t[ri][jb]
                op("tensor", lambda t, jb=jb, inj=inj, b=b: t.transpose(psb[b][:, jb * 128:(jb + 1) * 128], inj.ap, ident),
                   reads=inj.keys + ["ident"], writes=[pk(b)])
            op("vector", lambda v, b=b, cdst=cdst, ri=ri: v.tensor_scalar(
                out=cdst.ap, in0=psb[b].rearrange("p (j h) -> p j h", j=32), scalar1=(1.0 if ri == 0 else -1.0), scalar2=None,
                op0=MUL), reads=[pk(b)], writes=cdst.keys)
        ua = av(77 * K1, 2 * K1, F32, "p (j h) -> p j h", j=32)
        ub = av(79 * K1, 2 * K1, F32, "p (j h) -> p j h", j=32)
        freb = fre.ap.unsqueeze(2).broadcast_to([128, 32, 16])
        fimb = fim.ap.unsqueeze(2).broadcast_to([128, 32, 16])
        bk = braw[0].keys + braw[1].keys + sk
        uk = ua.keys + ub.keys
        vop(ua.ap, braw[0].ap, freb, MUL, bk, uk)
        vop(ub.ap, braw[1].ap, fimb, MUL, bk, uk)
        vop(Bbr.ap, ua.ap, ub.ap, SUB, uk, Bbr.keys)
        vop(ua.ap, braw[1].ap, freb, MUL, bk, uk)
        vop(ub.ap, braw[0].ap, fimb, MUL, bk, uk)
        vop(Bbi.ap, ua.ap, ub.ap, ADD, uk, Bbi.keys)
        def build_masked():
            op("gpsimd", lambda g: g.memset(Bbm.ap, 0.0), writes=Bbm.keys)
            op("gpsimd", lambda g: g.memset(C0m.ap, 0.0), writes=C0m.keys)
            for hf in range(2):
                ps_, cs_ = slice(hf * 64, (hf + 1) * 64), slice(hf * 16, (hf + 1) * 16)
                op("vector", lambda v, ps_=ps_, cs_=cs_: v.tensor_copy(out=Bbm.ap[ps_, :, 0, cs_], in_=Bbr.ap[ps_]),
                   reads=Bbr.keys, writes=Bbm.keys)
                op("vector", lambda v, ps_=ps_, cs_=cs_: v.tensor_copy(out=Bbm.ap[ps_, :, 1, cs_], in_=Bbi.ap[ps_]),
                   reads=Bbi.keys, writes=Bbm.keys)
                op("vector", lambda v, ps_=ps_, cs_=cs_: v.tensor_copy(out=C0m.ap[ps_, :, 0, cs_], in_=Cr.ap[ps_]),
                   reads=Cr.keys, writes=C0m.keys)
                op("vector", lambda v, ps_=ps_, cs_=cs_: v.tensor_copy(out=C0m.ap[ps_, :, 1, cs_], in_=Ci.ap[ps_]),
                   reads=Ci.keys, writes=C0m.keys)
        for ri, hsrc in enumerate([sre, sim]):
            b = fw.bank()
            for sb in range(4):
                hl = hl_t[ri][sb]
                op("tensor", lambda t, sb=sb, hl=hl, b=b: t.transpose(psb[b][:, sb * 128:(sb + 1) * 128], hl.ap, ident),
                   reads=hl.keys + ["ident"], writes=[pk(b)])
            op("vector", lambda v, b=b, ri=ri: v.tensor_copy(out=H0.ap[:, :, ri * 32:(ri + 1) * 32],
                                                             in_=psb[b].rearrange("p (s j) -> p s j", s=NSEQ)),
               reads=[pk(b)], writes=H0.keys)
        ssm_norm()
        XBmb = [XBm, av(67 * K1, 8 * K1, F32, "p (e j m) -> p e j m", e=16, j=4)]
        Bwb = [Bw, av(44 * K1, 8 * K1, BF16, "p (r e m) -> p r e m", r=2, e=16)]
        for X_ in XBmb:
            op("gpsimd", lambda g, X_=X_: g.memset(X_.ap, 0.0), writes=X_.keys)
        tA4 = tAf.ap.rearrange("p (a b h) -> p a b h", a=16, b=4)
        tB4 = tBf.ap.rearrange("p (a b h) -> p a b h", a=16, b=4)

        def bgen(c, ri):
            j0 = 4 * c
            X_ = XBmb[ri]
            prb = Pr.ap[:, j0:j0 + 4, 0:16].rearrange("p j e -> p e j").unsqueeze(3).broadcast_to([128, 16, 4, 16])
            pib = Pi.ap[:, j0:j0 + 4, 0:16].rearrange("p j e -> p e j").unsqueeze(3).broadcast_to([128, 16, 4, 16])
            bbr = Bbr.ap[:, j0:j0 + 4, :].unsqueeze(1).broadcast_to([128, 16, 4, 16])
            bbi = Bbi.ap[:, j0:j0 + 4, :].unsqueeze(1).broadcast_to([128, 16, 4, 16])
            rk_ = pk_ + Bbr.keys + Bbi.keys
            if ri == 0:
                vop(tA4, prb, bbr, MUL, rk_, tAf.keys)
                vop(tB4, pib, bbi, MUL, rk_, tBf.keys, eng="gpsimd")
                o_ = SUB
            else:
                vop(tA4, prb, bbi, MUL, rk_, tAf.keys)
                vop(tB4, pib, bbr, MUL, rk_, tBf.keys, eng="gpsimd")
                o_ = ADD
            for hf in range(2):
                ps_, cs_ = slice(hf * 64, (hf + 1) * 64), slice(hf * 16, (hf + 1) * 16)
                vop(X_.ap[ps_, :, :, cs_], tA4[ps_], tB4[ps_], o_, tk, X_.keys)

        def bbanks(c):
            return [0, 1, 2, 3] if c % 2 == 0 else [4, 5, 6, 7]

        def btrans(c, ri):
            X_, W_ = XBmb[ri], Bwb[c % 2]
            for q4 in range(4):
                b = bbanks(c)[q4]
                for ee in range(4):
                    e = q4 * 4 + ee
                    op("tensor", lambda t, b=b, ee=ee, e=e: t.transpose(
                        psb[b][:, ee * 128:(ee + 1) * 128], X_.ap[:, e].rearrange("p j m -> p (j m)"), ident),
                       reads=X_.keys + ["ident"], writes=[pk(b)])
                evac_copy(W_.ap[:, ri, q4 * 4:(q4 + 1) * 4, :], psb[b].rearrange("p (e m) -> p e m", e=4),
                          reads=[pk(b)], writes=W_.keys, eng="scalar")

        def bside_mm(c, jjs):
            W_, uv, hr, bk4 = Bwb[c % 2], hbu(c), u_keys(c), bbanks(c)
            for ri in range(2):
                for k in range(LCH):
                    for jj in jjs:
                        rs = slice(32 * jj, 32 * jj + 32) if jj < 3 else slice(64, 128)
                        pb = bk4[jj]
                        op("tensor", lambda t, k=k, pb=pb, ri=ri, rs=rs: t.matmul(
                            psb[pb][:, ri * NBX:(ri + 1) * NBX], lhsT=W_.ap[rs, ri, 15 - k, :],
                            rhs=uv[rs, k, :], start=(k == 0), stop=(k == LCH - 1)),
                           reads=W_.keys + hr, writes=[pk(pb)])

        for ri in range(2):
            bgen(0, ri)
            btrans(0, ri)
        for c in range(NCH):
            j0 = 4 * c
            if c + 1 < NCH:
                bgen(c + 1, 0)
                bgen(c + 1, 1)
            bside_mm(c, [0, 1, 2])
            op("vector", lambda v: v.memset(Bwb[c % 2].ap[64:96, 0], 0.0), reads=[], writes=Bwb[c % 2].keys)
            op("gpsimd", lambda g: g.memset(Bwb[c % 2].ap[64:96, 1], 0.0), reads=[], writes=Bwb[c % 2].keys)
            if c + 1 < NCH:
                btrans(c + 1, 0)
                btrans(c + 1, 1)
            bside_mm(c, [3])
            bk4 = bbanks(c)
            for jj in range(4):
                evac_copy(G.ap[:, :, j0 + jj:64:32], psb[bk4[jj]][:, 0:2 * NBX].rearrange("p (r n) -> p n r", r=2),
                          reads=[pk(bk4[jj])], writes=G.keys, eng="scalar")

        Cwb = [Cw, av(18 * K1, 10 * K1, BF16, "p (r e b m) -> p r e b m", r=2, e=16, b=5)]
        Lwb = [Lw, av(59 * K1, 4 * K1, BF16, "p (t m) -> p t m", t=16)]
        Scb = [Sc, av(28 * K1, 2 * 4 * NBX * 2, BF16, "p (r j n) -> p r j n", r=2, j=4)]
        ZBb = [ZB, av(31 * K1, 256, BF16, "p (r m) -> p r m", r=2)]
        build_masked()
        for t_ in [Cwb[0], Lwb[0], Lwb[1], Scb[0], ZBb[0]]:
            op("gpsimd", lambda g, t_=t_: g.memset(t_.ap, 0.0), reads=Bbm.keys + C0m.keys, writes=t_.keys)
        fa = av(87 * K1, 2 * K1, F32, "p (s j) -> p s j", s=NSEQ)
        fb = av(89 * K1, 2 * K1, F32, "p (s j) -> p s j", s=NSEQ)
        fk = fa.keys + fb.keys
        H0r, H0i = H0.ap[:, :, 0:32], H0.ap[:, :, 32:64]
        q1, q2, q3 = sm(10), sm(11), sm(12)
        vop(q1.ap, Pr.ap[:, :, 8], Pr.ap[:, :, 8], MUL, pk_, sk)
        vop(q2.ap, Pi.ap[:, :, 8], Pi.ap[:, :, 8], MUL, pk_, sk)
        vop(q1.ap, q1.ap, q2.ap, ADD, sk, sk)
        op("vector", lambda v: v.reciprocal(out=q1.ap, in_=q1.ap), reads=sk, writes=sk)
        vop(q2.ap, Pr.ap[:, :, 8], q1.ap, MUL, pk_ + sk, sk)
        op("vector", lambda v: v.scalar_tensor_tensor(out=q3.ap, in0=Pi.ap[:, :, 8], scalar=-1.0, in1=q1.ap, op0=MUL, op1=MUL),
           reads=pk_ + sk, writes=sk)
        m8r = q2.ap.unsqueeze(1).broadcast_to([128, NSEQ, 32])
        m8i = q3.ap.unsqueeze(1).broadcast_to([128, NSEQ, 32])
        fc = av(85 * K1, 2 * K1, F32, "p (s j) -> p s j", s=NSEQ)
        vop(fa.ap, H0r, m8r, MUL, H0.keys + sk, fk)
        vop(fb.ap, H0i, m8i, MUL, H0.keys + sk, fk)
        vop(fc.ap, fa.ap, fb.ap, SUB, fk, fc.keys)
        vop(fa.ap, H0i, m8r, MUL, H0.keys + sk, fk)
        vop(fb.ap, H0r, m8i, MUL, H0.keys + sk, fk)
        vop(H0i, fa.ap, fb.ap, ADD, fk, H0.keys)
        op("vector", lambda v: v.tensor_copy(out=H0r, in_=fc.ap), reads=fc.keys, writes=H0.keys)

        A1 = av(91 * K1, 256, F32); A2 = av(91 * K1 + 256, 256, F32)
        A1q = av(91 * K1 + 512, 256, F32); A2q = av(91 * K1 + 768, 256, F32)
        s1 = av(92 * K1, 256, F32); s2 = av(92 * K1 + 256, 256, F32)
        QTr = av(75 * K1, 1 * K1, F32, "p (k j) -> p k j", k=8)
        QTi = av(76 * K1, 1 * K1, F32, "p (k j) -> p k j", k=8)
        qa = av(77 * K1, 512, F32); qb = av(77 * K1 + 512, 512, F32)
        qk = QTr.keys + QTi.keys + qa.keys
        ak = A1.keys + s1.keys
        op("vector", lambda v: v.tensor_copy(out=QTr.ap[:, 0, :], in_=Pr.ap[:, :, 16]), reads=pk_, writes=qk)
        op("vector", lambda v: v.tensor_copy(out=QTi.ap[:, 0, :], in_=Pi.ap[:, :, 16]), reads=pk_, writes=qk)
        m = 1
        while m < 8:
            ta = qa.ap[:, 0:32 * m].rearrange("p (k j) -> p k j", j=32)
            tb_ = qb.ap[:, 0:32 * m].rearrange("p (k j) -> p k j", j=32)
            qrs, qis = QTr.ap[:, 0:m, :], QTi.ap[:, 0:m, :]
            qrm = QTr.ap[:, m - 1:m, :].broadcast_to([128, m, 32])
            qim = QTi.ap[:, m - 1:m, :].broadcast_to([128, m, 32])
            vop(ta, qrs, qrm, MUL, qk, qk)
            vop(tb_, qis, qim, MUL, qk, qk)
            vop(QTr.ap[:, m:2 * m, :], ta, tb_, SUB, qk, qk)
            vop(ta, qrs, qim, MUL, qk, qk)
            vop(tb_, qis, qrm, MUL, qk, qk)
            vop(QTi.ap[:, m:2 * m, :], ta, tb_, ADD, qk, qk)
            m *= 2
        for (A1_, A2_, kq) in ((A1, A2, 0), (A1q, A2q, 7)):
            for hf in range(2):
                op("vector", lambda v, hf=hf, A1_=A1_, kq=kq: v.tensor_copy(out=A1_.ap[:, hf * 32:(hf + 1) * 32], in_=QTr.ap[:, kq, :]),
                   reads=qk, writes=ak)
            op("vector", lambda v, A2_=A2_, kq=kq: v.tensor_scalar(out=A2_.ap[:, 0:32], in0=QTi.ap[:, kq, :], scalar1=-1.0, scalar2=None,
                                                                  op0=MUL), reads=qk, writes=ak)
            op("vector", lambda v, A2_=A2_, kq=kq: v.tensor_copy(out=A2_.ap[:, 32:64], in_=QTi.ap[:, kq, :]), reads=qk, writes=ak)
        NBLK, BL = 16, 8
        Gv = G.ap[:, 0:NB, :].rearrange("p (a b) c -> p a b c", b=BL)
        S1 = tAf.ap.rearrange("p (a c) -> p a c", a=NBLK)
        S2 = tBf.ap.rearrange("p (a c) -> p a c", a=NBLK)
        A1b = A1.ap.unsqueeze(1).broadcast_to([128, NBLK, 64])
        A2lo = A2.ap[:, 0:32].unsqueeze(1).broadcast_to([128, NBLK, 32])
        A2hi = A2.ap[:, 32:64].unsqueeze(1).broadcast_to([128, NBLK, 32])
        sk2 = tAf.keys + tBf.keys
        for b_ in range(1, BL):
            rd = G.keys + ak if b_ == 1 else ["gscan"] + ak
            src_, dst_ = Gv[:, :, b_ - 1, :], Gv[:, :, b_, :]
            vop(S1, A1b, src_, MUL, rd, sk2)
            vop(S2[:, :, 0:32], A2lo, src_[:, :, 32:64], MUL, rd, sk2)
            vop(S2[:, :, 32:64], A2hi, src_[:, :, 0:32], MUL, rd, sk2)
            vop(dst_, dst_, S1, ADD, sk2 + ["gscan"], ["gscan"])
            vop(dst_, dst_, S2, ADD, sk2 + ["gscan"], ["gscan"])
        for a_ in range(1, NBLK):
            prev, cur = Gv[:, a_ - 1, BL - 1, :], Gv[:, a_, BL - 1, :]
            vop(s1.ap, A1q.ap, prev, MUL, ["gscan"] + ak, ak)
            vop(s2.ap[:, 0:32], A2q.ap[:, 0:32], prev[:, 32:64], MUL, ["gscan"] + ak, ak)
            vop(s2.ap[:, 32:64], A2q.ap[:, 32:64], prev[:, 0:32], MUL, ["gscan"] + ak, ak)
            vop(cur, cur, s1.ap, ADD, ak + ["gscan"], ["gscan"])
            vop(cur, cur, s2.ap, ADD, ak + ["gscan"], ["gscan"])
        for a0 in range(1, NBLK, 4):
            na = min(4, NBLK - a0)
            shp = [128, na, BL - 1, 32]
            cre = Gv[:, a0 - 1:a0 - 1 + na, BL - 1, 0:32].unsqueeze(2).broadcast_to(shp)
            cim = Gv[:, a0 - 1:a0 - 1 + na, BL - 1, 32:64].unsqueeze(2).broadcast_to(shp)
            qr_ = QTr.ap[:, 0:BL - 1, :].unsqueeze(1).broadcast_to(shp)
            qi_ = QTi.ap[:, 0:BL - 1, :].unsqueeze(1).broadcast_to(shp)
            dre = Gv[:, a0:a0 + na, 0:BL - 1, 0:32]
            dim_ = Gv[:, a0:a0 + na, 0:BL - 1, 32:64]
            tre = tAf.ap[:, 0:na * 7 * 32].rearrange("p (a b j) -> p a b j", a=na, b=BL - 1)
            tim = tBf.ap[:, 0:na * 7 * 32].rearrange("p (a b j) -> p a b j", a=na, b=BL - 1)
            last = (a0 + 4 >= NBLK)
            wre = (G.keys if last else []) + ["gscan_re"]
            wim = (G.keys if last else []) + ["gscan_im"]
            vop(tre, qr_, cre, MUL, ["gscan"] + qk, tAf.keys)
            vop(dre, dre, tre, ADD, tAf.keys + ["gscan", "gscan_re"], ["gscan_re"])
            vop(tre, qi_, cim, MUL, ["gscan"] + qk, tAf.keys)
            vop(dre, dre, tre, SUB, tAf.keys + ["gscan", "gscan_re"], wre)
            ie = "gpsimd" if a0 < 9 else "vector"
            vop(tim, qr_, cim, MUL, ["gscan"] + qk, tBf.keys, eng=ie)
            vop(dim_, dim_, tim, ADD, tBf.keys + ["gscan", "gscan_im"], ["gscan_im"], eng=ie)
            vop(tim, qi_, cre, MUL, ["gscan"] + qk, tBf.keys, eng=ie)
            vop(dim_, dim_, tim, ADD, tBf.keys + ["gscan", "gscan_im"], wim, eng=ie)
        Sall = av(0, 18 * K1, BF16, "p (n c) -> p n c", n=NBX)
        stmp = av(83 * K1, 9 * 64 * 2, BF16, "p (n c) -> p n c", n=9)
        op("vector", lambda v: v.tensor_copy(out=stmp.ap, in_=G.ap[:, 0:9, :]), reads=G.keys, writes=stmp.keys)
        op("vector", lambda v: v.tensor_copy(out=Sall.ap[:, 0:9, :], in_=stmp.ap), reads=stmp.keys, writes=G.keys)
        for (a_, b_) in ((9, 18), (18, 36), (36, 72), (72, 144)):
            op("vector", lambda v, a_=a_, b_=b_: v.tensor_copy(out=Sall.ap[:, a_:b_, :], in_=G.ap[:, a_:b_, :]),
               reads=G.keys, writes=G.keys)
        for t_ in [Cwb[1], Scb[1], ZBb[1]]:
            op("gpsimd", lambda g, t_=t_: g.memset(t_.ap, 0.0), writes=t_.keys)
        GT = [[av(75 * K1, 2 * K1, F32), av(77 * K1, 2 * K1, F32)], [av(79 * K1, 2 * K1, F32), av(81 * K1, 2 * K1, F32)]]
        gctr2 = [0]

        def gelu_bank(pbank, width, dst, in_view, wkeys):
            g1, g2 = GT[gctr2[0] % 2]
            gctr2[0] += 1
            ps_ = psb[pbank][:, 0:width]
            op("scalar", lambda s: s.activation(out=g1.ap[:, 0:width], in_=ps_, func=AF.Square), reads=[pk(pbank)], writes=g1.keys)
            op("vector", lambda v: v.tensor_scalar(out=g1.ap[:, 0:width], in0=g1.ap[:, 0:width], scalar1=0.044715, scalar2=1.0,
                                                   op0=MUL, op1=ADD), reads=g1.keys, writes=g1.keys)
            op("vector", lambda v: v.tensor_tensor(out=g2.ap[:, 0:width], in0=ps_, in1=g1.ap[:, 0:width], op=MUL),
               reads=[pk(pbank)] + g1.keys, writes=g2.keys)
            op("scalar", lambda s: s.activation(out=g2.ap[:, 0:width], in_=g2.ap[:, 0:width], func=AF.Sigmoid, scale=TWO_S),
               reads=g2.keys, writes=g2.keys)
            op("vector", lambda v: v.tensor_tensor(out=dst, in0=in_view(ps_), in1=in_view(g2.ap[:, 0:width]), op=MUL),
               reads=[pk(pbank)] + g2.keys, writes=wkeys)

        tA5 = tAf.ap.rearrange("p (a b h) -> p a b h", a=4, b=16)
        tB5 = tBf.ap.rearrange("p (a b h) -> p a b h", a=4, b=16)

        def cgen(c):
            j0, Cw_ = 4 * c, Cwb[c % 2]
            prb = Pr.ap[:, j0:j0 + 4, 1:17].unsqueeze(3).broadcast_to([128, 4, 16, 16])
            pib = Pi.ap[:, j0:j0 + 4, 1:17].unsqueeze(3).broadcast_to([128, 4, 16, 16])
            crb = Cr.ap[:, j0:j0 + 4, :].unsqueeze(2).broadcast_to([128, 4, 16, 16])
            cib = Ci.ap[:, j0:j0 + 4, :].unsqueeze(2).broadcast_to([128, 4, 16, 16])
            rk_ = pk_ + Cr.keys + Ci.keys

            def cw_write(ri, o2):
                for hf in range(2):
                    ps_, cs_ = slice(hf * 64, (hf + 1) * 64), slice(hf * 16, (hf + 1) * 16)
                    for (jsl, bsl) in ((slice(0, 3), slice(0, 3)), (slice(3, 4), slice(4, 5))):
                        o_ = Cw_.ap[ps_, ri, :, bsl, cs_]
                        a_ = tA5[ps_, jsl].rearrange("p j r h -> p r j h")
                        b_ = tB5[ps_, jsl].rearrange("p j r h -> p r j h")
                        vop(o_, a_, b_, o2, tk, Cw_.keys, eng="gpsimd")

            vop(tA5, prb, crb, MUL, rk_, tAf.keys, eng="gpsimd")
            vop(tB5, pib, cib, MUL, rk_, tBf.keys, eng="gpsimd")
            cw_write(0, ADD)
            vop(tA5, prb, cib, MUL, rk_, tAf.keys, eng="gpsimd")
            vop(tB5, pib, crb, MUL, rk_, tBf.keys, eng="gpsimd")
            cw_write(1, SUB)

        def clags(c):
            j0, Cw_, Lw_, ZB_, Sc_ = 4 * c, Cwb[c % 2], Lwb[c % 2], ZBb[c % 2], Scb[c % 2]
            op("vector", lambda v: v.tensor_copy(out=ZB_.ap[:, :, 32:64], in_=Bbm.ap[:, j0 + 3, :, :]), reads=Bbm.keys, writes=ZB_.keys)
            lb_ = fw.bank()
            lb3 = fw.bank()
            for jj in range(4):
                if jj < 3:
                    rs, bank_ = slice(32 * jj, 32 * jj + 32), lb_
                    lt = lambda ri: Bbm.ap[:, j0 + jj, ri, :]
                    lk = Bbm.keys
                else:
                    rs, bank_ = slice(64, 128), lb3
                    lt = lambda ri: ZB_.ap[:, ri, :]
                    lk = ZB_.keys
                for ri in range(2):
                    op("tensor", lambda t, ri=ri: t.matmul(
                        psb[bank_][rs, 0:32], lhsT=lt(ri), rhs=C0m.ap[:, j0 + jj, ri, :],
                        start=(ri == 0), stop=(ri == 1)), reads=lk + C0m.keys, writes=[pk(bank_)])
                for ri in range(2):
                    op("tensor", lambda t, ri=ri: t.matmul(
                        psb[bank_][rs, 32:512].rearrange("p (e m) -> p e m", e=15), lhsT=lt(ri),
                        rhs=Cw_.ap[:, ri, 0:15, CB[jj], :],
                        start=(ri == 0), stop=(ri == 1)), reads=lk + Cw_.keys, writes=[pk(bank_)])
            for jj in range(3):
                rs = slice(32 * jj, 32 * jj + 32)
                evac_copy(Lw_.ap[rs, :, 32 * jj:32 * jj + 32], psb[lb_][rs, :].rearrange("p (t m) -> p t m", t=16),
                          reads=[pk(lb_)], writes=Lw_.keys)
            evac_copy(Lw_.ap[64:128, :, 96:128], psb[lb3][64:128, :].rearrange("p (t m) -> p t m", t=16),
                      reads=[pk(lb3)], writes=Lw_.keys)
            op("vector", lambda v: v.scalar_tensor_tensor(out=Lw_.ap[:, 0, :], in0=ident, scalar=gs("ssm_d", c), in1=Lw_.ap[:, 0, :],
                                                          op0=MUL, op1=ADD), reads=Lw_.keys + ["ident", "gvec"], writes=Lw_.keys)
            op("vector", lambda v: v.tensor_copy(
                out=Sc_.ap[:, :, :, 1:NB],
                in_=Sall.ap[:, 0:NB - 1, :].rearrange("p n (r j) -> p r j n", r=2)[:, :, j0:j0 + 4, :]),
               reads=Sall.keys, writes=Sc_.keys)
            op("vector", lambda v: v.tensor_copy(
                out=Sc_.ap[:, :, :, NB:NBX],
                in_=H0.ap.rearrange("p s (r j) -> p r j s", r=2)[:, :, j0:j0 + 4, :]),
               reads=H0.keys, writes=Sc_.keys)

        RG = [(0, 3), (3, 3), (6, 3), (9, 3), (12, 3), (15, 1)]

        def cy(c, groups):
            Cw_, Lw_, Sc_ = Cwb[c % 2], Lwb[c % 2], Scb[c % 2]
            hr, uv, zv = u_keys(c), hbu(c), hbz(c)
            for (r0, nr) in groups:
                yb_ = fw.bank()
                for tau in range(r0 + nr):
                    ra = max(r0, tau)
                    nrow = r0 + nr - ra
                    op("tensor", lambda t, tau=tau, ra=ra, nrow=nrow, yb_=yb_: t.matmul(
                        psb[yb_][:, (ra - r0) * NBX:(ra - r0 + nrow) * NBX], lhsT=Lw_.ap[:, tau, :],
                        rhs=uv[:, ra - tau:ra - tau + nrow, :].rearrange("p r n -> p (r n)"),
                        start=(tau == 0), stop=False, skip_group_check=True), reads=Lw_.keys + hr, writes=[pk(yb_)])
                for rr in range(nr):
                    r = r0 + rr
                    cols = slice(rr * NBX, (rr + 1) * NBX)
                    for jj in range(4):
                        rs = slice(32 * jj, 32 * jj + 32) if jj < 3 else slice(64, 128)
                        for ri in range(2):
                            lt_ = Cw_.ap[:, ri, r, CB[jj], :] if jj < 3 else Cw_.ap[:, ri, r, 3:5, :].rearrange("p b m -> p (b m)")
                            op("tensor", lambda t, jj=jj, ri=ri, rs=rs, r=r, cols=cols, yb_=yb_, lt_=lt_: t.matmul(
                                psb[yb_][rs, cols], lhsT=lt_, rhs=Sc_.ap[:, ri, jj, :],
                                start=False, stop=(jj == 3 and ri == 1), skip_group_check=True),
                               reads=Cw_.keys + Sc_.keys, writes=[pk(yb_)])
                gelu_bank(yb_, nr * NBX, zv[:, r0:r0 + nr, :].rearrange("p r n -> p (r n)"), lambda a: a, z_keys(c))

        wa = av(0, 16 * K1, BF16, "p (k n) -> p k n", k=8)
        wg2 = av(36 * K1, 16 * K1, BF16, "p (k n) -> p k n", k=8)
        cgen(0)
        clags(0)
        for c in range(NCH):
            if c + 1 < NCH:
                cgen(c + 1)
            else:
                fw.dma("gpsimd", wa.ap, w_glu[:, 0:D].rearrange("(k p) n -> p k n", p=128), writes=wa.keys)
                fw.dma("gpsimd", wg2.ap, w_glu[:, D:2 * D].rearrange("(k p) n -> p k n", p=128), writes=wg2.keys)
            cy(c, RG[0:5])
            if c + 1 < NCH:
                clags(c + 1)
            cy(c, RG[5:6])

        stp = av(91 * K1 - 1 * K1, 1 * K1, F32)
        b = fw.bank()
        for ri in range(2):
            op("tensor", lambda t, ri=ri, b=b: t.transpose(psb[b][0:32, ri * 128:(ri + 1) * 128],
                                                          G.ap[:, NB - 1, ri * 32:(ri + 1) * 32], ident),
               reads=G.keys + ["ident"], writes=[pk(b)])
        evac_copy(stp.ap[0:32, 0:256], psb[b][0:32, 0:256], reads=[pk(b)], writes=stp.keys)
        fw.dma("sync", sre_p, stp.ap[0:32, 0:128], reads=stp.keys)
        fw.dma("sync", sim_p, stp.ap[0:32, 128:256], reads=stp.keys)
        fa = av(87 * K1, 2 * K1, F32, "p (s j) -> p s j", s=NSEQ)
        fb = av(89 * K1, 2 * K1, F32, "p (s j) -> p s j", s=NSEQ)
        p8r = Pr.ap[:, :, 16].unsqueeze(1).broadcast_to([128, NSEQ, 32])
        p8i = Pi.ap[:, :, 16].unsqueeze(1).broadcast_to([128, NSEQ, 32])
        fk = fa.keys + fb.keys
        Gre, Gim = G.ap[:, NB:NBX, 0:32], G.ap[:, NB:NBX, 32:64]
        H0r, H0i = H0.ap[:, :, 0:32], H0.ap[:, :, 32:64]
        vop(fa.ap, H0r, p8r, MUL, H0.keys + pk_, fk)
        vop(fb.ap, H0i, p8i, MUL, H0.keys + pk_, fk)
        vop(Gre, Gre, fa.ap, ADD, fk + G.keys, G.keys)
        vop(Gre, Gre, fb.ap, SUB, fk + G.keys, G.keys)
        vop(fa.ap, H0i, p8r, MUL, H0.keys + pk_, fk)
        vop(fb.ap, H0r, p8i, MUL, H0.keys + pk_, fk)
        vop(Gim, Gim, fa.ap, ADD, fk + G.keys, G.keys)
        vop(Gim, Gim, fb.ap, ADD, fk + G.keys, G.keys)
        fst = av(83 * K1, 4 * K1, F32, "p (r s j) -> p r s j", r=2, s=NSEQ)
        op("vector", lambda v: v.tensor_copy(out=fst.ap, in_=G.ap[:, NB:NBX, :].rearrange("p s (r j) -> p r s j", r=2)),
           reads=G.keys, writes=fst.keys)
        for ri, dst in enumerate([sre_s, sim_s]):
            b = fw.bank()
            for sb in range(4):
                op("tensor", lambda t, sb=sb, b=b, ri=ri: t.transpose(
                    psb[b][:, sb * 128:(sb + 1) * 128], fst.ap[:, ri, sb * 4:(sb + 1) * 4, :].rearrange("p s j -> p (s j)"), ident),
                   reads=fst.keys + ["ident"], writes=[pk(b)])
            so = av(75 * K1 + ri * 2 * K1, 2 * K1, F32)
            evac_copy(so.ap, psb[b], reads=[pk(b)], writes=so.keys)
            fw.dma("sync", dst.rearrange("(sb r) q -> r sb q", r=128), so.ap.rearrange("p (sb q) -> p sb q", sb=4), reads=so.keys)
        for tbi, (t0, n) in enumerate(TBS):
            hr = [hk(k, t_) for k in range(NCH) for t_ in range(5)]
            if tbi < 4:
                n0 = t0 // LCH
                zr_ = lambda k: hbz(k)[:, :, n0:n0 + 32]
                pview = lambda a: a.rearrange("p (r n) -> p r n", r=LCH)
                xview = lambda a: a.rearrange("p (n r) -> p r n", r=LCH)
            else:
                zr_ = lambda k: hbz(k)[:, 8:16, NB:NBX]
                pview = lambda a: a.rearrange("p (t s) -> p t s", t=TSEQ)
                xview = lambda a: a.rearrange("p (s t) -> p t s", t=TSEQ)
            for c in range(NCH):
                mm_val = [(wa.ap[:, k, c * 128:(c + 1) * 128], zr_(k), wa.keys + hr) for k in range(NCH)]
                mm_gate = [(wg2.ap[:, k, c * 128:(c + 1) * 128], zr_(k), wg2.keys + hr) for k in range(NCH)]
                gated_block(tbi, c, mm_val, mm_gate, pview=pview, xview=xview, tmp_base=75 * K1)

    octr_ = [0]

    def final_tb(tbi):
        yfin = av(28 * K1, 16 * K1, F32, "p (c t) -> p c t", c=8)
        yo = [av(20 * K1, 4 * K1, F32), av(24 * K1, 4 * K1, F32)]
        t0, n = TBS[tbi]
        b = nctr[0] % 2
        nctr[0] += 1
        sq = av(SQB[b], 8 * K1, BF16, "p (c t) -> p c t", c=8)
        for c in range(NCH):
            op("scalar", lambda s, c=c: s.activation(out=sq.ap[:, c, 0:n], in_=x[:, c, t0:t0 + n], func=AF.Square),
               reads=[xk(c, tbi)], writes=sq.keys)
        yield
        pb = fw.bank()
        for c in range(NCH):
            op("tensor", lambda t, c=c: t.matmul(psb[pb][:, 0:n], lhsT=ones_bf, rhs=sq.ap[:, c, 0:n],
                                                 start=(c == 0), stop=(c == NCH - 1)),
               reads=sq.keys + ["ones"], writes=[pk(pb)])
        rt = av(RT, 2 * K1, F32)
        rstd = av(RSTD[b], 2 * K1, F32)
        op("scalar", lambda s: s.activation(out=rt.ap[:, 0:n], in_=psb[pb][:, 0:n], func=AF.Ln, scale=1.0 / D, bias=EPS),
           reads=[pk(pb)], writes=rt.keys)
        op("scalar", lambda s: s.activation(out=rstd.ap[:, 0:n], in_=rt.ap[:, 0:n], func=AF.Exp, scale=-0.5),
           reads=rt.keys, writes=rstd.keys)
        for c in range(NCH):
            op("vector", lambda v, c=c: v.scalar_tensor_tensor(
                out=yfin.ap[:, c, 0:n], in0=x[:, c, t0:t0 + n], scalar=gs("g_final", c), in1=rstd.ap[:, 0:n],
                op0=ALU.mult, op1=ALU.mult), reads=[xk(c, tbi), "gvec"] + rstd.keys, writes=yfin.keys)
        yield
        for q in range(n // 128):
            tt = t0 // 128 + q
            o = yo[octr_[0] % 2]
            octr_[0] += 1
            for half in range(2):
                b_ = fw.bank()
                for cc in range(4):
                    c = half * 4 + cc
                    op("tensor", lambda t, b_=b_, cc=cc, c=c, q=q: t.transpose(
                        psb[b_][:, cc * 128:(cc + 1) * 128], yfin.ap[:, c, q * 128:(q + 1) * 128], ident),
                       reads=yfin.keys + ["ident"], writes=[pk(b_)])
                evac_copy(o.ap[:, half * 512:(half + 1) * 512], psb[b_], reads=[pk(b_)], writes=o.keys)
            dst = y_p[tt * 128:(tt + 1) * 128, :] if tt < 16 else y_s
            fw.dma("sync", dst, o.ap, reads=o.keys)
            if q % 2 == 1:
                yield

    load_ffn_slice(0, 0)
    pool_mixer(pre_hook=x_load_tb,
               post_hook=lambda tbi: norm_to_hb(tbi - 1, "g_ffn0") if tbi >= 1 else None)
    norm_to_hb(4, "g_ffn0")
    ffn(0, pre_normed=True, each_gen=ple_ptrans(0), mid_hook=lambda: ple_weights(0),
        tail_hook=lambda tbi: ple_norm(0, tbi))
    ssm_gen = ssm_mixer() if enable_ssm else iter(())
    next(ssm_gen, None)
    ple_body(0)
    for _ in ssm_gen:
        pass
    load_ffn_slice(1, 0)
    ffn(1, each_gen=ple_ptrans(1), mid_hook=lambda: ple_weights(1), tail_hook=lambda tbi: ple_norm(1, tbi))
    ple_body(1, tb_gen=final_tb)
    fw.finish("sync")
    return nc


_NC_CACHE = {}


def _get_program(enable_ssm=True):
    if enable_ssm not in _NC_CACHE:
        _NC_CACHE[enable_ssm] = build_program(enable_ssm)
    return _NC_CACHE[enable_ssm]


def kernel(x_prompt, x_sample, state_pool, state_ssm_re, state_ssm_im, p_prompt, p_sample,
           g_mix, g_ffn, g_ple, g_final, pool_w, pool_scale,
           ssm_lambda_re, ssm_lambda_im, ssm_log_dt, ssm_b_re, ssm_b_im, ssm_c_re, ssm_c_im,
           ssm_d, ssm_w_glu, ffn_w_gate, ffn_w_up, ffn_w_down, ple_w_in, ple_w_gate, _enable_ssm=True):
    f = lambda a: np.ascontiguousarray(np.asarray(a, dtype=np.float32))
    x_prompt, x_sample, state_pool, state_ssm_re, state_ssm_im, p_prompt, p_sample = map(
        f, (x_prompt, x_sample, state_pool, state_ssm_re, state_ssm_im, p_prompt, p_sample))
    gvecs = np.ascontiguousarray(np.concatenate([f(g_mix), f(g_ffn), f(g_ple), f(g_final)[None, :], f(pool_scale), f(ssm_d)], axis=0))
    shared = {
        "gvecs": gvecs,
        "pool_w": f(pool_w)[0],
        "lam_re": f(ssm_lambda_re)[0].reshape(32, 128), "lam_im": f(ssm_lambda_im)[0].reshape(32, 128),
        "log_dt": f(ssm_log_dt)[0].reshape(32, 2),
        "b_re": f(ssm_b_re)[0].reshape(32, 128, 16), "b_im": f(ssm_b_im)[0].reshape(32, 128, 16),
        "c_re": f(ssm_c_re)[0].reshape(32, 2, 16, 64), "c_im": f(ssm_c_im)[0].reshape(32, 2, 16, 64),
        "w_glu": f(ssm_w_glu)[0],
        "w_gate": f(ffn_w_gate), "w_up": f(ffn_w_up), "w_down": f(ffn_w_down),
        "ple_w_in": f(ple_w_in), "ple_w_gate": f(ple_w_gate),
    }
    in_maps = []
    for i in range(NCORES):
        sl = slice(i * NSEQ, (i + 1) * NSEQ)
        m = dict(shared)
        m["xp"] = x_prompt[i]
        m["xs"] = x_sample[sl].reshape(TS, D)
        m["pp"] = np.ascontiguousarray(p_prompt[:, i])
        m["psm"] = np.ascontiguousarray(p_sample[:, sl].reshape(2, TS, PLE))
        m["spool"] = state_pool[0, sl].reshape(NSEQ * 15, D)
        m["sre"] = state_ssm_re[0, sl].reshape(NSEQ * 32, 128)
        m["sim"] = state_ssm_im[0, sl].reshape(NSEQ * 32, 128)
        in_maps.append(m)
    nc = _get_program(_enable_ssm)
    res = run_bass_kernel_spmd(nc, in_maps, core_ids=list(range(NCORES)))
    r = res.results
    y_prompt = np.stack([r[i]["y_p"] for i in range(NCORES)], 0)
    y_sample = np.concatenate([r[i]["y_s"].reshape(NSEQ, TSEQ, D) for i in range(NCORES)], 0)
    pool_prompt = np.stack([r[i]["pool_p"] for i in range(NCORES)], 0)[None]
    pool_sample = np.concatenate([r[i]["pool_s"].reshape(NSEQ, 15, D) for i in range(NCORES)], 0)[None]
    sre_p = np.stack([r[i]["sre_p"].reshape(64, 64) for i in range(NCORES)], 0)[None]
    sim_p = np.stack([r[i]["sim_p"].reshape(64, 64) for i in range(NCORES)], 0)[None]
    sre_s = np.concatenate([r[i]["sre_s"].reshape(NSEQ, 64, 64) for i in range(NCORES)], 0)[None]
    sim_s = np.concatenate([r[i]["sim_s"].reshape(NSEQ, 64, 64) for i in range(NCORES)], 0)[None]
    return (y_prompt, y_sample, pool_prompt, pool_sample, sre_p, sim_p, sre_s, sim_s)
```
